# Optimizing a Trainium2 kernel written in Bass

```python
import math
import jax
import jax.numpy as jnp
from jax import lax
import numpy as np

D_MODEL = 2048
BATCH = 2
SEQ = 16384
DEPTH = 2

CTX_LEN = 256
GRID_W = 64
ATT_HEADS = 8
ATT_HD = 64
ATT_VD = 2 * ATT_HD
ROPE_BASE = 10000.0
Q_BLOCK = 128
SGU_GROUPS = 8
SGU_CH = 128
SGU_CHUNK = 128
FOURIER_GROUPS = 4
FOURIER_CH = 256
MLP_HIDDEN = 4 * D_MODEL
N_MOD = 6
EPS = 1e-6
SUBLN_EPS = 1e-5

ATT_QK_WIDTH = ATT_HEADS * ATT_HD
ATT_V_WIDTH = ATT_HEADS * ATT_VD
SGU_WIDTH = SGU_GROUPS * SGU_CH
FOURIER_WIDTH = FOURIER_GROUPS * FOURIER_CH
SEGMENTS = (
    ("q1", ATT_QK_WIDTH), ("q2", ATT_QK_WIDTH), ("k1", ATT_QK_WIDTH), ("k2", ATT_QK_WIDTH),
    ("v", ATT_V_WIDTH), ("su", SGU_WIDTH), ("sv", SGU_WIDTH), ("f", FOURIER_WIDTH),
    ("ga", D_MODEL), ("gg", D_MODEL), ("gf", D_MODEL),
)
IN_WIDTH = 4 * ATT_QK_WIDTH + ATT_V_WIDTH + 2 * SGU_WIDTH + FOURIER_WIDTH + 3 * D_MODEL

kernel_name = "hybrid_diffattn_sgu_fourier_dit_block"


def seg_range(name):
    start = 0
    for n, w in SEGMENTS:
        if n == name:
            return start, start + w
        start += w
    raise KeyError(name)


def seg(p, name, base=0):
    a, b = seg_range(name)
    return p[..., a - base:b - base]


def rmsnorm(x, g, eps=EPS):
    xf = x.astype(jnp.float32)
    y = xf * lax.rsqrt(jnp.mean(xf * xf, axis=-1, keepdims=True) + eps)
    return (y * g.astype(jnp.float32)).astype(x.dtype)


def modulate(h, shift, scale):
    return h * (1.0 + scale) + shift


def heads(t, hd):
    return t.reshape(t.shape[0], t.shape[1], ATT_HEADS, hd)


def axial_rope_tables(n_tokens):
    n_rows = n_tokens // GRID_W
    row = jnp.broadcast_to(jnp.arange(n_rows, dtype=jnp.float32)[:, None], (n_rows, GRID_W)).reshape(-1)
    col = jnp.broadcast_to(jnp.arange(GRID_W, dtype=jnp.float32)[None, :], (n_rows, GRID_W)).reshape(-1)
    n_freq = ATT_HD // 4
    inv = ROPE_BASE ** (-jnp.arange(n_freq, dtype=jnp.float32) / n_freq)
    ar = row[:, None] * inv
    ac = col[:, None] * inv
    ang = jnp.concatenate([ar, ar, ac, ac], axis=-1)
    return jnp.cos(ang), jnp.sin(ang)


def rotate_half(t):
    t1, t2 = jnp.split(t, 2, axis=-1)
    return jnp.concatenate([-t2, t1], axis=-1)


def apply_axial_rope(x, cos, sin):
    xf = x.astype(jnp.float32)
    half = ATT_HD // 2
    rot = jnp.concatenate([rotate_half(xf[..., :half]), rotate_half(xf[..., half:])], axis=-1)
    return (xf * cos[None, :, None, :] + rot * sin[None, :, None, :]).astype(x.dtype)


def diff_attend(q1, q2, k1, k2, v, lam):
    scale = ATT_HD ** -0.5
    s1 = jnp.einsum("bqhd,bkhd->bhqk", q1, k1).astype(jnp.float32) * scale
    s2 = jnp.einsum("bqhd,bkhd->bhqk", q2, k2).astype(jnp.float32) * scale
    a = jax.nn.softmax(s1, axis=-1) - lam * jax.nn.softmax(s2, axis=-1)
    return jnp.einsum("bhqk,bkhe->bqhe", a.astype(v.dtype), v)


def blocked_diff_attend(q1, q2, k1, k2, v, lam):
    b, n, h, d = q1.shape
    nblk = n // Q_BLOCK

    def to_blocks(q):
        return q.reshape(b, nblk, Q_BLOCK, h, d).transpose(1, 0, 2, 3, 4)

    out = lax.map(lambda qq: diff_attend(qq[0], qq[1], k1, k2, v, lam), (to_blocks(q1), to_blocks(q2)))
    return out.transpose(1, 0, 2, 3, 4).reshape(b, n, h, ATT_VD)


def diff_head_out(o, g, lambda_init):
    o = rmsnorm(o, g, SUBLN_EPS) * (1.0 - lambda_init)
    return o.reshape(o.shape[0], o.shape[1], ATT_V_WIDTH)


def spatial_gating(u, v, w_s, b_s, g_v):
    bsz, L, _ = v.shape
    v = rmsnorm(v, g_v)
    nch = L // SGU_CHUNK
    v = v.reshape(bsz, nch, SGU_CHUNK, SGU_GROUPS, SGU_CH)
    mixed = jnp.einsum("gpq,bnqgc->bnpgc", w_s, v) + b_s.T[None, None, :, :, None]
    return u * mixed.reshape(bsz, L, SGU_WIDTH)


def fourier_mix(f):
    bsz, L, _ = f.shape
    fg = f.reshape(bsz, L, FOURIER_GROUPS, FOURIER_CH).astype(jnp.float32)
    y = jnp.fft.fft2(fg, axes=(1, 3), norm="ortho").real
    return y.astype(f.dtype).reshape(bsz, L, FOURIER_WIDTH)


def merge_branches(p, att_o, w_s, b_s, g_v, w_pa, w_ps, w_pf, w_o):
    sgu_o = spatial_gating(jax.nn.gelu(seg(p, "su"), approximate=False),
                           jax.nn.gelu(seg(p, "sv"), approximate=False), w_s, b_s, g_v)
    four_o = fourier_mix(seg(p, "f"))
    y = (jax.nn.sigmoid(seg(p, "ga")) * (att_o @ w_pa)
         + jax.nn.sigmoid(seg(p, "gg")) * (sgu_o @ w_ps)
         + jax.nn.sigmoid(seg(p, "gf")) * (four_o @ w_pf))
    return y @ w_o


def sq_relu_mlp(h, w1, w2):
    return jnp.square(jax.nn.relu(h @ w1)) @ w2


def setup_inputs(seed: int = 0) -> dict:
    key = jax.random.key(seed)
    ks = jax.random.split(key, 24)
    f32 = jnp.float32
    nrm = lambda k, shape, s: jax.random.normal(k, shape, f32) * s
    gain = lambda k, shape: 1.0 + 0.02 * jax.random.normal(k, shape, f32)
    return {
        "x": nrm(ks[0], (BATCH, SEQ, D_MODEL), 1.0),
        "c": nrm(ks[1], (BATCH, D_MODEL), 1.0),
        "ctx": nrm(ks[2], (BATCH, CTX_LEN, D_MODEL), 1.0),
        "c_ctx": nrm(ks[3], (D_MODEL,), 1.0),
        "w_mod": nrm(ks[4], (DEPTH, D_MODEL, N_MOD * D_MODEL), 0.5 * D_MODEL ** -0.5),
        "b_mod": nrm(ks[5], (DEPTH, N_MOD * D_MODEL), 0.02),
        "norm1_g": gain(ks[6], (DEPTH, D_MODEL)),
        "norm2_g": gain(ks[7], (DEPTH, D_MODEL)),
        "w_in": nrm(ks[8], (DEPTH, D_MODEL, IN_WIDTH), D_MODEL ** -0.5),
        "lambda_q1": nrm(ks[9], (DEPTH, ATT_HD), 0.1),
        "lambda_k1": nrm(ks[10], (DEPTH, ATT_HD), 0.1),
        "lambda_q2": nrm(ks[11], (DEPTH, ATT_HD), 0.1),
        "lambda_k2": nrm(ks[12], (DEPTH, ATT_HD), 0.1),
        "subln_g": gain(ks[13], (DEPTH, ATT_VD)),
        "sgu_norm_g": gain(ks[14], (DEPTH, SGU_WIDTH)),
        "sgu_w": nrm(ks[15], (DEPTH, SGU_GROUPS, SGU_CHUNK, SGU_CHUNK), SGU_CHUNK ** -0.5),
        "sgu_b": gain(ks[16], (DEPTH, SGU_GROUPS, SGU_CHUNK)),
        "w_proj_att": nrm(ks[17], (DEPTH, ATT_V_WIDTH, D_MODEL), ATT_V_WIDTH ** -0.5),
        "w_proj_sgu": nrm(ks[18], (DEPTH, SGU_WIDTH, D_MODEL), SGU_WIDTH ** -0.5),
        "w_proj_fourier": nrm(ks[19], (DEPTH, FOURIER_WIDTH, D_MODEL), FOURIER_WIDTH ** -0.5),
        "w_out": nrm(ks[20], (DEPTH, D_MODEL, D_MODEL), D_MODEL ** -0.5),
        "w_mlp_in": nrm(ks[21], (DEPTH, D_MODEL, MLP_HIDDEN), D_MODEL ** -0.5),
        "w_mlp_out": nrm(ks[22], (DEPTH, MLP_HIDDEN, D_MODEL), MLP_HIDDEN ** -0.5),
        "final_g": gain(ks[23], (D_MODEL,)),
    }


def reference(x, c, ctx, c_ctx, w_mod, b_mod, norm1_g, norm2_g, w_in, lambda_q1, lambda_k1, lambda_q2,
              lambda_k2, subln_g, sgu_norm_g, sgu_w, sgu_b, w_proj_att, w_proj_sgu, w_proj_fourier, w_out,
              w_mlp_in, w_mlp_out, final_g):
    n_tok = x.shape[1]
    cos, sin = axial_rope_tables(n_tok)
    for l in range(DEPTH):
        last = l == DEPTH - 1
        lambda_init = 0.8 - 0.6 * math.exp(-0.3 * l)
        lam = (jnp.exp(jnp.sum(lambda_q1[l].astype(jnp.float32) * lambda_k1[l].astype(jnp.float32)))
               - jnp.exp(jnp.sum(lambda_q2[l].astype(jnp.float32) * lambda_k2[l].astype(jnp.float32)))
               + lambda_init)
        mod = jax.nn.silu(c) @ w_mod[l] + b_mod[l]
        sh1, sc1, gt1, sh2, sc2, gt2 = [m[:, None, :] for m in jnp.split(mod, N_MOD, axis=-1)]
        mod_c = jax.nn.silu(c_ctx) @ w_mod[l] + b_mod[l]
        csh1, csc1, cgt1, csh2, csc2, cgt2 = jnp.split(mod_c, N_MOD, axis=-1)

        hx = modulate(rmsnorm(x, norm1_g[l]), sh1, sc1)
        hc = modulate(rmsnorm(ctx, norm1_g[l]), csh1, csc1)

        if last:
            base = seg_range("k1")[0]
            pc = hc @ w_in[l][:, base:seg_range("v")[1]]
        else:
            base = 0
            pc = hc @ w_in[l]
        kc1 = heads(seg(pc, "k1", base), ATT_HD)
        kc2 = heads(seg(pc, "k2", base), ATT_HD)
        vc = heads(seg(pc, "v", base), ATT_VD)

        px = hx @ w_in[l]
        q1 = apply_axial_rope(heads(seg(px, "q1"), ATT_HD), cos, sin)
        q2 = apply_axial_rope(heads(seg(px, "q2"), ATT_HD), cos, sin)
        k1 = jnp.concatenate([apply_axial_rope(heads(seg(px, "k1"), ATT_HD), cos, sin), kc1], axis=1)
        k2 = jnp.concatenate([apply_axial_rope(heads(seg(px, "k2"), ATT_HD), cos, sin), kc2], axis=1)
        v = jnp.concatenate([heads(seg(px, "v"), ATT_VD), vc], axis=1)
        att_x = diff_head_out(blocked_diff_attend(q1, q2, k1, k2, v, lam), subln_g[l], lambda_init)
        mix_x = merge_branches(px, att_x, sgu_w[l], sgu_b[l], sgu_norm_g[l], w_proj_att[l], w_proj_sgu[l],
                               w_proj_fourier[l], w_out[l])

        if not last:
            qc1 = heads(seg(pc, "q1"), ATT_HD)
            qc2 = heads(seg(pc, "q2"), ATT_HD)
            att_c = diff_head_out(diff_attend(qc1, qc2, kc1, kc2, vc, lam), subln_g[l], lambda_init)
            mix_c = merge_branches(pc, att_c, sgu_w[l], sgu_b[l], sgu_norm_g[l], w_proj_att[l], w_proj_sgu[l],
                                   w_proj_fourier[l], w_out[l])
            ctx = ctx + cgt1 * mix_c
            hc2 = modulate(rmsnorm(ctx, norm2_g[l]), csh2, csc2)
            ctx = ctx + cgt2 * sq_relu_mlp(hc2, w_mlp_in[l], w_mlp_out[l])

        x = x + gt1 * mix_x
        hx2 = modulate(rmsnorm(x, norm2_g[l]), sh2, sc2)
        x = x + gt2 * sq_relu_mlp(hx2, w_mlp_in[l], w_mlp_out[l])
    return rmsnorm(x, final_g)
```

```python
import contextlib
import math
import numpy as np
import concourse.bass as bass
import concourse.mybir as mybir
from concourse.bass_utils import run_bass_kernel_spmd

F32 = mybir.dt.float32
BF16 = mybir.dt.bfloat16
AF = mybir.ActivationFunctionType
ALU = mybir.AluOpType
AX = mybir.AxisListType

D = 2048
NL = 4096
NC = 256
TT = NL + NC
NBLK = [(i * 512, 512) for i in range(8)] + [(NL, NC)]
DEPTH = 2
EXN = TT * 4096
KT_OFF, V_OFF, Z_OFF = 0, 1024 * TT, 2048 * TT
ENGS = ("pe", "act", "dve", "pool", "sp")


class Res:
    __slots__ = ("name", "last_w", "readers", "sem", "cnt", "inh")

    def __init__(self, name):
        self.name = name
        self.last_w = None
        self.readers = []
        self.sem = None
        self.cnt = 0
        self.inh = []


class Sched:
    def __init__(self, nc, sem_alloc):
        self.nc = nc
        self.sem_alloc = sem_alloc
        self.streams = {e: [] for e in ENGS}
        self.count = {e: 0 for e in ENGS}
        self.esem = {e: sem_alloc("c_" + e) for e in ENGS if e != "sp"}
        self.waited = {e: {} for e in ENGS}
        self.pending = []
        self.nsem = 4
        self.last = {}
        self.free = []
        self.phase_stack = []

    def _need(self, eng, tok, raw):
        sem, val, teng = tok[0], tok[1], tok[2]
        if teng == eng and (eng in ("pe", "sp") or not raw):
            return False
        key = id(sem)
        if self.waited[eng].get(key, 0) >= val:
            return False
        self.waited[eng][key] = val
        return True

    def op(self, eng, fn, reads=(), writes=(), dma_res=None, extra=(), store=False, dma_inc=16):
        is_dma = dma_res is not None
        deps = [(t, False) for t in extra]
        follow = set()
        for r in reads:
            if r.last_w is not None:
                deps.append((r.last_w, True))
        for w in writes:
            if is_dma and w is dma_res and w.last_w is not None and w.last_w[3] and not w.readers:
                deps.extend(w.inh)
                follow.add(id(w))
            else:
                if w.last_w is not None:
                    deps.append((w.last_w, False))
                deps.extend((t, False) for t in w.readers)
        waits = []
        for t, raw in deps:
            if self._need(eng, t, raw):
                waits.append((t[0], t[1]))
        if is_dma:
            if dma_res.sem is None:
                if self.free:
                    dma_res.sem, dma_res.cnt = self.free.pop()
                else:
                    dma_res.sem, dma_res.cnt = self.sem_alloc("d%d_%s" % (self.nsem, dma_res.name)), 0
                    self.nsem += 1
                if self.phase_stack:
                    self.phase_stack[-1].append(dma_res)
            dma_res.cnt += dma_inc
            tok = (dma_res.sem, dma_res.cnt, None, True)
            inc = (dma_res.sem, dma_inc)
        else:
            self.count[eng] += 1
            tok = (self.esem[eng], self.count[eng], eng, False)
            inc = (self.esem[eng], 1)
        self.streams[eng].append((waits, fn, inc))
        if not is_dma:
            self.last[eng] = tok
        for r in reads:
            r.readers.append(tok)
        for w in writes:
            if id(w) not in follow:
                w.inh = list(deps)
            w.last_w = tok
            w.readers = []
        if store:
            self.pending.append(tok)
        return tok

    def ensure_sem(self, res):
        if res.sem is None:
            res.sem, res.cnt = self.sem_alloc("g%d_%s" % (self.nsem, res.name)), 0
            self.nsem += 1

    def barrier(self, eng="sp"):
        waits = []
        for t in self.pending:
            if self._need(eng, t, True):
                waits.append((t[0], t[1]))
        self.pending = []
        self.streams[eng].append((waits, None, None))

    def full_barrier(self):
        toks = list(self.last.values()) + list(self.pending)
        self.pending = []
        for eng in ENGS:
            self.wait_tok(eng, toks)

    def wait_tok(self, eng, toks):
        waits = []
        for t in toks:
            if self._need(eng, t, True):
                waits.append((t[0], t[1]))
        self.streams[eng].append((waits, None, None))

    def emit(self, block):
        def run(stream):
            def f(e):
                for waits, fn, inc in stream:
                    for (s, v) in waits:
                        e.wait_ge(s, v)
                    if fn is not None:
                        fn(e).then_inc(inc[0], inc[1])
            return f
        block.tensor(run(self.streams["pe"]))
        block.scalar(run(self.streams["act"]))
        block.vector(run(self.streams["dve"]))
        block.gpsimd(run(self.streams["pool"]))
        block.sync(run(self.streams["sp"]))


class Ring:
    def __init__(self, tiles, name):
        self.tiles = tiles
        self.res = [Res("%s%d" % (name, i)) for i in range(len(tiles))]
        self.i = 0

    def next(self):
        k = self.i % len(self.tiles)
        self.i += 1
        return self.tiles[k], self.res[k]


def build_program(n_layers=DEPTH, phases=("p1", "ag", "p2", "four", "p3"), dump=None):
    nc = bass.Bass("TRN2", target_bir_lowering=False)
    P = 128

    def din(name, shape, dt=F32):
        return nc.dram_tensor(name, list(shape), dt, kind="ExternalInput").ap()

    def dint(name, shape, dt=BF16):
        return nc.dram_tensor(name, list(shape), dt).ap()

    xin = din("xin", [D, TT])
    cvec = din("cvec", [P, 16, 2])
    ropeT = din("ropeT", [P, 2, TT])
    w_mod = din("w_mod", [DEPTH, 96, P, 2048])
    b_modT = din("b_modT", [DEPTH, P, 96])
    g1T = din("g1T", [DEPTH, P, 16])
    g2T = din("g2T", [DEPTH, P, 16])
    gfT = din("gfT", [P, 16])
    w_fm = din("w_fm", [DEPTH, 96, P, 2048])
    w_tm = din("w_tm", [DEPTH, 16, P, 2048])
    lamv = din("lamv", [DEPTH, P, 4, 64])
    sublnT = din("sublnT", [DEPTH, P, 1])
    sgu_gbc = din("sgu_gbc", [DEPTH, P, 1024])
    sgu_wT = din("sgu_wT", [DEPTH, P, 1024])
    sgu_bbc = din("sgu_bbc", [DEPTH, P, 1024])
    w_pa = din("w_pa", [DEPTH, 16, P, 1024])
    w_ps = din("w_ps", [DEPTH, 16, P, 1024])
    w_pf = din("w_pf", [DEPTH, 16, P, 1024])
    w_o = din("w_o", [DEPTH, 16, P, 2048])
    w_1 = din("w_1", [DEPTH, 64, P, 2048])
    w_2 = din("w_2", [DEPTH, 64, P, 2048])
    constsA = din("constsA", [P, 2944])
    constsB = din("constsB", [P, 320])
    outT = nc.dram_tensor("outT", [D, NL], F32, kind="ExternalOutput").ap()
    dumps = {}

    wb_fm = dint("wb_fm", [DEPTH, 96, P, 2048])
    wb_tm = dint("wb_tm", [DEPTH, 16, P, 2048])
    wb_pa = dint("wb_pa", [DEPTH, 16, P, 1024])
    wb_ps = dint("wb_ps", [DEPTH, 16, P, 1024])
    wb_pf = dint("wb_pf", [DEPTH, 16, P, 1024])
    wb_o = dint("wb_o", [DEPTH, 16, P, 2048])
    wb_1 = dint("wb_1", [DEPTH, 64, P, 2048])
    wb_2 = dint("wb_2", [DEPTH, 64, P, 2048])
    qT = dint("qT", [1024, TT])
    gatesT = dint("gatesT", [6144, TT])
    sguT = dint("sguT", [1024, TT])
    attT = dint("attT", [1024, TT])
    fourTM = dint("fourTM", [TT, 1024])
    xT1 = dint("xT1", [D, TT], F32)
    exk_in = dint("exk_in", [1024, NL])
    exk_out = dint("exk_out", [16, 4, 64, NL])
    exv_in = dint("exv_in", [8, NL, 128])
    exv_out = dint("exv_out", [8, 4 * NL, 128])
    exz_in = dint("exz_in", [16, NL, 128])
    exz_out = dint("exz_out", [16, 4 * NL, 128])
    exkc_in = dint("exkc_in", [1024, NC])
    exkc_out = dint("exkc_out", [4, 1024, NC])
    exvc_in = dint("exvc_in", [NC, 1024])
    exvc_out = dint("exvc_out", [4, NC, 1024])
    zctx = dint("zctx", [16, NC, 128])
    if dump:
        for nm, shape, dt in dump:
            dumps[nm] = nc.dram_tensor("dump_" + nm, list(shape), dt, kind="ExternalOutput").ap()

    es = contextlib.ExitStack()
    with es:
        def sem_alloc(name):
            return es.enter_context(nc.semaphore(name))

        S = Sched(nc, sem_alloc)

        arena = {"cur": 16640}

        def sb(name, shape, dt=F32):
            n = 1
            for d_ in shape[1:]:
                n *= d_
            nbytes = n * (4 if dt == F32 else 2)
            off = (arena["cur"] + 63) // 64 * 64
            arena["cur"] = off + nbytes
            assert arena["cur"] <= 229376 - 256, (name, arena["cur"])
            uid = "%s_%d" % (name, off)
            return nc.alloc_sbuf_tensor_at(uid + "_%d" % len(arena), list(shape), dt, offset=off)

        marks = []

        def phase_begin():
            marks.append(arena["cur"])
            S.phase_stack.append([])
            if "wring" in arena:
                arena["wring"].res = [Res("wr%d" % i) for i in range(len(arena["wring"].tiles))]
            arena[len(arena)] = 0

        def phase_end():
            S.full_barrier()
            for r_ in S.phase_stack.pop():
                S.free.append((r_.sem, r_.cnt))
            arena["cur"] = marks.pop()
            arena[len(arena)] = 0

        ps2 = [nc.alloc_psum_tensor("ps2_%d" % i, [P, 1024], F32) for i in range(4)]
        psum = [ps2[i // 2][:, (i % 2) * 512:(i % 2 + 1) * 512] for i in range(8)]
        Rps = [Res("ps%d" % i) for i in range(8)]

        cstB = sb("cstB", [P, 320])
        cb = sb("cb", [P, 2944], BF16)
        cownb = sb("cownb", [P, 64], BF16)
        Rcst = Res("cstB")
        Rcb = Res("cb")
        S.op("sp", lambda e: e.dma_start(out=cstB[:], in_=constsB[:, :]), writes=[Rcst], dma_res=Rcst)
        phase_begin()
        cstA = sb("cstA", [P, 2944])
        RcA = Res("cstA")
        S.op("sp", lambda e: e.dma_start(out=cstA[:], in_=constsA[:, :]), writes=[RcA], dma_res=RcA)
        S.op("dve", lambda e: e.tensor_copy(out=cb[:], in_=cstA[:]), reads=[RcA], writes=[Rcb])
        S.op("dve", lambda e: e.tensor_copy(out=cownb[:], in_=cstB[:, 256:320]), reads=[Rcst], writes=[Rcb])
        phase_end()
        ones_b = cb[:, 0:128]
        ident_b = cb[:, 128:256]
        CS1 = cb[:, 256:512]
        CS2 = cb[:, 512:768]
        Wc = [cb[:, 768:1280], cb[:, 1280:1792]]
        Cx = [[cb[:, 1792 + n * 512:1792 + n * 512 + 256], cb[:, 1792 + n * 512 + 256:1792 + (n + 1) * 512]] for n in range(2)]
        Tc = cstB[:, 0:128]
        Ts = cstB[:, 128:256]
        eps6 = cstB[:, 256 - 0 + 0:256 + 0] if False else None
        epst = sb("epst", [P, 2])
        Reps = Res("eps")
        S.op("pool", lambda e: e.memset(epst[:, 0:1], 1e-6), writes=[Reps])
        S.op("pool", lambda e: e.memset(epst[:, 1:2], 1e-5), writes=[Reps])
        eps6 = epst[:, 0:1]
        eps5 = epst[:, 1:2]

        Rw = {}

        def cast_w(name, src, dst, l, rows):
            r = Res("w_%s%d" % (name, l))
            Rw[(name, l)] = r
            s2 = src[l].rearrange("r p x -> (r p) x")
            d2 = dst[l].rearrange("r p x -> (r p) x")
            n = rows * P
            step = 1024
            for a in range(0, n, step):
                b = min(n, a + step)
                S.op("pool", (lambda a=a, b=b: lambda e: e.dma_start(out=d2[a:b, :], in_=s2[a:b, :]))(),
                     writes=[r], dma_res=r)

        for l in range(n_layers):
            cast_w("fm", w_fm, wb_fm, l, 96)
            cast_w("tm", w_tm, wb_tm, l, 16)
            cast_w("pa", w_pa, wb_pa, l, 16)
            cast_w("ps", w_ps, wb_ps, l, 16)
            cast_w("pf", w_pf, wb_pf, l, 16)
            cast_w("o", w_o, wb_o, l, 16)
            cast_w("1", w_1, wb_1, l, 64)
            cast_w("2", w_2, wb_2, l, 64)

        NW = 6
        wring = Ring([sb("wr%d" % i, [P, 2048], BF16) for i in range(NW)], "wr")
        arena["wring"] = wring

        class WPipe:
            def __init__(self, items):
                self.items = items
                self.slots = []
                self.k = 0
                for _ in range(min(NW - 1, len(items))):
                    self._issue()

            def _issue(self):
                i = len(self.slots)
                ap, wres = self.items[i]
                t, r = wring.next()
                x = ap.shape[-1]
                S.op("sp", lambda e: e.dma_start(out=t[:, 0:x], in_=ap), reads=[wres], writes=[r], dma_res=r)
                self.slots.append((t, r))

            def get(self):
                t, r = self.slots[self.k]
                self.k += 1
                if len(self.slots) < len(self.items):
                    self._issue()
                return t, r

        modT = sb("modT", [P, DEPTH, 96, 2])
        tabA = sb("tabA", [P, DEPTH, 2, 2, 16])
        bmt = sb("bmt", [P, DEPTH, 96])
        g12 = sb("g12", [P, DEPTH, 2, 16])
        gft = sb("gft", [P, 16])
        cv = sb("cv", [P, 16, 2])
        cvs = sb("cvs", [P, 16, 2])
        lamt = sb("lamt", [P, DEPTH, 4, 64])
        lamp = sb("lamp", [P, DEPTH, 2, 64])
        lams = sb("lams", [P, DEPTH, 2])
        neglam = sb("neglam", [P, DEPTH])
        sublt = sb("sublt", [P, DEPTH])
        Rmod, Rcv, Rsm, Rlam = Res("mod"), Res("cv"), Res("small"), Res("lam")
        S.op("sp", lambda e: e.dma_start(out=cv[:], in_=cvec[:, :, :]), writes=[Rcv], dma_res=Rcv)
        S.op("act", lambda e: e.activation(out=cvs[:], in_=cv[:], func=AF.Sigmoid), reads=[Rcv], writes=[Rmod])
        S.op("dve", lambda e: e.tensor_tensor(out=cvs[:], in0=cvs[:], in1=cv[:], op=ALU.mult), reads=[Rmod, Rcv], writes=[Rmod])
        for l in range(DEPTH):
            S.op("sp", (lambda l=l: lambda e: e.dma_start(out=bmt[:, l, :], in_=b_modT[l]))(), writes=[Rsm], dma_res=Rsm)
            S.op("sp", (lambda l=l: lambda e: e.dma_start(out=g12[:, l, 0, :], in_=g1T[l]))(), writes=[Rsm], dma_res=Rsm)
            S.op("sp", (lambda l=l: lambda e: e.dma_start(out=g12[:, l, 1, :], in_=g2T[l]))(), writes=[Rsm], dma_res=Rsm)
        S.op("sp", lambda e: e.dma_start(out=gft[:], in_=gfT[:, :]), writes=[Rsm], dma_res=Rsm)

        phase_begin()
        wm_ring = Ring([sb("wm%d" % i, [P, 2048]) for i in range(3)], "wm")
        for l in range(n_layers):
            for j in range(96):
                t, r = wm_ring.next()
                S.op("sp", (lambda t=t, l=l, j=j: lambda e: e.dma_start(out=t[:], in_=w_mod[l, j]))(), writes=[r], dma_res=r)
                pb = j % 2
                for kc in range(16):
                    S.op("pe", (lambda t=t, kc=kc, pb=pb: lambda e: e.matmul(psum[pb][:, 0:2], lhsT=t[:, kc * 128:(kc + 1) * 128], rhs=cvs[:, kc, :], start=(kc == 0), stop=(kc == 15)))(),
                         reads=[r, Rmod], writes=[Rps[pb]])
                S.op("dve", (lambda l=l, j=j, pb=pb: lambda e: e.tensor_tensor(out=modT[:, l, j, :], in0=psum[pb][:, 0:2], in1=bmt[:, l, j:j + 1].to_broadcast([P, 2]), op=ALU.add))(),
                     reads=[Rps[pb], Rsm], writes=[Rmod])
            for w, j0 in ((0, 16), (1, 64)):
                for kind in range(2):
                    S.op("dve", (lambda l=l, w=w, j0=j0, kind=kind: lambda e: e.scalar_tensor_tensor(
                        out=tabA[:, l, w, kind, :], in0=modT[:, l, j0:j0 + 16, kind], scalar=1.0, in1=g12[:, l, w, :],
                        op0=ALU.add, op1=ALU.mult))(), reads=[Rmod, Rsm], writes=[Rmod])
            linit = 0.8 - 0.6 * math.exp(-0.3 * l)
            S.op("sp", (lambda l=l: lambda e: e.dma_start(out=lamt[:, l], in_=lamv[l]))(), writes=[Rlam], dma_res=Rlam)
            S.op("sp", (lambda l=l: lambda e: e.dma_start(out=sublt[:, l:l + 1], in_=sublnT[l]))(), writes=[Rlam], dma_res=Rlam)
            S.op("dve", (lambda l=l: lambda e: e.tensor_tensor(out=lamp[:, l, 0, :], in0=lamt[:, l, 0, :], in1=lamt[:, l, 1, :], op=ALU.mult))(), reads=[Rlam], writes=[Rlam])
            S.op("dve", (lambda l=l: lambda e: e.tensor_tensor(out=lamp[:, l, 1, :], in0=lamt[:, l, 2, :], in1=lamt[:, l, 3, :], op=ALU.mult))(), reads=[Rlam], writes=[Rlam])
            S.op("dve", (lambda l=l: lambda e: e.tensor_reduce(out=lams[:, l, :], in_=lamp[:, l], axis=AX.X, op=ALU.add))(), reads=[Rlam], writes=[Rlam])
            S.op("act", (lambda l=l: lambda e: e.activation(out=lams[:, l, :], in_=lams[:, l, :], func=AF.Exp))(), reads=[Rlam], writes=[Rlam])
            S.op("dve", (lambda l=l, linit=linit: lambda e: e.scalar_tensor_tensor(out=neglam[:, l:l + 1], in0=lams[:, l, 1:2], scalar=-linit, in1=lams[:, l, 0:1], op0=ALU.add, op1=ALU.subtract))(), reads=[Rlam], writes=[Rlam])
            S.op("dve", (lambda l=l, linit=linit: lambda e: e.tensor_scalar(out=sublt[:, l:l + 1], in0=sublt[:, l:l + 1], scalar1=1.0 - linit, scalar2=None, op0=ALU.mult))(), reads=[Rlam], writes=[Rlam])
        phase_end()

        def mod_col(l, j0, kind, kc):
            return modT[:, l, j0 + kc, kind:kind + 1]

        def norm_stats(T, n):
            S.op("act", lambda e: e.activation(out=T.sqb[:, :, 0:n], in_=T.xt[:, :, 0:n], func=AF.Square), reads=[T.Rxt], writes=[T.Rsq])
            for kc in range(16):
                S.op("pe", (lambda kc=kc: lambda e: e.matmul(psum[7][:, 0:n], lhsT=ones_b, rhs=T.sqb[:, kc, 0:n], start=(kc == 0), stop=(kc == 15)))(),
                     reads=[T.Rsq, Rcb], writes=[Rps[7]])
            S.op("act", lambda e: e.activation(out=T.rstd[:, 0:n], in_=psum[7][:, 0:n], func=AF.Sqrt, scale=1.0 / D, bias=eps6), reads=[Rps[7], Reps], writes=[T.Rrstd])
            S.op("dve", lambda e: e.reciprocal(out=T.rstd[:, 0:n], in_=T.rstd[:, 0:n]), reads=[T.Rrstd], writes=[T.Rrstd])

        def norm_mod(T, l, w, kind, n):
            j_sh = 0 if w == 0 else 48
            norm_stats(T, n)
            for kc in range(16):
                tf, rtf = T.tmpf[kc % 2], T.Rtmpf[kc % 2]
                S.op("dve", (lambda kc=kc, tf=tf: lambda e: e.scalar_tensor_tensor(out=tf[:, 0:n], in0=T.xt[:, kc, 0:n], scalar=tabA[:, l, w, kind, kc:kc + 1], in1=T.rstd[:, 0:n], op0=ALU.mult, op1=ALU.mult))(),
                     reads=[T.Rxt, T.Rrstd, Rmod], writes=[rtf])
                S.op("act", (lambda kc=kc, tf=tf: lambda e: e.activation(out=T.hx[:, kc, 0:n], in_=tf[:, 0:n], func=AF.Identity, bias=mod_col(l, j_sh, kind, kc), scale=1.0))(),
                     reads=[rtf, Rmod], writes=[T.Rhx])

        class NS:
            pass

        def common_tiles():
            T = NS()
            T.xt = sb("xt", [P, 16, 512]); T.Rxt = Res("xt")
            T.sqb = sb("sqb", [P, 16, 512], BF16); T.Rsq = Res("sqb")
            T.hx = sb("hx", [P, 16, 512], BF16); T.Rhx = Res("hx")
            T.rstd = sb("rstd", [P, 512]); T.Rrstd = Res("rstd")
            T.tmpf = [sb("tmpf%d" % i, [P, 512]) for i in range(4)]
            T.Rtmpf = [Res("tmpf%d" % i) for i in range(4)]
            T.stg = Ring([sb("stg%d" % i, [P, 512], BF16) for i in range(4)], "stg")
            return T

        Rexout = Res("exout")
        S.ensure_sem(Rexout)

        def do_layer(l):
            last = l == DEPTH - 1
            xsrc = xin if l == 0 else xT1

            if "p1" in phases:
                phase_begin()
                T = common_tiles()
                sgw_f = sb("sgw_f", [P, 1024]); sgw_b = sb("sgw_b", [P, 1024], BF16)
                sgb = sb("sgb", [P, 1024]); sgg = sb("sgg", [P, 1024]); Rsg = Res("sg")
                rope_t = [sb("rope%d" % i, [P, 2, 512]) for i in range(2)]
                Rrope = [Res("rope%d" % i) for i in range(2)]
                sut = sb("sut", [P, 8, 512], BF16); Rsut = Res("sut")
                ft = sb("ft", [P, 8, 512], BF16); Rft = Res("ft")
                svall = [sb("svall%d" % i, [P, 1024]) for i in range(4)]
                Rsvall = [Res("svall%d" % i) for i in range(4)]
                Rsvf = Res("svf")
                vn = sb("vn", [P, 1024], BF16); Rvn = Res("vn")
                ssq = sb("ssq", [P, 2]); junk = sb("junk", [P, 1024])
                zst = [sb("zst%d" % i, [P, 2048], BF16) for i in range(2)]
                Rzst = [Res("zst%d" % i) for i in range(2)]
                S.op("sp", (lambda l=l: lambda e: e.dma_start(out=sgw_f[:], in_=sgu_wT[l]))(), writes=[Rsg], dma_res=Rsg)
                S.op("sp", (lambda l=l: lambda e: e.dma_start(out=sgb[:], in_=sgu_bbc[l]))(), writes=[Rsg], dma_res=Rsg)
                S.op("sp", (lambda l=l: lambda e: e.dma_start(out=sgg[:], in_=sgu_gbc[l]))(), writes=[Rsg], dma_res=Rsg)
                S.op("dve", lambda e: e.tensor_copy(out=sgw_b[:], in_=sgw_f[:]), reads=[Rsg], writes=[Rsg])
                def p1_block(bi, t0, n):
                    kind = 0 if bi < 8 else 1
                    nsub = n // 128
                    S.op("sp", (lambda t0=t0, n=n: lambda e: e.dma_start(out=T.xt[:, :, 0:n], in_=xsrc[:, t0:t0 + n].rearrange("(k p) t -> p k t", p=P)))(),
                         writes=[T.Rxt], dma_res=T.Rxt)
                    rp, Rrp = rope_t[bi % 2], Rrope[bi % 2]
                    S.op("sp", (lambda t0=t0, n=n, rp=rp: lambda e: e.dma_start(out=rp[:, :, 0:n], in_=ropeT[:, :, t0:t0 + n]))(), writes=[Rrp], dma_res=Rrp)
                    norm_mod(T, l, 0, kind, n)
                    need_all = (not last) or kind == 0
                    items = []
                    fm_list = [("qk", c) for c in range(16) if (need_all or c >= 8)]
                    if need_all:
                        fm_list += [("su", c) for c in range(8)] + [("f", c) for c in range(8)] + [("g", c) for c in range(48)]
                    for kindc, c in fm_list:
                        if kindc == "qk":
                            items.append((wb_fm[l, c], Rw[("fm", l)]))
                            items.append((wb_fm[l, 16 + c], Rw[("fm", l)]))
                        else:
                            base = {"su": 32, "f": 40, "g": 48}[kindc]
                            items.append((wb_fm[l, base + c], Rw[("fm", l)]))
                    tm_list = [0, 1] + ([2, 3] if need_all else [])
                    for wh in tm_list:
                        for kq in range(4):
                            items.append((wb_tm[l, wh * 4 + kq], Rw[("tm", l)]))
                    pipe = WPipe(items)
                    pcount = [0]

                    def nextps():
                        k = pcount[0] % 6
                        pcount[0] += 1
                        return psum[k], Rps[k]

                    for kindc, c in fm_list:
                        wt, wr = pipe.get()
                        pa, Rpa = nextps()
                        for kc in range(16):
                            S.op("pe", (lambda wt=wt, kc=kc, pa=pa: lambda e: e.matmul(pa[:, 0:n], lhsT=wt[:, kc * 128:(kc + 1) * 128], rhs=T.hx[:, kc, 0:n], start=(kc == 0), stop=(kc == 15)))(),
                                 reads=[wr, T.Rhx], writes=[Rpa])
                        if kindc == "qk":
                            wt2, wr2 = pipe.get()
                            pb_, Rpb = nextps()
                            for kc in range(16):
                                S.op("pe", (lambda wt2=wt2, kc=kc, pb_=pb_: lambda e: e.matmul(pb_[:, 0:n], lhsT=wt2[:, kc * 128:(kc + 1) * 128], rhs=T.hx[:, kc, 0:n], start=(kc == 0), stop=(kc == 15)))(),
                                     reads=[wr2, T.Rhx], writes=[Rpb])
                            S.op("dve", (lambda pa=pa, rp=rp: lambda e: e.tensor_tensor(out=T.tmpf[2][:, 0:n], in0=pa[:, 0:n], in1=rp[:, 0, 0:n], op=ALU.mult))(), reads=[Rpa, Rrp], writes=[T.Rtmpf[2]])
                            S.op("dve", (lambda pb_=pb_, rp=rp: lambda e: e.tensor_tensor(out=T.tmpf[3][:, 0:n], in0=pb_[:, 0:n], in1=rp[:, 1, 0:n], op=ALU.mult))(), reads=[Rpb, Rrp], writes=[T.Rtmpf[3]])
                            st, rst = T.stg.next()
                            S.op("pool", (lambda st=st: lambda e: e.tensor_tensor(out=st[:, 0:n], in0=T.tmpf[2][:, 0:n], in1=T.tmpf[3][:, 0:n], op=ALU.add))(), reads=[T.Rtmpf[2], T.Rtmpf[3]], writes=[rst])
                            if c < 8:
                                dst = qT[c * 128:(c + 1) * 128, t0:t0 + n]
                            elif kind == 0:
                                dst = exk_in[(c - 8) * 128:(c - 7) * 128, t0:t0 + n]
                            else:
                                dst = exkc_in[(c - 8) * 128:(c - 7) * 128, 0:NC]
                            S.op("sp", (lambda st=st, dst=dst: lambda e: e.dma_start(out=dst, in_=st[:, 0:n]))(), reads=[rst], dma_res=rst, store=True)
                        elif kindc == "su":
                            S.op("act", (lambda pa=pa, c=c: lambda e: e.activation(out=sut[:, c, 0:n], in_=pa[:, 0:n], func=AF.Gelu))(), reads=[Rpa], writes=[Rsut])
                        elif kindc == "f":
                            S.op("dve", (lambda pa=pa, c=c: lambda e: e.tensor_copy(out=ft[:, c, 0:n], in_=pa[:, 0:n]))(), reads=[Rpa], writes=[Rft])
                        else:
                            st, rst = T.stg.next()
                            S.op("act", (lambda pa=pa, st=st: lambda e: e.activation(out=st[:, 0:n], in_=pa[:, 0:n], func=AF.Sigmoid))(), reads=[Rpa], writes=[rst])
                            S.op("sp", (lambda st=st, c=c: lambda e: e.dma_start(out=gatesT[c * 128:(c + 1) * 128, t0:t0 + n], in_=st[:, 0:n]))(), reads=[rst], dma_res=rst, store=True)

                    for wh in tm_list:
                        for kq in range(4):
                            wt, wr = pipe.get()
                            for s in range(nsub):
                                for k4 in range(4):
                                    kc = kq * 4 + k4
                                    S.op("pe", (lambda wt=wt, s=s, k4=k4, kc=kc: lambda e: e.matmul(psum[s][:, :], lhsT=T.hx[:, kc, s * 128:(s + 1) * 128], rhs=wt[:, k4 * 512:(k4 + 1) * 512], start=(kc == 0), stop=(kc == 15)))(),
                                         reads=[wr, T.Rhx], writes=[Rps[s]])
                        col0 = (wh % 2) * 512
                        for s in range(nsub):
                            if wh < 2:
                                st, rst = T.stg.next()
                                S.op("dve", (lambda s=s, st=st: lambda e: e.tensor_copy(out=st[:, 0:512], in_=psum[s][:, :]))(), reads=[Rps[s]], writes=[rst])
                                if kind == 0:
                                    h0 = (wh % 2) * 4
                                    vdst = exv_in[h0:h0 + 4, t0 + s * 128:t0 + (s + 1) * 128, :].rearrange("h t e -> t h e")
                                    S.op("sp", (lambda st=st, vdst=vdst: lambda e: e.dma_start(out=vdst, in_=st[:, 0:512].rearrange("p (h e) -> p h e", e=128)))(), reads=[rst], dma_res=rst, store=True)
                                else:
                                    S.op("sp", (lambda s=s, st=st, col0=col0: lambda e: e.dma_start(out=exvc_in[s * 128:(s + 1) * 128, col0:col0 + 512], in_=st[:, 0:512]))(), reads=[rst], dma_res=rst, store=True)
                            else:
                                S.op("act", (lambda s=s, col0=col0: lambda e: e.activation(out=svall[s][:, col0:col0 + 512], in_=psum[s][:, :], func=AF.Gelu))(), reads=[Rps[s]], writes=[Rsvall[s]])

                    if need_all:
                        for s in range(nsub):
                            zs, Rzs = zst[s % 2], Rzst[s % 2]
                            S.op("dve", (lambda s=s: lambda e: e.tensor_tensor(out=junk[:], in0=svall[s][:], in1=svall[s][:], op=ALU.mult))(), reads=[Rsvall[s]], writes=[Rsvf])
                            S.op("dve", lambda e: e.tensor_reduce(out=ssq[:, 0:1], in_=junk[:], axis=AX.X, op=ALU.add), reads=[Rsvf], writes=[Rsvf])
                            S.op("act", lambda e: e.activation(out=ssq[:, 1:2], in_=ssq[:, 0:1], func=AF.Sqrt, scale=1.0 / 1024, bias=eps6), reads=[Rsvf, Reps], writes=[Rsvf])
                            S.op("dve", lambda e: e.reciprocal(out=ssq[:, 1:2], in_=ssq[:, 1:2]), reads=[Rsvf], writes=[Rsvf])
                            S.op("dve", (lambda s=s: lambda e: e.scalar_tensor_tensor(out=vn[:], in0=svall[s][:], scalar=ssq[:, 1:2], in1=sgg[:], op0=ALU.mult, op1=ALU.mult))(), reads=[Rsvall[s], Rsvf, Rsg], writes=[Rvn])
                            for g in range(8):
                                pg, Rpg = psum[4 + g % 2], Rps[4 + g % 2]
                                S.op("pe", (lambda g=g, pg=pg: lambda e: e.matmul(pg[:, 0:128], lhsT=vn[:, g * 128:(g + 1) * 128], rhs=sgw_b[:, g * 128:(g + 1) * 128], start=True, stop=True))(),
                                     reads=[Rvn, Rsg], writes=[Rpg])
                                S.op("dve", (lambda g=g, pg=pg: lambda e: e.tensor_tensor(out=T.tmpf[2][:, 0:128], in0=pg[:, 0:128], in1=sgb[:, g * 128:(g + 1) * 128], op=ALU.add))(), reads=[Rpg, Rsg], writes=[T.Rtmpf[2]])
                                S.op("dve", (lambda g=g, s=s: lambda e: e.tensor_tensor(out=sut[:, g, s * 128:(s + 1) * 128], in0=T.tmpf[2][:, 0:128], in1=sut[:, g, s * 128:(s + 1) * 128], op=ALU.mult))(), reads=[T.Rtmpf[2], Rsut], writes=[Rsut])
                            for g in range(4):
                                pz, Rpz = psum[6 + g % 2], Rps[6 + g % 2]
                                for k2 in range(2):
                                    S.op("pe", (lambda g=g, k2=k2, pz=pz, s=s: lambda e: e.matmul(pz[:, :], lhsT=ft[:, g * 2 + k2, s * 128:(s + 1) * 128], rhs=Wc[k2], start=(k2 == 0), stop=(k2 == 1)))(),
                                         reads=[Rft, Rcb], writes=[Rpz])
                                S.op("act", (lambda g=g, pz=pz, zs=zs: lambda e: e.activation(out=zs[:, g * 512:(g + 1) * 512], in_=pz[:, :], func=AF.Identity))(), reads=[Rpz], writes=[Rzs])
                            zdst = (exz_in[:, t0 + s * 128:t0 + (s + 1) * 128, :] if kind == 0 else zctx[:, s * 128:(s + 1) * 128, :]).rearrange("q t c -> t q c")
                            S.op("sp", (lambda zs=zs, zdst=zdst: lambda e: e.dma_start(out=zdst, in_=zs[:].rearrange("p (q c) -> p q c", c=128)))(), reads=[Rzs], dma_res=Rzs, store=True)
                        S.op("sp", (lambda t0=t0, n=n: lambda e: e.dma_start(out=sguT[:, t0:t0 + n].rearrange("(g p) t -> p g t", p=P), in_=sut[:, :, 0:n]))(), reads=[Rsut], dma_res=Rsut, store=True)
                for bi, (t0, n) in enumerate(NBLK):
                    p1_block(bi, t0, n)
                phase_end()

            if "ag" in phases:
                pieces = []
                for i in range(16):
                    pieces.append((exk_in[i * 64:(i + 1) * 64, :], exk_out[i]))
                for i in range(8):
                    pieces.append((exv_in[i], exv_out[i]))
                for i in range(16):
                    pieces.append((exz_in[i], exz_out[i]))
                pieces.append((exkc_in, exkc_out))
                pieces.append((exvc_in, exvc_out))
                for (pi, po) in pieces:
                    S.op("pool", (lambda pi=pi, po=po: lambda e: e.collective_compute("AllGather", ALU.bypass, replica_groups=[[0, 1, 2, 3], [4, 5, 6, 7]], ins=[pi.opt()], outs=[po.opt()]))(),
                         writes=[Rexout], dma_res=Rexout, dma_inc=1)


            if "p2" in phases:
                phase_begin()
                KT = sb("KT", [P, 16640], BF16)
                Vt = sb("Vt", [P, 130, 128], BF16)
                QT = sb("QT", [P, TT], BF16)
                Rkv = Res("kv")
                ptr = Ring([sb("pt%d" % i, [P, 1024], BF16) for i in range(3)], "pt")
                ef = [sb("ef%d" % i, [P, 512]) for i in range(3)]
                Ref = [Res("ef%d" % i) for i in range(3)]
                e3b = sb("e3b", [P, 512], BF16); Re3 = Res("e3b")
                ostg = Ring([sb("ostg%d" % i, [P, 512], BF16) for i in range(2)], "ostg")
                RS = [Res("S0"), Res("S1")]
                RO = [Res("O0"), Res("L0"), Res("O1"), Res("L1")]
                Obank = [psum[4], psum[5], psum[6], psum[7]]
                qblocks = [(i * 512, 512, list(range(130))) for i in range(8)]
                if not last:
                    qblocks.append((NL, NC, [128, 129]))
                def p2_head(h):
                    for m in range(2):
                        S.op("sp", (lambda m=m: lambda e: e.dma_start(out=KT[m * 64:(m + 1) * 64, 0:4 * NL].rearrange("p (r t) -> p r t", r=4), in_=exk_out[m * 8 + h].rearrange("r f t -> f r t")))(),
                             reads=[Rexout], writes=[Rkv], dma_res=Rkv)
                        S.op("sp", (lambda m=m: lambda e: e.dma_start(out=KT[m * 64:(m + 1) * 64, 16384:16640], in_=exkc_out[0, m * 512 + h * 64:m * 512 + (h + 1) * 64, :]))(),
                             reads=[Rexout], writes=[Rkv], dma_res=Rkv)
                        S.op("sp", (lambda m=m: lambda e: e.dma_start(out=QT[m * 64:(m + 1) * 64, :], in_=qT[m * 512 + h * 64:m * 512 + (h + 1) * 64, :]))(),
                             writes=[Rkv], dma_res=Rkv)
                    S.op("sp", lambda e: e.dma_start(out=Vt[:, 0:128, :], in_=exv_out[h].rearrange("(b p) c -> p b c", p=P)), reads=[Rexout], writes=[Rkv], dma_res=Rkv)
                    S.op("sp", lambda e: e.dma_start(out=Vt[:, 128:130, :], in_=exvc_out[0, :, h * 128:(h + 1) * 128].rearrange("(b p) c -> p b c", p=P)), reads=[Rexout], writes=[Rkv], dma_res=Rkv)

                    def p2_qblock(q0, nq, kbs):
                        def qk(i):
                            kb = kbs[i]
                            Sx = ps2[i % 2]
                            for m in range(2):
                                S.op("pe", (lambda m=m, kb=kb, Sx=Sx: lambda e: e.matmul(Sx[:, m * 512:m * 512 + nq], lhsT=KT[m * 64:(m + 1) * 64, kb * 128:(kb + 1) * 128], rhs=QT[m * 64:(m + 1) * 64, q0:q0 + nq], start=True, stop=True))(),
                                     reads=[Rkv], writes=[RS[i % 2]])
                        qk(0)
                        nk = len(kbs)
                        for i in range(nk):
                            kb = kbs[i]
                            if i + 1 < nk:
                                qk(i + 1)
                            Sx = ps2[i % 2]
                            pt, Rpt = ptr.next()
                            S.op("act", (lambda Sx=Sx, pt=pt: lambda e: e.activation(out=pt[:].rearrange("p (m q) -> p m q", m=2)[:, :, 0:nq], in_=Sx[:].rearrange("p (m q) -> p m q", m=2)[:, :, 0:nq], func=AF.Exp, scale=0.125))(),
                                 reads=[RS[i % 2]], writes=[Rpt])
                            for m in range(2):
                                S.op("pe", (lambda m=m, kb=kb, pt=pt, i=i: lambda e: e.matmul(Obank[2 * m][:, 0:nq], lhsT=Vt[:, kb, :], rhs=pt[:, m * 512:m * 512 + nq], start=(i == 0), stop=(i == nk - 1)))(),
                                     reads=[Rkv, Rpt], writes=[RO[2 * m]])
                                S.op("pe", (lambda m=m, pt=pt, i=i: lambda e: e.matmul(Obank[2 * m + 1][:, 0:nq], lhsT=ones_b, rhs=pt[:, m * 512:m * 512 + nq], start=(i == 0), stop=(i == nk - 1)))(),
                                     reads=[Rcb, Rpt], writes=[RO[2 * m + 1]])
                        S.op("dve", lambda e: e.reciprocal(out=ef[0][:, 0:nq], in_=Obank[1][:, 0:nq]), reads=[RO[1]], writes=[Ref[0]])
                        S.op("dve", lambda e: e.reciprocal(out=ef[1][:, 0:nq], in_=Obank[3][:, 0:nq]), reads=[RO[3]], writes=[Ref[1]])
                        S.op("dve", lambda e: e.tensor_tensor(out=ef[0][:, 0:nq], in0=Obank[0][:, 0:nq], in1=ef[0][:, 0:nq], op=ALU.mult), reads=[RO[0], Ref[0]], writes=[Ref[0]])
                        S.op("dve", lambda e: e.tensor_tensor(out=ef[1][:, 0:nq], in0=Obank[2][:, 0:nq], in1=ef[1][:, 0:nq], op=ALU.mult), reads=[RO[2], Ref[1]], writes=[Ref[1]])
                        S.op("dve", lambda e: e.scalar_tensor_tensor(out=ef[2][:, 0:nq], in0=ef[1][:, 0:nq], scalar=neglam[:, l:l + 1], in1=ef[0][:, 0:nq], op0=ALU.mult, op1=ALU.add), reads=[Ref[0], Ref[1], Rlam], writes=[Ref[2]])
                        S.op("act", lambda e: e.activation(out=e3b[:, 0:nq], in_=ef[2][:, 0:nq], func=AF.Square), reads=[Ref[2]], writes=[Re3])
                        S.op("pe", lambda e: e.matmul(ps2[0][:, 0:nq], lhsT=ones_b, rhs=e3b[:, 0:nq], start=True, stop=True), reads=[Re3, Rcb], writes=[RS[0]])
                        S.op("act", lambda e: e.activation(out=ef[0][:, 0:nq], in_=ps2[0][:, 0:nq], func=AF.Sqrt, scale=1.0 / 128, bias=eps5), reads=[RS[0], Reps], writes=[Ref[0]])
                        S.op("dve", lambda e: e.reciprocal(out=ef[0][:, 0:nq], in_=ef[0][:, 0:nq]), reads=[Ref[0]], writes=[Ref[0]])
                        st, rst = ostg.next()
                        S.op("dve", (lambda st=st: lambda e: e.scalar_tensor_tensor(out=st[:, 0:nq], in0=ef[2][:, 0:nq], scalar=sublt[:, l:l + 1], in1=ef[0][:, 0:nq], op0=ALU.mult, op1=ALU.mult))(), reads=[Ref[2], Ref[0], Rlam], writes=[rst])
                        S.op("sp", (lambda st=st: lambda e: e.dma_start(out=attT[h * 128:(h + 1) * 128, q0:q0 + nq], in_=st[:, 0:nq]))(), reads=[rst], dma_res=rst, store=True)
                    for (q0, nq, kbs) in qblocks:
                        p2_qblock(q0, nq, kbs)
                for h in range(8):
                    p2_head(h)
                phase_end()

            if "four" in phases:
                phase_begin()
                Zt = sb("Zt", [P, 128, 128], BF16); RZt = Res("Zt")
                Bt = sb("Bt", [P, 2, 64, 128], BF16); RBt = Res("Bt")
                Yt = sb("Yt", [P, 128, 256], BF16); RYt = Res("Yt")
                tw = [sb("tw%d" % i, [P, 512]) for i in range(4)]
                Rtw = [Res("tw%d" % i) for i in range(4)]
                Tc4 = Tc.unsqueeze(1).to_broadcast([P, 4, 128])
                Ts4 = Ts.unsqueeze(1).to_broadcast([P, 4, 128])
                for g in range(4):
                    for cq in range(4):
                        q = g * 4 + cq
                        S.op("sp", (lambda q=q: lambda e: e.dma_start(out=Zt[:, :, :], in_=exz_out[q].rearrange("(a n) c -> a n c", n=128)))(),
                             reads=[Rexout], writes=[RZt], dma_res=RZt)
                        for cp in range(32):
                            pz, Rpz = psum[cp % 2], Rps[cp % 2]
                            for ci in range(2):
                                c = 2 * cp + ci
                                S.op("pe", (lambda c=c, ci=ci, pz=pz: lambda e: e.matmul(pz[:, ci * 256:(ci + 1) * 256], lhsT=Zt[:, :, c], rhs=CS1, start=True, stop=False))(), reads=[RZt, Rcb], writes=[Rpz])
                                S.op("pe", (lambda c=c, ci=ci, pz=pz: lambda e: e.matmul(pz[:, ci * 256:(ci + 1) * 256], lhsT=Zt[:, :, 64 + c], rhs=CS2, start=False, stop=True))(), reads=[RZt, Rcb], writes=[Rpz])
                            k = (cp % 2) * 2
                            t1, t2 = tw[k], tw[k + 1]
                            S.op("dve", (lambda pz=pz, t1=t1: lambda e: e.tensor_tensor(out=t1[:].rearrange("p (a k) -> p a k", k=128), in0=pz.rearrange("p (a k) -> p a k", k=128), in1=Tc4, op=ALU.mult))(), reads=[Rpz, Rcst], writes=[Rtw[k]])
                            S.op("dve", (lambda pz=pz, t2=t2: lambda e: e.tensor_tensor(out=t2[:].rearrange("p (a k) -> p a k", k=128), in0=pz.rearrange("p (a k) -> p a k", k=128), in1=Ts4, op=ALU.mult))(), reads=[Rpz, Rcst], writes=[Rtw[k + 1]])
                            t1v = t1[:].rearrange("p (c r k) -> p c r k", c=2, r=2)
                            t2v = t2[:].rearrange("p (c r k) -> p c r k", c=2, r=2)
                            S.op("pool", (lambda cp=cp, t1v=t1v, t2v=t2v: lambda e: e.tensor_tensor(out=Bt[:, 0, 2 * cp:2 * cp + 2, :], in0=t1v[:, :, 0, :], in1=t2v[:, :, 1, :], op=ALU.add))(), reads=[Rtw[k], Rtw[k + 1]], writes=[RBt])
                            S.op("pool", (lambda cp=cp, t1v=t1v, t2v=t2v: lambda e: e.tensor_tensor(out=Bt[:, 1, 2 * cp:2 * cp + 2, :], in0=t1v[:, :, 1, :], in1=t2v[:, :, 0, :], op=ALU.subtract))(), reads=[Rtw[k], Rtw[k + 1]], writes=[RBt])
                        for cg in range(16):
                            py, Rpy = psum[2 + cg % 2], Rps[2 + cg % 2]
                            S.op("pe", (lambda cg=cg, py=py: lambda e: e.matmul(py[0:32, :], lhsT=cownb[:, 0:32], rhs=Bt[:, 0, 4 * cg:4 * cg + 4, :], start=True, stop=False))(), reads=[RBt, Rcb], writes=[Rpy])
                            S.op("pe", (lambda cg=cg, py=py: lambda e: e.matmul(py[0:32, :], lhsT=cownb[:, 32:64], rhs=Bt[:, 1, 4 * cg:4 * cg + 4, :], start=False, stop=True))(), reads=[RBt, Rcb], writes=[Rpy])
                            ch0 = cq * 64 + 4 * cg
                            S.op("act", (lambda py=py, ch0=ch0: lambda e: e.activation(out=Yt[0:32, :, ch0:ch0 + 4].rearrange("p k c -> p c k"), in_=py[0:32, :].rearrange("p (c k) -> p c k", k=128), func=AF.Identity, scale=1.0 / 2048))(), reads=[Rpy], writes=[RYt])
                    S.op("sp", (lambda g=g: lambda e: e.dma_start(out=fourTM[0:NL, g * 256:(g + 1) * 256].rearrange("(a k) c -> a k c", k=128), in_=Yt[0:32, :, :]))(), reads=[RYt], dma_res=RYt, store=True)
                if not last:
                    Zc = sb("Zc", [P, 2, 16, 128], BF16); RZc = Res("Zc")
                    Yc = sb("Yc", [P, 2, 1024], BF16); RYc = Res("Yc")
                    for nch in range(2):
                        S.op("sp", (lambda nch=nch: lambda e: e.dma_start(out=Zc[:, nch, :, :], in_=zctx[:, nch * 128:(nch + 1) * 128, :].rearrange("q t c -> t q c")))(), writes=[RZc], dma_res=RZc)
                    for kch in range(2):
                        for g in range(4):
                            pc, Rpc = psum[4 + g % 2], Rps[4 + g % 2]
                            cnt = 0
                            for nch in range(2):
                                for ri in range(2):
                                    S.op("pe", (lambda nch=nch, ri=ri, g=g, kch=kch, pc=pc, cnt=cnt: lambda e: e.matmul(pc[:, 0:256], lhsT=Cx[nch][ri][:, kch * 128:(kch + 1) * 128], rhs=Zc[:, nch, 4 * g:4 * g + 4, ri * 64:(ri + 1) * 64], start=(cnt == 0), stop=(cnt == 3)))(),
                                         reads=[RZc, Rcb], writes=[Rpc])
                                    cnt += 1
                            S.op("act", (lambda pc=pc, kch=kch, g=g: lambda e: e.activation(out=Yc[:, kch, g * 256:(g + 1) * 256], in_=pc[:, 0:256], func=AF.Identity, scale=1.0 / 256))(), reads=[Rpc], writes=[RYc])
                    S.op("sp", lambda e: e.dma_start(out=fourTM[NL:TT, :].rearrange("(k p) c -> p k c", p=P), in_=Yc[:]), reads=[RYc], dma_res=RYc, store=True)
                phase_end()

            if "p3" in phases:
                phase_begin()
                T = common_tiles()
                hid = sb("hid", [P, 32, 512], BF16); Rhid = Res("hid")
                att = sb("att", [P, 8, 512], BF16); sgu = sb("sgu", [P, 8, 512], BF16)
                fourT = sb("fourT", [P, 8, 512], BF16); RfT = Res("fourT")
                fstg = sb("fstg", [P, 4, 1024], BF16)
                Rin = Res("p3in")
                Rfs = Res("fstg")
                gring = Ring([sb("gat%d" % i, [P, 3, 512], BF16) for i in range(2)], "gat")
                gsrc = gatesT.rearrange("(w f) t -> f w t", w=3)
                blocks = NBLK if not last else NBLK[:8]
                def p3_block(bi, t0, n):
                    kind = 0 if bi < 8 else 1
                    nsub = n // 128
                    S.op("sp", (lambda t0=t0, n=n: lambda e: e.dma_start(out=T.xt[:, :, 0:n], in_=xsrc[:, t0:t0 + n].rearrange("(k p) t -> p k t", p=P)))(), writes=[T.Rxt], dma_res=T.Rxt)
                    S.op("sp", (lambda t0=t0, n=n: lambda e: e.dma_start(out=att[:, :, 0:n], in_=attT[:, t0:t0 + n].rearrange("(k p) t -> p k t", p=P)))(), writes=[Rin], dma_res=Rin)
                    S.op("sp", (lambda t0=t0, n=n: lambda e: e.dma_start(out=sgu[:, :, 0:n], in_=sguT[:, t0:t0 + n].rearrange("(k p) t -> p k t", p=P)))(), writes=[Rin], dma_res=Rin)
                    S.op("sp", (lambda t0=t0, n=n, nsub=nsub: lambda e: e.dma_start(out=fstg[:, 0:nsub, :], in_=fourTM[t0:t0 + n, :].rearrange("(s p) c -> p s c", p=P)))(), writes=[Rfs], dma_res=Rfs)
                    for s in range(nsub):
                        pT, RpT = psum[6 + s % 2], Rps[6 + s % 2]
                        pTb = pT.bitcast(BF16)
                        for c in range(8):
                            S.op("pe", (lambda s=s, c=c, pTb=pTb: lambda e: e.transpose(pTb[:, c * 128:(c + 1) * 128], fstg[:, s, c * 128:(c + 1) * 128], ident_b))(), reads=[Rfs, Rcb], writes=[RpT])
                        S.op("dve", (lambda s=s, pTb=pTb: lambda e: e.tensor_copy(out=fourT[:, :, s * 128:(s + 1) * 128], in_=pTb[:, 0:1024].rearrange("p (c k) -> p c k", k=128)))(), reads=[RpT], writes=[RfT])
                    items = []
                    for j in range(16):
                        items += [(wb_pa[l, j], Rw[("pa", l)]), (wb_ps[l, j], Rw[("ps", l)]), (wb_pf[l, j], Rw[("pf", l)])]
                    items += [(wb_o[l, j], Rw[("o", l)]) for j in range(16)]
                    for half in range(2):
                        items += [(wb_1[l, half * 32 + m], Rw[("1", l)]) for m in range(32)]
                        for j in range(16):
                            items += [(wb_2[l, j * 4 + half * 2 + pp], Rw[("2", l)]) for pp in range(2)]
                    pipe = WPipe(items)
                    for j in range(16):
                        gt_, Rgt = gring.next()
                        S.op("sp", (lambda j=j, gt_=gt_: lambda e: e.dma_start(out=gt_[:, :, 0:n], in_=gsrc[j * 128:(j + 1) * 128, :, t0:t0 + n]))(), writes=[Rgt], dma_res=Rgt)
                        srcs = [(att, Rin), (sgu, Rin), (fourT, RfT)]
                        for bidx in range(3):
                            wt, wr = pipe.get()
                            src, Rsrc = srcs[bidx]
                            for kc in range(8):
                                S.op("pe", (lambda wt=wt, kc=kc, bidx=bidx, src=src: lambda e: e.matmul(psum[bidx][:, 0:n], lhsT=wt[:, kc * 128:(kc + 1) * 128], rhs=src[:, kc, 0:n], start=(kc == 0), stop=(kc == 7)))(),
                                     reads=[wr, Rsrc], writes=[Rps[bidx]])
                        S.op("dve", (lambda gt_=gt_: lambda e: e.tensor_tensor(out=T.tmpf[0][:, 0:n], in0=psum[0][:, 0:n], in1=gt_[:, 0, 0:n], op=ALU.mult))(), reads=[Rps[0], Rgt], writes=[T.Rtmpf[0]])
                        S.op("dve", (lambda gt_=gt_: lambda e: e.tensor_tensor(out=T.tmpf[1][:, 0:n], in0=psum[1][:, 0:n], in1=gt_[:, 1, 0:n], op=ALU.mult))(), reads=[Rps[1], Rgt], writes=[T.Rtmpf[1]])
                        S.op("dve", (lambda gt_=gt_: lambda e: e.tensor_tensor(out=T.tmpf[2][:, 0:n], in0=psum[2][:, 0:n], in1=gt_[:, 2, 0:n], op=ALU.mult))(), reads=[Rps[2], Rgt], writes=[T.Rtmpf[2]])
                        S.op("pool", lambda e: e.tensor_tensor(out=T.tmpf[0][:, 0:n], in0=T.tmpf[0][:, 0:n], in1=T.tmpf[1][:, 0:n], op=ALU.add), reads=[T.Rtmpf[0], T.Rtmpf[1]], writes=[T.Rtmpf[0]])
                        S.op("pool", (lambda j=j: lambda e: e.tensor_tensor(out=T.sqb[:, j, 0:n], in0=T.tmpf[0][:, 0:n], in1=T.tmpf[2][:, 0:n], op=ALU.add))(), reads=[T.Rtmpf[0], T.Rtmpf[2]], writes=[T.Rsq])
                    for j in range(16):
                        wt, wr = pipe.get()
                        pa, Rpa = psum[3 + j % 2], Rps[3 + j % 2]
                        for kc in range(16):
                            S.op("pe", (lambda wt=wt, kc=kc, pa=pa: lambda e: e.matmul(pa[:, 0:n], lhsT=wt[:, kc * 128:(kc + 1) * 128], rhs=T.sqb[:, kc, 0:n], start=(kc == 0), stop=(kc == 15)))(),
                                 reads=[wr, T.Rsq], writes=[Rpa])
                        S.op("dve", (lambda j=j, pa=pa: lambda e: e.scalar_tensor_tensor(out=T.xt[:, j, 0:n], in0=pa[:, 0:n], scalar=mod_col(l, 32, kind, j), in1=T.xt[:, j, 0:n], op0=ALU.mult, op1=ALU.add))(),
                             reads=[Rpa, T.Rxt, Rmod], writes=[T.Rxt])
                    norm_mod(T, l, 1, kind, n)
                    for half in range(2):
                        for m in range(32):
                            wt, wr = pipe.get()
                            pa, Rpa = psum[m % 4], Rps[m % 4]
                            for kc in range(16):
                                S.op("pe", (lambda wt=wt, kc=kc, pa=pa: lambda e: e.matmul(pa[:, 0:n], lhsT=wt[:, kc * 128:(kc + 1) * 128], rhs=T.hx[:, kc, 0:n], start=(kc == 0), stop=(kc == 15)))(),
                                     reads=[wr, T.Rhx], writes=[Rpa])
                            tf, rtf = T.tmpf[m % 4], T.Rtmpf[m % 4]
                            S.op("act", (lambda pa=pa, tf=tf: lambda e: e.activation(out=tf[:, 0:n], in_=pa[:, 0:n], func=AF.Relu))(), reads=[Rpa], writes=[rtf])
                            S.op("pool", (lambda m=m, tf=tf: lambda e: e.tensor_tensor(out=hid[:, m, 0:n], in0=tf[:, 0:n], in1=tf[:, 0:n], op=ALU.mult))(), reads=[rtf], writes=[Rhid])
                        for j in range(16):
                            pa, Rpa = psum[4 + j % 2], Rps[4 + j % 2]
                            for pp in range(2):
                                wt, wr = pipe.get()
                                for kc in range(16):
                                    S.op("pe", (lambda wt=wt, kc=kc, pp=pp, pa=pa: lambda e: e.matmul(pa[:, 0:n], lhsT=wt[:, kc * 128:(kc + 1) * 128], rhs=hid[:, pp * 16 + kc, 0:n], start=(pp == 0 and kc == 0), stop=(pp == 1 and kc == 15)))(),
                                         reads=[wr, Rhid], writes=[Rpa])
                            S.op("dve", (lambda j=j, pa=pa: lambda e: e.scalar_tensor_tensor(out=T.xt[:, j, 0:n], in0=pa[:, 0:n], scalar=mod_col(l, 80, kind, j), in1=T.xt[:, j, 0:n], op0=ALU.mult, op1=ALU.add))(),
                                 reads=[Rpa, T.Rxt, Rmod], writes=[T.Rxt])
                    if not last:
                        S.op("sp", (lambda t0=t0, n=n: lambda e: e.dma_start(out=xT1[:, t0:t0 + n].rearrange("(k p) t -> p k t", p=P), in_=T.xt[:, :, 0:n]))(), reads=[T.Rxt], dma_res=T.Rxt, store=True)
                    else:
                        norm_stats(T, n)
                        for kc in range(16):
                            S.op("dve", (lambda kc=kc: lambda e: e.scalar_tensor_tensor(out=T.xt[:, kc, 0:n], in0=T.xt[:, kc, 0:n], scalar=gft[:, kc:kc + 1], in1=T.rstd[:, 0:n], op0=ALU.mult, op1=ALU.mult))(),
                                 reads=[T.Rxt, T.Rrstd, Rsm], writes=[T.Rxt])
                        S.op("sp", (lambda t0=t0, n=n: lambda e: e.dma_start(out=outT[:, t0:t0 + n].rearrange("(k p) t -> p k t", p=P), in_=T.xt[:, :, 0:n]))(), reads=[T.Rxt], dma_res=T.Rxt, store=True)
                for bi, (t0, n) in enumerate(blocks):
                    p3_block(bi, t0, n)
                phase_end()

        for l in range(n_layers):
            do_layer(l)

        if dump:
            srcmap = {"qT": qT, "gatesT": gatesT, "sguT": sguT, "attT": attT, "fourTM": fourTM, "xT1": xT1,
                      "modT": None}
            Rd = Res("dump")
            for nm, shape, dt in dump:
                if nm == "modT":
                    S.op("sp", lambda e: e.dma_start(out=dumps["modT"], in_=modT[:].rearrange("p l j k -> p (l j k)")), reads=[Rmod], dma_res=Rd, store=True)
                else:
                    src = srcmap[nm]
                    S.op("sp", (lambda nm=nm, src=src: lambda e: e.dma_start(out=dumps[nm], in_=src))(), dma_res=Rd, store=True)
        S.full_barrier()
        block = es.enter_context(nc.Block())
        S.emit(block)
    return nc


def _tile_fm(W):
    K, M = W.shape
    t = W.reshape(K // 128, 128, M // 128, 128).transpose(2, 1, 0, 3)
    return np.ascontiguousarray(t).reshape(M // 128, 128, (K // 128) * 128)


def _tile_tm(W):
    K, M = W.shape
    nb = M // 512
    t = W.reshape(4, 4, 128, nb, 512).transpose(3, 0, 2, 1, 4)
    return np.ascontiguousarray(t).reshape(nb * 4, 128, 2048)


def _constants():
    f32 = np.float32
    A = np.zeros((128, 2944), f32)
    A[:, 0:128] = 1.0
    A[:, 128:256] = np.eye(128, dtype=f32)
    n = np.arange(128)
    ang = 2 * np.pi * np.outer(n, n) / 128.0
    C, Sn = np.cos(ang), np.sin(ang)
    A[:, 256:384], A[:, 384:512] = C, -Sn
    A[:, 512:640], A[:, 640:768] = Sn, C
    ch = np.arange(256)
    a2 = 2 * np.pi * np.outer(ch, ch) / 256.0
    Cc, Sc = np.cos(a2), np.sin(a2)
    Wp = np.zeros((256, 512))
    for cq in range(4):
        Wp[:, cq * 128:cq * 128 + 64] = Cc[:, cq * 64:(cq + 1) * 64]
        Wp[:, cq * 128 + 64:cq * 128 + 128] = -Sc[:, cq * 64:(cq + 1) * 64]
    A[:, 768:1280] = Wp[0:128]
    A[:, 1280:1792] = Wp[128:256]
    for nch in range(2):
        A[:, 1792 + nch * 512:1792 + nch * 512 + 256] = Cc[nch * 128:(nch + 1) * 128]
        A[:, 1792 + nch * 512 + 256:1792 + (nch + 1) * 512] = Sc[nch * 128:(nch + 1) * 128]
    Bc = np.zeros((8, 128, 320), f32)
    at = 2 * np.pi * np.outer(n, n) / 16384.0
    Bc[:, :, 0:128] = np.cos(at)
    Bc[:, :, 128:256] = np.sin(at)
    for core in range(8):
        rk = core % 4
        Bc[core, :, 256:288] = C[:, 32 * rk:32 * rk + 32]
        Bc[core, :, 288:320] = Sn[:, 32 * rk:32 * rk + 32]
    return A, Bc


def _rope_tables():
    f32 = np.float32
    n_freq = 16
    inv = (np.float32(10000.0) ** (-np.arange(n_freq, dtype=f32) / f32(n_freq))).astype(f32)
    tok = np.arange(16384)
    row = (tok // 64).astype(f32)
    col = (tok % 64).astype(f32)
    ar = row[:, None] * inv
    ac = col[:, None] * inv
    ang = np.concatenate([ar, ar, ac, ac], axis=-1).astype(f32)
    cos, sin = np.cos(ang).astype(f32), np.sin(ang).astype(f32)
    d = np.arange(64)
    sign = np.where((d % 32) < 16, -1.0, 1.0).astype(f32)
    sin_s = sin * sign[None, :]
    return cos.T.copy(), sin_s.T.copy()


_ROT_PERM = np.array([(d + 16) if (d % 32) < 16 else (d - 16) for d in range(64)])


def _prep_inputs(inp):
    f32 = np.float32
    g = {k: np.asarray(v, dtype=f32) for k, v in inp.items()}
    A, Bc = _constants()
    cosT, sinT = _rope_tables()
    shared = {}
    shared["w_mod"] = np.stack([_tile_fm(g["w_mod"][l]) for l in range(DEPTH)])
    shared["b_modT"] = np.ascontiguousarray(g["b_mod"].reshape(DEPTH, 96, 128).transpose(0, 2, 1))
    shared["g1T"] = np.ascontiguousarray(g["norm1_g"].reshape(DEPTH, 16, 128).transpose(0, 2, 1))
    shared["g2T"] = np.ascontiguousarray(g["norm2_g"].reshape(DEPTH, 16, 128).transpose(0, 2, 1))
    shared["gfT"] = np.ascontiguousarray(g["final_g"].reshape(16, 128).T)
    w_fm, w_tm = [], []
    perm_cols = np.concatenate([hh * 64 + _ROT_PERM for hh in range(32)])
    for l in range(DEPTH):
        W = g["w_in"][l]
        qk = W[:, 0:2048]
        fm_cols = np.concatenate([qk, qk[:, perm_cols], W[:, 3072:4096], W[:, 5120:6144], W[:, 6144:12288]], axis=1)
        w_fm.append(_tile_fm(fm_cols))
        tm_cols = np.concatenate([W[:, 2048:3072], W[:, 4096:5120]], axis=1)
        w_tm.append(_tile_tm(tm_cols))
    shared["w_fm"] = np.stack(w_fm)
    shared["w_tm"] = np.stack(w_tm)
    lam = np.stack([g["lambda_q1"], g["lambda_k1"], g["lambda_q2"], g["lambda_k2"]], axis=1)
    shared["lamv"] = np.ascontiguousarray(np.broadcast_to(lam[:, None], (DEPTH, 128, 4, 64)))
    shared["sublnT"] = np.ascontiguousarray(g["subln_g"].reshape(DEPTH, 128, 1))
    shared["sgu_gbc"] = np.ascontiguousarray(np.broadcast_to(g["sgu_norm_g"][:, None, :], (DEPTH, 128, 1024)))
    shared["sgu_wT"] = np.ascontiguousarray(g["sgu_w"].transpose(0, 3, 1, 2)).reshape(DEPTH, 128, 1024)
    shared["sgu_bbc"] = np.ascontiguousarray(np.broadcast_to(g["sgu_b"].reshape(DEPTH, 1, 1024), (DEPTH, 128, 1024)))
    shared["w_pa"] = np.stack([_tile_fm(g["w_proj_att"][l]) for l in range(DEPTH)])
    shared["w_ps"] = np.stack([_tile_fm(g["w_proj_sgu"][l]) for l in range(DEPTH)])
    shared["w_pf"] = np.stack([_tile_fm(g["w_proj_fourier"][l]) for l in range(DEPTH)])
    shared["w_o"] = np.stack([_tile_fm(g["w_out"][l]) for l in range(DEPTH)])
    shared["w_1"] = np.stack([_tile_fm(g["w_mlp_in"][l]) for l in range(DEPTH)])
    w2 = []
    for l in range(DEPTH):
        t = _tile_fm(g["w_mlp_out"][l])
        t = t.reshape(16, 128, 4, 2048).transpose(0, 2, 1, 3)
        w2.append(np.ascontiguousarray(t).reshape(64, 128, 2048))
    shared["w_2"] = np.stack(w2)
    shared["constsA"] = A
    in_maps = []
    for core in range(8):
        b, rk = core // 4, core % 4
        m = dict(shared)
        xin = np.zeros((D, TT), f32)
        xin[:, 0:NL] = g["x"][b, rk * NL:(rk + 1) * NL, :].T
        if rk == 0:
            xin[:, NL:TT] = g["ctx"][b].T
        m["xin"] = xin
        cv = np.stack([g["c"][b], g["c_ctx"]], axis=-1)
        m["cvec"] = np.ascontiguousarray(cv.reshape(16, 128, 2).transpose(1, 0, 2))
        rope = np.zeros((128, 2, TT), f32)
        rope[:, 0, NL:TT] = 1.0
        cs = cosT[:, rk * NL:(rk + 1) * NL]
        sn = sinT[:, rk * NL:(rk + 1) * NL]
        rope[0:64, 0, 0:NL] = cs
        rope[64:128, 0, 0:NL] = cs
        rope[0:64, 1, 0:NL] = sn
        rope[64:128, 1, 0:NL] = sn
        m["ropeT"] = rope
        m["constsB"] = Bc[core]
        in_maps.append(m)
    return in_maps


_NC_CACHE = {}


def kernel(**inputs):
    in_maps = _prep_inputs(inputs)
    if "nc" not in _NC_CACHE:
        _NC_CACHE["nc"] = build_program()
    nc = _NC_CACHE["nc"]
    res = run_bass_kernel_spmd(nc, in_maps, core_ids=list(range(8)))
    out = np.empty((2, 16384, D), np.float32)
    for core in range(8):
        b, rk = core // 4, core % 4
        out[b, rk * NL:(rk + 1) * NL, :] = res.results[core]["outT"].T
    return out
```

```python
import contextlib
import math
import numpy as np
import concourse.bass as bass
import concourse.mybir as mybir
from concourse.bass_utils import run_bass_kernel_spmd

F32 = mybir.dt.float32
BF16 = mybir.dt.bfloat16
AF = mybir.ActivationFunctionType
ALU = mybir.AluOpType
AX = mybir.AxisListType

D = 2048
NL = 4096
NC = 256
TT = NL + NC
NBLK = [(i * 512, 512) for i in range(8)] + [(NL, NC)]
DEPTH = 2
EXN = TT * 4096
KT_OFF, V_OFF, Z_OFF = 0, 1024 * TT, 2048 * TT
ENGS = ("pe", "act", "dve", "pool", "sp")


class Res:
    __slots__ = ("name", "last_w", "readers", "sem", "cnt", "inh")

    def __init__(self, name):
        self.name = name
        self.last_w = None
        self.readers = []
        self.sem = None
        self.cnt = 0
        self.inh = []


class Sched:
    def __init__(self, nc, sem_alloc):
        self.nc = nc
        self.sem_alloc = sem_alloc
        self.streams = {e: [] for e in ENGS}
        self.count = {e: 0 for e in ENGS}
        self.esem = {e: sem_alloc("c_" + e) for e in ENGS if e != "sp"}
        self.waited = {e: {} for e in ENGS}
        self.pending = []
        self.nsem = 4
        self.last = {}
        self.free = []
        self.phase_stack = []

    def _need(self, eng, tok, raw):
        sem, val, teng = tok[0], tok[1], tok[2]
        if teng == eng and (eng in ("pe", "sp") or not raw):
            return False
        key = id(sem)
        if self.waited[eng].get(key, 0) >= val:
            return False
        self.waited[eng][key] = val
        return True

    def op(self, eng, fn, reads=(), writes=(), dma_res=None, extra=(), store=False, dma_inc=16):
        is_dma = dma_res is not None
        deps = [(t, False) for t in extra]
        follow = set()
        for r in reads:
            if r.last_w is not None:
                deps.append((r.last_w, True))
        for w in writes:
            if is_dma and w is dma_res and w.last_w is not None and w.last_w[3] and not w.readers:
                deps.extend(w.inh)
                follow.add(id(w))
            else:
                if w.last_w is not None:
                    deps.append((w.last_w, False))
                deps.extend((t, False) for t in w.readers)
        waits = []
        for t, raw in deps:
            if self._need(eng, t, raw):
                waits.append((t[0], t[1]))
        if is_dma:
            if dma_res.sem is None:
                if self.free:
                    dma_res.sem, dma_res.cnt = self.free.pop()
                else:
                    dma_res.sem, dma_res.cnt = self.sem_alloc("d%d_%s" % (self.nsem, dma_res.name)), 0
                    self.nsem += 1
                if self.phase_stack:
                    self.phase_stack[-1].append(dma_res)
            dma_res.cnt += dma_inc
            tok = (dma_res.sem, dma_res.cnt, None, True)
            inc = (dma_res.sem, dma_inc)
        else:
            self.count[eng] += 1
            tok = (self.esem[eng], self.count[eng], eng, False)
            inc = (self.esem[eng], 1)
        self.streams[eng].append((waits, fn, inc))
        if not is_dma:
            self.last[eng] = tok
        for r in reads:
            r.readers.append(tok)
        for w in writes:
            if id(w) not in follow:
                w.inh = list(deps)
            w.last_w = tok
            w.readers = []
        if store:
            self.pending.append(tok)
        return tok

    def ensure_sem(self, res):
        if res.sem is None:
            res.sem, res.cnt = self.sem_alloc("g%d_%s" % (self.nsem, res.name)), 0
            self.nsem += 1

    def barrier(self, eng="sp"):
        waits = []
        for t in self.pending:
            if self._need(eng, t, True):
                waits.append((t[0], t[1]))
        self.pending = []
        self.streams[eng].append((waits, None, None))

    def full_barrier(self):
        toks = list(self.last.values()) + list(self.pending)
        self.pending = []
        for eng in ENGS:
            self.wait_tok(eng, toks)

    def wait_tok(self, eng, toks):
        waits = []
        for t in toks:
            if self._need(eng, t, True):
                waits.append((t[0], t[1]))
        self.streams[eng].append((waits, None, None))

    def emit(self, block):
        def run(stream):
            def f(e):
                for waits, fn, inc in stream:
                    for (s, v) in waits:
                        e.wait_ge(s, v)
                    if fn is not None:
                        fn(e).then_inc(inc[0], inc[1])
            return f
        block.tensor(run(self.streams["pe"]))
        block.scalar(run(self.streams["act"]))
        block.vector(run(self.streams["dve"]))
        block.gpsimd(run(self.streams["pool"]))
        block.sync(run(self.streams["sp"]))


class Ring:
    def __init__(self, tiles, name):
        self.tiles = tiles
        self.res = [Res("%s%d" % (name, i)) for i in range(len(tiles))]
        self.i = 0

    def next(self):
        k = self.i % len(self.tiles)
        self.i += 1
        return self.tiles[k], self.res[k]


def build_program(n_layers=DEPTH, phases=("p1", "ag", "p2", "four", "p3"), dump=None):
    nc = bass.Bass("TRN2", target_bir_lowering=False)
    P = 128

    def din(name, shape, dt=F32):
        return nc.dram_tensor(name, list(shape), dt, kind="ExternalInput").ap()

    def dint(name, shape, dt=BF16):
        return nc.dram_tensor(name, list(shape), dt).ap()

    xin = din("xin", [D, TT])
    cvec = din("cvec", [P, 16, 2])
    ropeT = din("ropeT", [P, 2, TT])
    w_mod = din("w_mod", [DEPTH, 96, P, 2048])
    b_modT = din("b_modT", [DEPTH, P, 96])
    g1T = din("g1T", [DEPTH, P, 16])
    g2T = din("g2T", [DEPTH, P, 16])
    gfT = din("gfT", [P, 16])
    w_fm = din("w_fm", [DEPTH, 96, P, 2048])
    w_tm = din("w_tm", [DEPTH, 16, P, 2048])
    lamv = din("lamv", [DEPTH, P, 4, 64])
    sublnT = din("sublnT", [DEPTH, P, 1])
    sgu_gbc = din("sgu_gbc", [DEPTH, P, 1024])
    sgu_wT = din("sgu_wT", [DEPTH, P, 1024])
    sgu_bbc = din("sgu_bbc", [DEPTH, P, 1024])
    w_pa = din("w_pa", [DEPTH, 16, P, 1024])
    w_ps = din("w_ps", [DEPTH, 16, P, 1024])
    w_pf = din("w_pf", [DEPTH, 16, P, 1024])
    w_o = din("w_o", [DEPTH, 16, P, 2048])
    w_1 = din("w_1", [DEPTH, 64, P, 2048])
    w_2 = din("w_2", [DEPTH, 64, P, 2048])
    constsA = din("constsA", [P, 2944])
    constsB = din("constsB", [P, 320])
    outT = nc.dram_tensor("outT", [D, NL], F32, kind="ExternalOutput").ap()
    dumps = {}

    wb_fm = dint("wb_fm", [DEPTH, 96, P, 2048])
    wb_tm = dint("wb_tm", [DEPTH, 16, P, 2048])
    wb_pa = dint("wb_pa", [DEPTH, 16, P, 1024])
    wb_ps = dint("wb_ps", [DEPTH, 16, P, 1024])
    wb_pf = dint("wb_pf", [DEPTH, 16, P, 1024])
    wb_o = dint("wb_o", [DEPTH, 16, P, 2048])
    wb_1 = dint("wb_1", [DEPTH, 64, P, 2048])
    wb_2 = dint("wb_2", [DEPTH, 64, P, 2048])
    qT = dint("qT", [1024, TT])
    gatesT = dint("gatesT", [6144, TT])
    sguT = dint("sguT", [1024, TT])
    attT = dint("attT", [1024, TT])
    fourTM = dint("fourTM", [TT, 1024])
    xT1 = dint("xT1", [D, TT], F32)
    exk_in = dint("exk_in", [1024, NL])
    exk_out = dint("exk_out", [16, 4, 64, NL])
    exv_in = dint("exv_in", [8, NL, 128])
    exv_out = dint("exv_out", [8, 4 * NL, 128])
    exz_in = dint("exz_in", [16, NL, 128])
    exz_out = dint("exz_out", [16, 4 * NL, 128])
    exkc_in = dint("exkc_in", [1024, NC])
    exkc_out = dint("exkc_out", [4, 1024, NC])
    exvc_in = dint("exvc_in", [NC, 1024])
    exvc_out = dint("exvc_out", [4, NC, 1024])
    zctx = dint("zctx", [16, NC, 128])
    if dump:
        for nm, shape, dt in dump:
            dumps[nm] = nc.dram_tensor("dump_" + nm, list(shape), dt, kind="ExternalOutput").ap()

    es = contextlib.ExitStack()
    with es:
        def sem_alloc(name):
            return es.enter_context(nc.semaphore(name))

        S = Sched(nc, sem_alloc)

        arena = {"cur": 16640}

        def sb(name, shape, dt=F32):
            n = 1
            for d_ in shape[1:]:
                n *= d_
            nbytes = n * (4 if dt == F32 else 2)
            off = (arena["cur"] + 63) // 64 * 64
            arena["cur"] = off + nbytes
            assert arena["cur"] <= 229376 - 256, (name, arena["cur"])
            uid = "%s_%d" % (name, off)
            return nc.alloc_sbuf_tensor_at(uid + "_%d" % len(arena), list(shape), dt, offset=off)

        marks = []

        def phase_begin():
            marks.append(arena["cur"])
            S.phase_stack.append([])
            if "wring" in arena:
                arena["wring"].res = [Res("wr%d" % i) for i in range(len(arena["wring"].tiles))]
            arena[len(arena)] = 0

        def phase_end():
            S.full_barrier()
            for r_ in S.phase_stack.pop():
                S.free.append((r_.sem, r_.cnt))
            arena["cur"] = marks.pop()
            arena[len(arena)] = 0

        ps2 = [nc.alloc_psum_tensor("ps2_%d" % i, [P, 1024], F32) for i in range(4)]
        psum = [ps2[i // 2][:, (i % 2) * 512:(i % 2 + 1) * 512] for i in range(8)]
        Rps = [Res("ps%d" % i) for i in range(8)]

        cstB = sb("cstB", [P, 320])
        cb = sb("cb", [P, 2944], BF16)
        cownb = sb("cownb", [P, 64], BF16)
        Rcst = Res("cstB")
        Rcb = Res("cb")
        S.op("sp", lambda e: e.dma_start(out=cstB[:], in_=constsB[:, :]), writes=[Rcst], dma_res=Rcst)
        phase_begin()
        cstA = sb("cstA", [P, 2944])
        RcA = Res("cstA")
        S.op("sp", lambda e: e.dma_start(out=cstA[:], in_=constsA[:, :]), writes=[RcA], dma_res=RcA)
        S.op("dve", lambda e: e.tensor_copy(out=cb[:], in_=cstA[:]), reads=[RcA], writes=[Rcb])
        S.op("dve", lambda e: e.tensor_copy(out=cownb[:], in_=cstB[:, 256:320]), reads=[Rcst], writes=[Rcb])
        phase_end()
        ones_b = cb[:, 0:128]
        ident_b = cb[:, 128:256]
        CS1 = cb[:, 256:512]
        CS2 = cb[:, 512:768]
        Wc = [cb[:, 768:1280], cb[:, 1280:1792]]
        Cx = [[cb[:, 1792 + n * 512:1792 + n * 512 + 256], cb[:, 1792 + n * 512 + 256:1792 + (n + 1) * 512]] for n in range(2)]
        Tc = cstB[:, 0:128]
        Ts = cstB[:, 128:256]
        eps6 = cstB[:, 256 - 0 + 0:256 + 0] if False else None
        epst = sb("epst", [P, 2])
        Reps = Res("eps")
        S.op("pool", lambda e: e.memset(epst[:, 0:1], 1e-6), writes=[Reps])
        S.op("pool", lambda e: e.memset(epst[:, 1:2], 1e-5), writes=[Reps])
        eps6 = epst[:, 0:1]
        eps5 = epst[:, 1:2]

        Rw = {}

        def cast_w(name, src, dst, l, rows):
            r = Res("w_%s%d" % (name, l))
            Rw[(name, l)] = r
            s2 = src[l].rearrange("r p x -> (r p) x")
            d2 = dst[l].rearrange("r p x -> (r p) x")
            n = rows * P
            step = 1024
            for a in range(0, n, step):
                b = min(n, a + step)
                S.op("pool", (lambda a=a, b=b: lambda e: e.dma_start(out=d2[a:b, :], in_=s2[a:b, :]))(),
                     writes=[r], dma_res=r)

        for l in range(n_layers):
            cast_w("fm", w_fm, wb_fm, l, 96)
            cast_w("tm", w_tm, wb_tm, l, 16)
            cast_w("pa", w_pa, wb_pa, l, 16)
            cast_w("ps", w_ps, wb_ps, l, 16)
            cast_w("pf", w_pf, wb_pf, l, 16)
            cast_w("o", w_o, wb_o, l, 16)
            cast_w("1", w_1, wb_1, l, 64)
            cast_w("2", w_2, wb_2, l, 64)

        NW = 6
        wring = Ring([sb("wr%d" % i, [P, 2048], BF16) for i in range(NW)], "wr")
        arena["wring"] = wring

        class WPipe:
            def __init__(self, items):
                self.items = items
                self.slots = []
                self.k = 0
                for _ in range(min(NW - 1, len(items))):
                    self._issue()

            def _issue(self):
                i = len(self.slots)
                ap, wres = self.items[i]
                t, r = wring.next()
                x = ap.shape[-1]
                S.op("sp", lambda e: e.dma_start(out=t[:, 0:x], in_=ap), reads=[wres], writes=[r], dma_res=r)
                self.slots.append((t, r))

            def get(self):
                t, r = self.slots[self.k]
                self.k += 1
                if len(self.slots) < len(self.items):
                    self._issue()
                return t, r

        modT = sb("modT", [P, DEPTH, 96, 2])
        tabA = sb("tabA", [P, DEPTH, 2, 2, 16])
        bmt = sb("bmt", [P, DEPTH, 96])
        g12 = sb("g12", [P, DEPTH, 2, 16])
        gft = sb("gft", [P, 16])
        cv = sb("cv", [P, 16, 2])
        cvs = sb("cvs", [P, 16, 2])
        lamt = sb("lamt", [P, DEPTH, 4, 64])
        lamp = sb("lamp", [P, DEPTH, 2, 64])
        lams = sb("lams", [P, DEPTH, 2])
        neglam = sb("neglam", [P, DEPTH])
        sublt = sb("sublt", [P, DEPTH])
        Rmod, Rcv, Rsm, Rlam = Res("mod"), Res("cv"), Res("small"), Res("lam")
        S.op("sp", lambda e: e.dma_start(out=cv[:], in_=cvec[:, :, :]), writes=[Rcv], dma_res=Rcv)
        S.op("act", lambda e: e.activation(out=cvs[:], in_=cv[:], func=AF.Sigmoid), reads=[Rcv], writes=[Rmod])
        S.op("dve", lambda e: e.tensor_tensor(out=cvs[:], in0=cvs[:], in1=cv[:], op=ALU.mult), reads=[Rmod, Rcv], writes=[Rmod])
        for l in range(DEPTH):
            S.op("sp", (lambda l=l: lambda e: e.dma_start(out=bmt[:, l, :], in_=b_modT[l]))(), writes=[Rsm], dma_res=Rsm)
            S.op("sp", (lambda l=l: lambda e: e.dma_start(out=g12[:, l, 0, :], in_=g1T[l]))(), writes=[Rsm], dma_res=Rsm)
            S.op("sp", (lambda l=l: lambda e: e.dma_start(out=g12[:, l, 1, :], in_=g2T[l]))(), writes=[Rsm], dma_res=Rsm)
        S.op("sp", lambda e: e.dma_start(out=gft[:], in_=gfT[:, :]), writes=[Rsm], dma_res=Rsm)

        phase_begin()
        wm_ring = Ring([sb("wm%d" % i, [P, 2048]) for i in range(3)], "wm")
        for l in range(n_layers):
            for j in range(96):
                t, r = wm_ring.next()
                S.op("sp", (lambda t=t, l=l, j=j: lambda e: e.dma_start(out=t[:], in_=w_mod[l, j]))(), writes=[r], dma_res=r)
                pb = j % 2
                for kc in range(16):
                    S.op("pe", (lambda t=t, kc=kc, pb=pb: lambda e: e.matmul(psum[pb][:, 0:2], lhsT=t[:, kc * 128:(kc + 1) * 128], rhs=cvs[:, kc, :], start=(kc == 0), stop=(kc == 15)))(),
                         reads=[r, Rmod], writes=[Rps[pb]])
                S.op("dve", (lambda l=l, j=j, pb=pb: lambda e: e.tensor_tensor(out=modT[:, l, j, :], in0=psum[pb][:, 0:2], in1=bmt[:, l, j:j + 1].to_broadcast([P, 2]), op=ALU.add))(),
                     reads=[Rps[pb], Rsm], writes=[Rmod])
            for w, j0 in ((0, 16), (1, 64)):
                for kind in range(2):
                    S.op("dve", (lambda l=l, w=w, j0=j0, kind=kind: lambda e: e.scalar_tensor_tensor(
                        out=tabA[:, l, w, kind, :], in0=modT[:, l, j0:j0 + 16, kind], scalar=1.0, in1=g12[:, l, w, :],
                        op0=ALU.add, op1=ALU.mult))(), reads=[Rmod, Rsm], writes=[Rmod])
            linit = 0.8 - 0.6 * math.exp(-0.3 * l)
            S.op("sp", (lambda l=l: lambda e: e.dma_start(out=lamt[:, l], in_=lamv[l]))(), writes=[Rlam], dma_res=Rlam)
            S.op("sp", (lambda l=l: lambda e: e.dma_start(out=sublt[:, l:l + 1], in_=sublnT[l]))(), writes=[Rlam], dma_res=Rlam)
            S.op("dve", (lambda l=l: lambda e: e.tensor_tensor(out=lamp[:, l, 0, :], in0=lamt[:, l, 0, :], in1=lamt[:, l, 1, :], op=ALU.mult))(), reads=[Rlam], writes=[Rlam])
            S.op("dve", (lambda l=l: lambda e: e.tensor_tensor(out=lamp[:, l, 1, :], in0=lamt[:, l, 2, :], in1=lamt[:, l, 3, :], op=ALU.mult))(), reads=[Rlam], writes=[Rlam])
            S.op("dve", (lambda l=l: lambda e: e.tensor_reduce(out=lams[:, l, :], in_=lamp[:, l], axis=AX.X, op=ALU.add))(), reads=[Rlam], writes=[Rlam])
            S.op("act", (lambda l=l: lambda e: e.activation(out=lams[:, l, :], in_=lams[:, l, :], func=AF.Exp))(), reads=[Rlam], writes=[Rlam])
            S.op("dve", (lambda l=l, linit=linit: lambda e: e.scalar_tensor_tensor(out=neglam[:, l:l + 1], in0=lams[:, l, 1:2], scalar=-linit, in1=lams[:, l, 0:1], op0=ALU.add, op1=ALU.subtract))(), reads=[Rlam], writes=[Rlam])
            S.op("dve", (lambda l=l, linit=linit: lambda e: e.tensor_scalar(out=sublt[:, l:l + 1], in0=sublt[:, l:l + 1], scalar1=1.0 - linit, scalar2=None, op0=ALU.mult))(), reads=[Rlam], writes=[Rlam])
        phase_end()

        def mod_col(l, j0, kind, kc):
            return modT[:, l, j0 + kc, kind:kind + 1]

        def norm_stats(T, n):
            S.op("act", lambda e: e.activation(out=T.sqb[:, :, 0:n], in_=T.xt[:, :, 0:n], func=AF.Square), reads=[T.Rxt], writes=[T.Rsq])
            for kc in range(16):
                S.op("pe", (lambda kc=kc: lambda e: e.matmul(psum[7][:, 0:n], lhsT=ones_b, rhs=T.sqb[:, kc, 0:n], start=(kc == 0), stop=(kc == 15)))(),
                     reads=[T.Rsq, Rcb], writes=[Rps[7]])
            S.op("act", lambda e: e.activation(out=T.rstd[:, 0:n], in_=psum[7][:, 0:n], func=AF.Sqrt, scale=1.0 / D, bias=eps6), reads=[Rps[7], Reps], writes=[T.Rrstd])
            S.op("dve", lambda e: e.reciprocal(out=T.rstd[:, 0:n], in_=T.rstd[:, 0:n]), reads=[T.Rrstd], writes=[T.Rrstd])

        def norm_mod(T, l, w, kind, n):
            j_sh = 0 if w == 0 else 48
            norm_stats(T, n)
            for kc in range(16):
                tf, rtf = T.tmpf[kc % 2], T.Rtmpf[kc % 2]
                S.op("dve", (lambda kc=kc, tf=tf: lambda e: e.scalar_tensor_tensor(out=tf[:, 0:n], in0=T.xt[:, kc, 0:n], scalar=tabA[:, l, w, kind, kc:kc + 1], in1=T.rstd[:, 0:n], op0=ALU.mult, op1=ALU.mult))(),
                     reads=[T.Rxt, T.Rrstd, Rmod], writes=[rtf])
                S.op("act", (lambda kc=kc, tf=tf: lambda e: e.activation(out=T.hx[:, kc, 0:n], in_=tf[:, 0:n], func=AF.Identity, bias=mod_col(l, j_sh, kind, kc), scale=1.0))(),
                     reads=[rtf, Rmod], writes=[T.Rhx])

        class NS:
            pass

        def common_tiles():
            T = NS()
            T.xt = sb("xt", [P, 16, 512]); T.Rxt = Res("xt")
            T.sqb = sb("sqb", [P, 16, 512], BF16); T.Rsq = Res("sqb")
            T.hx = sb("hx", [P, 16, 512], BF16); T.Rhx = Res("hx")
            T.rstd = sb("rstd", [P, 512]); T.Rrstd = Res("rstd")
            T.tmpf = [sb("tmpf%d" % i, [P, 512]) for i in range(4)]
            T.Rtmpf = [Res("tmpf%d" % i) for i in range(4)]
            T.stg = Ring([sb("stg%d" % i, [P, 512], BF16) for i in range(4)], "stg")
            return T

        Rexout = Res("exout")
        S.ensure_sem(Rexout)

        def do_layer(l):
            last = l == DEPTH - 1
            xsrc = xin if l == 0 else xT1

            if "p1" in phases:
                phase_begin()
                T = common_tiles()
                sgw_f = sb("sgw_f", [P, 1024]); sgw_b = sb("sgw_b", [P, 1024], BF16)
                sgb = sb("sgb", [P, 1024]); sgg = sb("sgg", [P, 1024]); Rsg = Res("sg")
                rope_t = [sb("rope%d" % i, [P, 2, 512]) for i in range(2)]
                Rrope = [Res("rope%d" % i) for i in range(2)]
                sut = sb("sut", [P, 8, 512], BF16); Rsut = Res("sut")
                ft = sb("ft", [P, 8, 512], BF16); Rft = Res("ft")
                svall = [sb("svall%d" % i, [P, 1024]) for i in range(4)]
                Rsvall = [Res("svall%d" % i) for i in range(4)]
                Rsvf = Res("svf")
                vn = sb("vn", [P, 1024], BF16); Rvn = Res("vn")
                ssq = sb("ssq", [P, 2]); junk = sb("junk", [P, 1024])
                zst = [sb("zst%d" % i, [P, 2048], BF16) for i in range(2)]
                Rzst = [Res("zst%d" % i) for i in range(2)]
                S.op("sp", (lambda l=l: lambda e: e.dma_start(out=sgw_f[:], in_=sgu_wT[l]))(), writes=[Rsg], dma_res=Rsg)
                S.op("sp", (lambda l=l: lambda e: e.dma_start(out=sgb[:], in_=sgu_bbc[l]))(), writes=[Rsg], dma_res=Rsg)
                S.op("sp", (lambda l=l: lambda e: e.dma_start(out=sgg[:], in_=sgu_gbc[l]))(), writes=[Rsg], dma_res=Rsg)
                S.op("dve", lambda e: e.tensor_copy(out=sgw_b[:], in_=sgw_f[:]), reads=[Rsg], writes=[Rsg])
                def p1_block(bi, t0, n):
                    kind = 0 if bi < 8 else 1
                    nsub = n // 128
                    S.op("sp", (lambda t0=t0, n=n: lambda e: e.dma_start(out=T.xt[:, :, 0:n], in_=xsrc[:, t0:t0 + n].rearrange("(k p) t -> p k t", p=P)))(),
                         writes=[T.Rxt], dma_res=T.Rxt)
                    rp, Rrp = rope_t[bi % 2], Rrope[bi % 2]
                    S.op("sp", (lambda t0=t0, n=n, rp=rp: lambda e: e.dma_start(out=rp[:, :, 0:n], in_=ropeT[:, :, t0:t0 + n]))(), writes=[Rrp], dma_res=Rrp)
                    norm_mod(T, l, 0, kind, n)
                    need_all = (not last) or kind == 0
                    items = []
                    fm_list = [("qk", c) for c in range(16) if (need_all or c >= 8)]
                    if need_all:
                        fm_list += [("su", c) for c in range(8)] + [("f", c) for c in range(8)] + [("g", c) for c in range(48)]
                    for kindc, c in fm_list:
                        if kindc == "qk":
                            items.append((wb_fm[l, c], Rw[("fm", l)]))
                            items.append((wb_fm[l, 16 + c], Rw[("fm", l)]))
                        else:
                            base = {"su": 32, "f": 40, "g": 48}[kindc]
                            items.append((wb_fm[l, base + c], Rw[("fm", l)]))
                    tm_list = [0, 1] + ([2, 3] if need_all else [])
                    for wh in tm_list:
                        for kq in range(4):
                            items.append((wb_tm[l, wh * 4 + kq], Rw[("tm", l)]))
                    pipe = WPipe(items)
                    pcount = [0]

                    def nextps():
                        k = pcount[0] % 6
                        pcount[0] += 1
                        return psum[k], Rps[k]

                    for kindc, c in fm_list:
                        wt, wr = pipe.get()
                        pa, Rpa = nextps()
                        for kc in range(16):
                            S.op("pe", (lambda wt=wt, kc=kc, pa=pa: lambda e: e.matmul(pa[:, 0:n], lhsT=wt[:, kc * 128:(kc + 1) * 128], rhs=T.hx[:, kc, 0:n], start=(kc == 0), stop=(kc == 15)))(),
                                 reads=[wr, T.Rhx], writes=[Rpa])
                        if kindc == "qk":
                            wt2, wr2 = pipe.get()
                            pb_, Rpb = nextps()
                            for kc in range(16):
                                S.op("pe", (lambda wt2=wt2, kc=kc, pb_=pb_: lambda e: e.matmul(pb_[:, 0:n], lhsT=wt2[:, kc * 128:(kc + 1) * 128], rhs=T.hx[:, kc, 0:n], start=(kc == 0), stop=(kc == 15)))(),
                                     reads=[wr2, T.Rhx], writes=[Rpb])
                            S.op("dve", (lambda pa=pa, rp=rp: lambda e: e.tensor_tensor(out=T.tmpf[2][:, 0:n], in0=pa[:, 0:n], in1=rp[:, 0, 0:n], op=ALU.mult))(), reads=[Rpa, Rrp], writes=[T.Rtmpf[2]])
                            S.op("dve", (lambda pb_=pb_, rp=rp: lambda e: e.tensor_tensor(out=T.tmpf[3][:, 0:n], in0=pb_[:, 0:n], in1=rp[:, 1, 0:n], op=ALU.mult))(), reads=[Rpb, Rrp], writes=[T.Rtmpf[3]])
                            st, rst = T.stg.next()
                            S.op("pool", (lambda st=st: lambda e: e.tensor_tensor(out=st[:, 0:n], in0=T.tmpf[2][:, 0:n], in1=T.tmpf[3][:, 0:n], op=ALU.add))(), reads=[T.Rtmpf[2], T.Rtmpf[3]], writes=[rst])
                            if c < 8:
                                dst = qT[c * 128:(c + 1) * 128, t0:t0 + n]
                            elif kind == 0:
                                dst = exk_in[(c - 8) * 128:(c - 7) * 128, t0:t0 + n]
                            else:
                                dst = exkc_in[(c - 8) * 128:(c - 7) * 128, 0:NC]
                            S.op("sp", (lambda st=st, dst=dst: lambda e: e.dma_start(out=dst, in_=st[:, 0:n]))(), reads=[rst], dma_res=rst, store=True)
                        elif kindc == "su":
                            S.op("act", (lambda pa=pa, c=c: lambda e: e.activation(out=sut[:, c, 0:n], in_=pa[:, 0:n], func=AF.Gelu))(), reads=[Rpa], writes=[Rsut])
                        elif kindc == "f":
                            S.op("dve", (lambda pa=pa, c=c: lambda e: e.tensor_copy(out=ft[:, c, 0:n], in_=pa[:, 0:n]))(), reads=[Rpa], writes=[Rft])
                        else:
                            st, rst = T.stg.next()
                            S.op("act", (lambda pa=pa, st=st: lambda e: e.activation(out=st[:, 0:n], in_=pa[:, 0:n], func=AF.Sigmoid))(), reads=[Rpa], writes=[rst])
                            S.op("sp", (lambda st=st, c=c: lambda e: e.dma_start(out=gatesT[c * 128:(c + 1) * 128, t0:t0 + n], in_=st[:, 0:n]))(), reads=[rst], dma_res=rst, store=True)

                    for wh in tm_list:
                        for kq in range(4):
                            wt, wr = pipe.get()
                            for s in range(nsub):
                                for k4 in range(4):
                                    kc = kq * 4 + k4
                                    S.op("pe", (lambda wt=wt, s=s, k4=k4, kc=kc: lambda e: e.matmul(psum[s][:, :], lhsT=T.hx[:, kc, s * 128:(s + 1) * 128], rhs=wt[:, k4 * 512:(k4 + 1) * 512], start=(kc == 0), stop=(kc == 15)))(),
                                         reads=[wr, T.Rhx], writes=[Rps[s]])
                        col0 = (wh % 2) * 512
                        for s in range(nsub):
                            if wh < 2:
                                st, rst = T.stg.next()
                                S.op("dve", (lambda s=s, st=st: lambda e: e.tensor_copy(out=st[:, 0:512], in_=psum[s][:, :]))(), reads=[Rps[s]], writes=[rst])
                                if kind == 0:
                                    h0 = (wh % 2) * 4
                                    vdst = exv_in[h0:h0 + 4, t0 + s * 128:t0 + (s + 1) * 128, :].rearrange("h t e -> t h e")
                                    S.op("sp", (lambda st=st, vdst=vdst: lambda e: e.dma_start(out=vdst, in_=st[:, 0:512].rearrange("p (h e) -> p h e", e=128)))(), reads=[rst], dma_res=rst, store=True)
                                else:
                                    S.op("sp", (lambda s=s, st=st, col0=col0: lambda e: e.dma_start(out=exvc_in[s * 128:(s + 1) * 128, col0:col0 + 512], in_=st[:, 0:512]))(), reads=[rst], dma_res=rst, store=True)
                            else:
                                S.op("act", (lambda s=s, col0=col0: lambda e: e.activation(out=svall[s][:, col0:col0 + 512], in_=psum[s][:, :], func=AF.Gelu))(), reads=[Rps[s]], writes=[Rsvall[s]])

                    if need_all:
                        for s in range(nsub):
                            zs, Rzs = zst[s % 2], Rzst[s % 2]
                            S.op("dve", (lambda s=s: lambda e: e.tensor_tensor(out=junk[:], in0=svall[s][:], in1=svall[s][:], op=ALU.mult))(), reads=[Rsvall[s]], writes=[Rsvf])
                            S.op("dve", lambda e: e.tensor_reduce(out=ssq[:, 0:1], in_=junk[:], axis=AX.X, op=ALU.add), reads=[Rsvf], writes=[Rsvf])
                            S.op("act", lambda e: e.activation(out=ssq[:, 1:2], in_=ssq[:, 0:1], func=AF.Sqrt, scale=1.0 / 1024, bias=eps6), reads=[Rsvf, Reps], writes=[Rsvf])
                            S.op("dve", lambda e: e.reciprocal(out=ssq[:, 1:2], in_=ssq[:, 1:2]), reads=[Rsvf], writes=[Rsvf])
                            S.op("dve", (lambda s=s: lambda e: e.scalar_tensor_tensor(out=vn[:], in0=svall[s][:], scalar=ssq[:, 1:2], in1=sgg[:], op0=ALU.mult, op1=ALU.mult))(), reads=[Rsvall[s], Rsvf, Rsg], writes=[Rvn])
                            for g in range(8):
                                pg, Rpg = psum[4 + g % 2], Rps[4 + g % 2]
                                S.op("pe", (lambda g=g, pg=pg: lambda e: e.matmul(pg[:, 0:128], lhsT=vn[:, g * 128:(g + 1) * 128], rhs=sgw_b[:, g * 128:(g + 1) * 128], start=True, stop=True))(),
                                     reads=[Rvn, Rsg], writes=[Rpg])
                                S.op("dve", (lambda g=g, pg=pg: lambda e: e.tensor_tensor(out=T.tmpf[2][:, 0:128], in0=pg[:, 0:128], in1=sgb[:, g * 128:(g + 1) * 128], op=ALU.add))(), reads=[Rpg, Rsg], writes=[T.Rtmpf[2]])
                                S.op("dve", (lambda g=g, s=s: lambda e: e.tensor_tensor(out=sut[:, g, s * 128:(s + 1) * 128], in0=T.tmpf[2][:, 0:128], in1=sut[:, g, s * 128:(s + 1) * 128], op=ALU.mult))(), reads=[T.Rtmpf[2], Rsut], writes=[Rsut])
                            for g in range(4):
                                pz, Rpz = psum[6 + g % 2], Rps[6 + g % 2]
                                for k2 in range(2):
                                    S.op("pe", (lambda g=g, k2=k2, pz=pz, s=s: lambda e: e.matmul(pz[:, :], lhsT=ft[:, g * 2 + k2, s * 128:(s + 1) * 128], rhs=Wc[k2], start=(k2 == 0), stop=(k2 == 1)))(),
                                         reads=[Rft, Rcb], writes=[Rpz])
                                S.op("act", (lambda g=g, pz=pz, zs=zs: lambda e: e.activation(out=zs[:, g * 512:(g + 1) * 512], in_=pz[:, :], func=AF.Identity))(), reads=[Rpz], writes=[Rzs])
                            zdst = (exz_in[:, t0 + s * 128:t0 + (s + 1) * 128, :] if kind == 0 else zctx[:, s * 128:(s + 1) * 128, :]).rearrange("q t c -> t q c")
                            S.op("sp", (lambda zs=zs, zdst=zdst: lambda e: e.dma_start(out=zdst, in_=zs[:].rearrange("p (q c) -> p q c", c=128)))(), reads=[Rzs], dma_res=Rzs, store=True)
                        S.op("sp", (lambda t0=t0, n=n: lambda e: e.dma_start(out=sguT[:, t0:t0 + n].rearrange("(g p) t -> p g t", p=P), in_=sut[:, :, 0:n]))(), reads=[Rsut], dma_res=Rsut, store=True)
                for bi, (t0, n) in enumerate(NBLK):
                    p1_block(bi, t0, n)
                phase_end()

            if "ag" in phases:
                pieces = []
                for i in range(16):
                    pieces.append((exk_in[i * 64:(i + 1) * 64, :], exk_out[i]))
                for i in range(8):
                    pieces.append((exv_in[i], exv_out[i]))
                for i in range(16):
                    pieces.append((exz_in[i], exz_out[i]))
                pieces.append((exkc_in, exkc_out))
                pieces.append((exvc_in, exvc_out))
                for (pi, po) in pieces:
                    S.op("pool", (lambda pi=pi, po=po: lambda e: e.collective_compute("AllGather", ALU.bypass, replica_groups=[[0, 1, 2, 3], [4, 5, 6, 7]], ins=[pi.opt()], outs=[po.opt()]))(),
                         writes=[Rexout], dma_res=Rexout, dma_inc=1)


            if "p2" in phases:
                phase_begin()
                KT = sb("KT", [P, 16640], BF16)
                Vt = sb("Vt", [P, 130, 128], BF16)
                QT = sb("QT", [P, TT], BF16)
                Rkv = Res("kv")
                ptr = Ring([sb("pt%d" % i, [P, 1024], BF16) for i in range(5)], "pt")
                accr = Ring([sb("pacc%d" % i, [P, 1024], BF16) for i in range(2)], "pacc")
                LG = 4
                ef = [sb("ef%d" % i, [P, 512]) for i in range(3)]
                Ref = [Res("ef%d" % i) for i in range(3)]
                e3b = sb("e3b", [P, 512], BF16); Re3 = Res("e3b")
                ostg = Ring([sb("ostg%d" % i, [P, 512], BF16) for i in range(2)], "ostg")
                RS = [Res("S0"), Res("S1")]
                RO = [Res("O0"), Res("L0"), Res("O1"), Res("L1")]
                Obank = [psum[4], psum[5], psum[6], psum[7]]
                qblocks = [(i * 512, 512, list(range(130))) for i in range(8)]
                if not last:
                    qblocks.append((NL, NC, [128, 129]))
                def p2_head(h):
                    for m in range(2):
                        S.op("sp", (lambda m=m: lambda e: e.dma_start(out=KT[m * 64:(m + 1) * 64, 0:4 * NL].rearrange("p (r t) -> p r t", r=4), in_=exk_out[m * 8 + h].rearrange("r f t -> f r t")))(),
                             reads=[Rexout], writes=[Rkv], dma_res=Rkv)
                        S.op("sp", (lambda m=m: lambda e: e.dma_start(out=KT[m * 64:(m + 1) * 64, 16384:16640], in_=exkc_out[0, m * 512 + h * 64:m * 512 + (h + 1) * 64, :]))(),
                             reads=[Rexout], writes=[Rkv], dma_res=Rkv)
                        S.op("sp", (lambda m=m: lambda e: e.dma_start(out=QT[m * 64:(m + 1) * 64, :], in_=qT[m * 512 + h * 64:m * 512 + (h + 1) * 64, :]))(),
                             writes=[Rkv], dma_res=Rkv)
                    S.op("sp", lambda e: e.dma_start(out=Vt[:, 0:128, :], in_=exv_out[h].rearrange("(b p) c -> p b c", p=P)), reads=[Rexout], writes=[Rkv], dma_res=Rkv)
                    S.op("sp", lambda e: e.dma_start(out=Vt[:, 128:130, :], in_=exvc_out[0, :, h * 128:(h + 1) * 128].rearrange("(b p) c -> p b c", p=P)), reads=[Rexout], writes=[Rkv], dma_res=Rkv)

                    def p2_qblock(q0, nq, kbs):
                        def qk(i):
                            kb = kbs[i]
                            Sx = ps2[i % 2]
                            for m in range(2):
                                S.op("pe", (lambda m=m, kb=kb, Sx=Sx: lambda e: e.matmul(Sx[:, m * 512:m * 512 + nq], lhsT=KT[m * 64:(m + 1) * 64, kb * 128:(kb + 1) * 128], rhs=QT[m * 64:(m + 1) * 64, q0:q0 + nq], start=True, stop=True))(),
                                     reads=[Rkv], writes=[RS[i % 2]])
                        qk(0)
                        nk = len(kbs)
                        grp = {}
                        for i in range(nk):
                            kb = kbs[i]
                            if i + 1 < nk:
                                qk(i + 1)
                            Sx = ps2[i % 2]
                            pt, Rpt = ptr.next()
                            S.op("act", (lambda Sx=Sx, pt=pt: lambda e: e.activation(out=pt[:].rearrange("p (m q) -> p m q", m=2)[:, :, 0:nq], in_=Sx[:].rearrange("p (m q) -> p m q", m=2)[:, :, 0:nq], func=AF.Exp, scale=0.125))(),
                                 reads=[RS[i % 2]], writes=[Rpt])
                            for m in range(2):
                                S.op("pe", (lambda m=m, kb=kb, pt=pt, i=i: lambda e: e.matmul(Obank[2 * m][:, 0:nq], lhsT=Vt[:, kb, :], rhs=pt[:, m * 512:m * 512 + nq], start=(i == 0), stop=(i == nk - 1)))(),
                                     reads=[Rkv, Rpt], writes=[RO[2 * m]])
                            gi = i % LG
                            if gi == 0:
                                grp["first"] = (pt, Rpt)
                                grp["src"] = (pt, Rpt)
                            elif gi == 1:
                                acc, Racc = accr.next()
                                fp_, Rfp = grp["first"]
                                S.op("dve", (lambda acc=acc, fp_=fp_, pt=pt: lambda e: e.tensor_tensor(out=acc[:], in0=fp_[:], in1=pt[:], op=ALU.add))(), reads=[Rfp, Rpt], writes=[Racc])
                                grp["src"] = (acc, Racc)
                            else:
                                acc, Racc = grp["src"]
                                S.op("dve", (lambda acc=acc, pt=pt: lambda e: e.tensor_tensor(out=acc[:], in0=acc[:], in1=pt[:], op=ALU.add))(), reads=[Racc, Rpt], writes=[Racc])
                            if gi == LG - 1 or i == nk - 1:
                                src, Rsrc = grp["src"]
                                first_g = (i // LG == 0)
                                for m in range(2):
                                    S.op("pe", (lambda m=m, src=src, first_g=first_g, i=i: lambda e: e.matmul(Obank[2 * m + 1][:, 0:nq], lhsT=ones_b, rhs=src[:, m * 512:m * 512 + nq], start=first_g, stop=(i == nk - 1)))(),
                                         reads=[Rcb, Rsrc], writes=[RO[2 * m + 1]])
                        S.op("dve", lambda e: e.reciprocal(out=ef[0][:, 0:nq], in_=Obank[1][:, 0:nq]), reads=[RO[1]], writes=[Ref[0]])
                        S.op("dve", lambda e: e.reciprocal(out=ef[1][:, 0:nq], in_=Obank[3][:, 0:nq]), reads=[RO[3]], writes=[Ref[1]])
                        S.op("dve", lambda e: e.tensor_tensor(out=ef[0][:, 0:nq], in0=Obank[0][:, 0:nq], in1=ef[0][:, 0:nq], op=ALU.mult), reads=[RO[0], Ref[0]], writes=[Ref[0]])
                        S.op("dve", lambda e: e.tensor_tensor(out=ef[1][:, 0:nq], in0=Obank[2][:, 0:nq], in1=ef[1][:, 0:nq], op=ALU.mult), reads=[RO[2], Ref[1]], writes=[Ref[1]])
                        S.op("dve", lambda e: e.scalar_tensor_tensor(out=ef[2][:, 0:nq], in0=ef[1][:, 0:nq], scalar=neglam[:, l:l + 1], in1=ef[0][:, 0:nq], op0=ALU.mult, op1=ALU.add), reads=[Ref[0], Ref[1], Rlam], writes=[Ref[2]])
                        S.op("act", lambda e: e.activation(out=e3b[:, 0:nq], in_=ef[2][:, 0:nq], func=AF.Square), reads=[Ref[2]], writes=[Re3])
                        S.op("pe", lambda e: e.matmul(ps2[0][:, 0:nq], lhsT=ones_b, rhs=e3b[:, 0:nq], start=True, stop=True), reads=[Re3, Rcb], writes=[RS[0]])
                        S.op("act", lambda e: e.activation(out=ef[0][:, 0:nq], in_=ps2[0][:, 0:nq], func=AF.Sqrt, scale=1.0 / 128, bias=eps5), reads=[RS[0], Reps], writes=[Ref[0]])
                        S.op("dve", lambda e: e.reciprocal(out=ef[0][:, 0:nq], in_=ef[0][:, 0:nq]), reads=[Ref[0]], writes=[Ref[0]])
                        st, rst = ostg.next()
                        S.op("dve", (lambda st=st: lambda e: e.scalar_tensor_tensor(out=st[:, 0:nq], in0=ef[2][:, 0:nq], scalar=sublt[:, l:l + 1], in1=ef[0][:, 0:nq], op0=ALU.mult, op1=ALU.mult))(), reads=[Ref[2], Ref[0], Rlam], writes=[rst])
                        S.op("sp", (lambda st=st: lambda e: e.dma_start(out=attT[h * 128:(h + 1) * 128, q0:q0 + nq], in_=st[:, 0:nq]))(), reads=[rst], dma_res=rst, store=True)
                    for (q0, nq, kbs) in qblocks:
                        p2_qblock(q0, nq, kbs)
                for h in range(8):
                    p2_head(h)
                phase_end()

            if "four" in phases:
                phase_begin()
                Zt = sb("Zt", [P, 128, 128], BF16); RZt = Res("Zt")
                Bt = sb("Bt", [P, 2, 64, 128], BF16); RBt = Res("Bt")
                Yt = sb("Yt", [P, 128, 256], BF16); RYt = Res("Yt")
                tw = [sb("tw%d" % i, [P, 512]) for i in range(4)]
                Rtw = [Res("tw%d" % i) for i in range(4)]
                Tc4 = Tc.unsqueeze(1).to_broadcast([P, 4, 128])
                Ts4 = Ts.unsqueeze(1).to_broadcast([P, 4, 128])
                for g in range(4):
                    for cq in range(4):
                        q = g * 4 + cq
                        S.op("sp", (lambda q=q: lambda e: e.dma_start(out=Zt[:, :, :], in_=exz_out[q].rearrange("(a n) c -> a n c", n=128)))(),
                             reads=[Rexout], writes=[RZt], dma_res=RZt)
                        for cp in range(32):
                            pz, Rpz = psum[cp % 2], Rps[cp % 2]
                            for ci in range(2):
                                c = 2 * cp + ci
                                S.op("pe", (lambda c=c, ci=ci, pz=pz: lambda e: e.matmul(pz[:, ci * 256:(ci + 1) * 256], lhsT=Zt[:, :, c], rhs=CS1, start=True, stop=False))(), reads=[RZt, Rcb], writes=[Rpz])
                                S.op("pe", (lambda c=c, ci=ci, pz=pz: lambda e: e.matmul(pz[:, ci * 256:(ci + 1) * 256], lhsT=Zt[:, :, 64 + c], rhs=CS2, start=False, stop=True))(), reads=[RZt, Rcb], writes=[Rpz])
                            k = (cp % 2) * 2
                            t1, t2 = tw[k], tw[k + 1]
                            S.op("dve", (lambda pz=pz, t1=t1: lambda e: e.tensor_tensor(out=t1[:].rearrange("p (a k) -> p a k", k=128), in0=pz.rearrange("p (a k) -> p a k", k=128), in1=Tc4, op=ALU.mult))(), reads=[Rpz, Rcst], writes=[Rtw[k]])
                            S.op("dve", (lambda pz=pz, t2=t2: lambda e: e.tensor_tensor(out=t2[:].rearrange("p (a k) -> p a k", k=128), in0=pz.rearrange("p (a k) -> p a k", k=128), in1=Ts4, op=ALU.mult))(), reads=[Rpz, Rcst], writes=[Rtw[k + 1]])
                            t1v = t1[:].rearrange("p (c r k) -> p c r k", c=2, r=2)
                            t2v = t2[:].rearrange("p (c r k) -> p c r k", c=2, r=2)
                            S.op("pool", (lambda cp=cp, t1v=t1v, t2v=t2v: lambda e: e.tensor_tensor(out=Bt[:, 0, 2 * cp:2 * cp + 2, :], in0=t1v[:, :, 0, :], in1=t2v[:, :, 1, :], op=ALU.add))(), reads=[Rtw[k], Rtw[k + 1]], writes=[RBt])
                            S.op("pool", (lambda cp=cp, t1v=t1v, t2v=t2v: lambda e: e.tensor_tensor(out=Bt[:, 1, 2 * cp:2 * cp + 2, :], in0=t1v[:, :, 1, :], in1=t2v[:, :, 0, :], op=ALU.subtract))(), reads=[Rtw[k], Rtw[k + 1]], writes=[RBt])
                        for cg in range(16):
                            py, Rpy = psum[2 + cg % 2], Rps[2 + cg % 2]
                            S.op("pe", (lambda cg=cg, py=py: lambda e: e.matmul(py[0:32, :], lhsT=cownb[:, 0:32], rhs=Bt[:, 0, 4 * cg:4 * cg + 4, :], start=True, stop=False))(), reads=[RBt, Rcb], writes=[Rpy])
                            S.op("pe", (lambda cg=cg, py=py: lambda e: e.matmul(py[0:32, :], lhsT=cownb[:, 32:64], rhs=Bt[:, 1, 4 * cg:4 * cg + 4, :], start=False, stop=True))(), reads=[RBt, Rcb], writes=[Rpy])
                            ch0 = cq * 64 + 4 * cg
                            S.op("act", (lambda py=py, ch0=ch0: lambda e: e.activation(out=Yt[0:32, :, ch0:ch0 + 4].rearrange("p k c -> p c k"), in_=py[0:32, :].rearrange("p (c k) -> p c k", k=128), func=AF.Identity, scale=1.0 / 2048))(), reads=[Rpy], writes=[RYt])
                    S.op("sp", (lambda g=g: lambda e: e.dma_start(out=fourTM[0:NL, g * 256:(g + 1) * 256].rearrange("(a k) c -> a k c", k=128), in_=Yt[0:32, :, :]))(), reads=[RYt], dma_res=RYt, store=True)
                if not last:
                    Zc = sb("Zc", [P, 2, 16, 128], BF16); RZc = Res("Zc")
                    Yc = sb("Yc", [P, 2, 1024], BF16); RYc = Res("Yc")
                    for nch in range(2):
                        S.op("sp", (lambda nch=nch: lambda e: e.dma_start(out=Zc[:, nch, :, :], in_=zctx[:, nch * 128:(nch + 1) * 128, :].rearrange("q t c -> t q c")))(), writes=[RZc], dma_res=RZc)
                    for kch in range(2):
                        for g in range(4):
                            pc, Rpc = psum[4 + g % 2], Rps[4 + g % 2]
                            cnt = 0
                            for nch in range(2):
                                for ri in range(2):
                                    S.op("pe", (lambda nch=nch, ri=ri, g=g, kch=kch, pc=pc, cnt=cnt: lambda e: e.matmul(pc[:, 0:256], lhsT=Cx[nch][ri][:, kch * 128:(kch + 1) * 128], rhs=Zc[:, nch, 4 * g:4 * g + 4, ri * 64:(ri + 1) * 64], start=(cnt == 0), stop=(cnt == 3)))(),
                                         reads=[RZc, Rcb], writes=[Rpc])
                                    cnt += 1
                            S.op("act", (lambda pc=pc, kch=kch, g=g: lambda e: e.activation(out=Yc[:, kch, g * 256:(g + 1) * 256], in_=pc[:, 0:256], func=AF.Identity, scale=1.0 / 256))(), reads=[Rpc], writes=[RYc])
                    S.op("sp", lambda e: e.dma_start(out=fourTM[NL:TT, :].rearrange("(k p) c -> p k c", p=P), in_=Yc[:]), reads=[RYc], dma_res=RYc, store=True)
                phase_end()

            if "p3" in phases:
                phase_begin()
                T = common_tiles()
                hid = sb("hid", [P, 32, 512], BF16); Rhid = Res("hid")
                att = sb("att", [P, 8, 512], BF16); sgu = sb("sgu", [P, 8, 512], BF16)
                fourT = sb("fourT", [P, 8, 512], BF16); RfT = Res("fourT")
                fstg = sb("fstg", [P, 4, 1024], BF16)
                Rin = Res("p3in")
                Rfs = Res("fstg")
                gring = Ring([sb("gat%d" % i, [P, 3, 512], BF16) for i in range(2)], "gat")
                gsrc = gatesT.rearrange("(w f) t -> f w t", w=3)
                blocks = NBLK if not last else NBLK[:8]
                def p3_block(bi, t0, n):
                    kind = 0 if bi < 8 else 1
                    nsub = n // 128
                    S.op("sp", (lambda t0=t0, n=n: lambda e: e.dma_start(out=T.xt[:, :, 0:n], in_=xsrc[:, t0:t0 + n].rearrange("(k p) t -> p k t", p=P)))(), writes=[T.Rxt], dma_res=T.Rxt)
                    S.op("sp", (lambda t0=t0, n=n: lambda e: e.dma_start(out=att[:, :, 0:n], in_=attT[:, t0:t0 + n].rearrange("(k p) t -> p k t", p=P)))(), writes=[Rin], dma_res=Rin)
                    S.op("sp", (lambda t0=t0, n=n: lambda e: e.dma_start(out=sgu[:, :, 0:n], in_=sguT[:, t0:t0 + n].rearrange("(k p) t -> p k t", p=P)))(), writes=[Rin], dma_res=Rin)
                    S.op("sp", (lambda t0=t0, n=n, nsub=nsub: lambda e: e.dma_start(out=fstg[:, 0:nsub, :], in_=fourTM[t0:t0 + n, :].rearrange("(s p) c -> p s c", p=P)))(), writes=[Rfs], dma_res=Rfs)
                    for s in range(nsub):
                        pT, RpT = psum[6 + s % 2], Rps[6 + s % 2]
                        pTb = pT.bitcast(BF16)
                        for c in range(8):
                            S.op("pe", (lambda s=s, c=c, pTb=pTb: lambda e: e.transpose(pTb[:, c * 128:(c + 1) * 128], fstg[:, s, c * 128:(c + 1) * 128], ident_b))(), reads=[Rfs, Rcb], writes=[RpT])
                        S.op("dve", (lambda s=s, pTb=pTb: lambda e: e.tensor_copy(out=fourT[:, :, s * 128:(s + 1) * 128], in_=pTb[:, 0:1024].rearrange("p (c k) -> p c k", k=128)))(), reads=[RpT], writes=[RfT])
                    items = []
                    for j in range(16):
                        items += [(wb_pa[l, j], Rw[("pa", l)]), (wb_ps[l, j], Rw[("ps", l)]), (wb_pf[l, j], Rw[("pf", l)])]
                    items += [(wb_o[l, j], Rw[("o", l)]) for j in range(16)]
                    for half in range(2):
                        items += [(wb_1[l, half * 32 + m], Rw[("1", l)]) for m in range(32)]
                        for j in range(16):
                            items += [(wb_2[l, j * 4 + half * 2 + pp], Rw[("2", l)]) for pp in range(2)]
                    pipe = WPipe(items)
                    for j in range(16):
                        gt_, Rgt = gring.next()
                        S.op("sp", (lambda j=j, gt_=gt_: lambda e: e.dma_start(out=gt_[:, :, 0:n], in_=gsrc[j * 128:(j + 1) * 128, :, t0:t0 + n]))(), writes=[Rgt], dma_res=Rgt)
                        srcs = [(att, Rin), (sgu, Rin), (fourT, RfT)]
                        for bidx in range(3):
                            wt, wr = pipe.get()
                            src, Rsrc = srcs[bidx]
                            for kc in range(8):
                                S.op("pe", (lambda wt=wt, kc=kc, bidx=bidx, src=src: lambda e: e.matmul(psum[bidx][:, 0:n], lhsT=wt[:, kc * 128:(kc + 1) * 128], rhs=src[:, kc, 0:n], start=(kc == 0), stop=(kc == 7)))(),
                                     reads=[wr, Rsrc], writes=[Rps[bidx]])
                        S.op("dve", (lambda gt_=gt_: lambda e: e.tensor_tensor(out=T.tmpf[0][:, 0:n], in0=psum[0][:, 0:n], in1=gt_[:, 0, 0:n], op=ALU.mult))(), reads=[Rps[0], Rgt], writes=[T.Rtmpf[0]])
                        S.op("dve", (lambda gt_=gt_: lambda e: e.tensor_tensor(out=T.tmpf[1][:, 0:n], in0=psum[1][:, 0:n], in1=gt_[:, 1, 0:n], op=ALU.mult))(), reads=[Rps[1], Rgt], writes=[T.Rtmpf[1]])
                        S.op("dve", (lambda gt_=gt_: lambda e: e.tensor_tensor(out=T.tmpf[2][:, 0:n], in0=psum[2][:, 0:n], in1=gt_[:, 2, 0:n], op=ALU.mult))(), reads=[Rps[2], Rgt], writes=[T.Rtmpf[2]])
                        S.op("pool", lambda e: e.tensor_tensor(out=T.tmpf[0][:, 0:n], in0=T.tmpf[0][:, 0:n], in1=T.tmpf[1][:, 0:n], op=ALU.add), reads=[T.Rtmpf[0], T.Rtmpf[1]], writes=[T.Rtmpf[0]])
                        S.op("pool", (lambda j=j: lambda e: e.tensor_tensor(out=T.sqb[:, j, 0:n], in0=T.tmpf[0][:, 0:n], in1=T.tmpf[2][:, 0:n], op=ALU.add))(), reads=[T.Rtmpf[0], T.Rtmpf[2]], writes=[T.Rsq])
                    for j in range(16):
                        wt, wr = pipe.get()
                        pa, Rpa = psum[3 + j % 2], Rps[3 + j % 2]
                        for kc in range(16):
                            S.op("pe", (lambda wt=wt, kc=kc, pa=pa: lambda e: e.matmul(pa[:, 0:n], lhsT=wt[:, kc * 128:(kc + 1) * 128], rhs=T.sqb[:, kc, 0:n], start=(kc == 0), stop=(kc == 15)))(),
                                 reads=[wr, T.Rsq], writes=[Rpa])
                        S.op("dve", (lambda j=j, pa=pa: lambda e: e.scalar_tensor_tensor(out=T.xt[:, j, 0:n], in0=pa[:, 0:n], scalar=mod_col(l, 32, kind, j), in1=T.xt[:, j, 0:n], op0=ALU.mult, op1=ALU.add))(),
                             reads=[Rpa, T.Rxt, Rmod], writes=[T.Rxt])
                    norm_mod(T, l, 1, kind, n)
                    for half in range(2):
                        for m in range(32):
                            wt, wr = pipe.get()
                            pa, Rpa = psum[m % 4], Rps[m % 4]
                            for kc in range(16):
                                S.op("pe", (lambda wt=wt, kc=kc, pa=pa: lambda e: e.matmul(pa[:, 0:n], lhsT=wt[:, kc * 128:(kc + 1) * 128], rhs=T.hx[:, kc, 0:n], start=(kc == 0), stop=(kc == 15)))(),
                                     reads=[wr, T.Rhx], writes=[Rpa])
                            tf, rtf = T.tmpf[m % 4], T.Rtmpf[m % 4]
                            S.op("act", (lambda pa=pa, tf=tf: lambda e: e.activation(out=tf[:, 0:n], in_=pa[:, 0:n], func=AF.Relu))(), reads=[Rpa], writes=[rtf])
                            S.op("pool", (lambda m=m, tf=tf: lambda e: e.tensor_tensor(out=hid[:, m, 0:n], in0=tf[:, 0:n], in1=tf[:, 0:n], op=ALU.mult))(), reads=[rtf], writes=[Rhid])
                        for j in range(16):
                            pa, Rpa = psum[4 + j % 2], Rps[4 + j % 2]
                            for pp in range(2):
                                wt, wr = pipe.get()
                                for kc in range(16):
                                    S.op("pe", (lambda wt=wt, kc=kc, pp=pp, pa=pa: lambda e: e.matmul(pa[:, 0:n], lhsT=wt[:, kc * 128:(kc + 1) * 128], rhs=hid[:, pp * 16 + kc, 0:n], start=(pp == 0 and kc == 0), stop=(pp == 1 and kc == 15)))(),
                                         reads=[wr, Rhid], writes=[Rpa])
                            S.op("dve", (lambda j=j, pa=pa: lambda e: e.scalar_tensor_tensor(out=T.xt[:, j, 0:n], in0=pa[:, 0:n], scalar=mod_col(l, 80, kind, j), in1=T.xt[:, j, 0:n], op0=ALU.mult, op1=ALU.add))(),
                                 reads=[Rpa, T.Rxt, Rmod], writes=[T.Rxt])
                    if not last:
                        S.op("sp", (lambda t0=t0, n=n: lambda e: e.dma_start(out=xT1[:, t0:t0 + n].rearrange("(k p) t -> p k t", p=P), in_=T.xt[:, :, 0:n]))(), reads=[T.Rxt], dma_res=T.Rxt, store=True)
                    else:
                        norm_stats(T, n)
                        for kc in range(16):
                            S.op("dve", (lambda kc=kc: lambda e: e.scalar_tensor_tensor(out=T.xt[:, kc, 0:n], in0=T.xt[:, kc, 0:n], scalar=gft[:, kc:kc + 1], in1=T.rstd[:, 0:n], op0=ALU.mult, op1=ALU.mult))(),
                                 reads=[T.Rxt, T.Rrstd, Rsm], writes=[T.Rxt])
                        S.op("sp", (lambda t0=t0, n=n: lambda e: e.dma_start(out=outT[:, t0:t0 + n].rearrange("(k p) t -> p k t", p=P), in_=T.xt[:, :, 0:n]))(), reads=[T.Rxt], dma_res=T.Rxt, store=True)
                for bi, (t0, n) in enumerate(blocks):
                    p3_block(bi, t0, n)
                phase_end()

        for l in range(n_layers):
            do_layer(l)

        if dump:
            srcmap = {"qT": qT, "gatesT": gatesT, "sguT": sguT, "attT": attT, "fourTM": fourTM, "xT1": xT1,
                      "modT": None}
            Rd = Res("dump")
            for nm, shape, dt in dump:
                if nm == "modT":
                    S.op("sp", lambda e: e.dma_start(out=dumps["modT"], in_=modT[:].rearrange("p l j k -> p (l j k)")), reads=[Rmod], dma_res=Rd, store=True)
                else:
                    src = srcmap[nm]
                    S.op("sp", (lambda nm=nm, src=src: lambda e: e.dma_start(out=dumps[nm], in_=src))(), dma_res=Rd, store=True)
        S.full_barrier()
        block = es.enter_context(nc.Block())
        S.emit(block)
    return nc


def _tile_fm(W):
    K, M = W.shape
    t = W.reshape(K // 128, 128, M // 128, 128).transpose(2, 1, 0, 3)
    return np.ascontiguousarray(t).reshape(M // 128, 128, (K // 128) * 128)


def _tile_tm(W):
    K, M = W.shape
    nb = M // 512
    t = W.reshape(4, 4, 128, nb, 512).transpose(3, 0, 2, 1, 4)
    return np.ascontiguousarray(t).reshape(nb * 4, 128, 2048)


def _constants():
    f32 = np.float32
    A = np.zeros((128, 2944), f32)
    A[:, 0:128] = 1.0
    A[:, 128:256] = np.eye(128, dtype=f32)
    n = np.arange(128)
    ang = 2 * np.pi * np.outer(n, n) / 128.0
    C, Sn = np.cos(ang), np.sin(ang)
    A[:, 256:384], A[:, 384:512] = C, -Sn
    A[:, 512:640], A[:, 640:768] = Sn, C
    ch = np.arange(256)
    a2 = 2 * np.pi * np.outer(ch, ch) / 256.0
    Cc, Sc = np.cos(a2), np.sin(a2)
    Wp = np.zeros((256, 512))
    for cq in range(4):
        Wp[:, cq * 128:cq * 128 + 64] = Cc[:, cq * 64:(cq + 1) * 64]
        Wp[:, cq * 128 + 64:cq * 128 + 128] = -Sc[:, cq * 64:(cq + 1) * 64]
    A[:, 768:1280] = Wp[0:128]
    A[:, 1280:1792] = Wp[128:256]
    for nch in range(2):
        A[:, 1792 + nch * 512:1792 + nch * 512 + 256] = Cc[nch * 128:(nch + 1) * 128]
        A[:, 1792 + nch * 512 + 256:1792 + (nch + 1) * 512] = Sc[nch * 128:(nch + 1) * 128]
    Bc = np.zeros((8, 128, 320), f32)
    at = 2 * np.pi * np.outer(n, n) / 16384.0
    Bc[:, :, 0:128] = np.cos(at)
    Bc[:, :, 128:256] = np.sin(at)
    for core in range(8):
        rk = core % 4
        Bc[core, :, 256:288] = C[:, 32 * rk:32 * rk + 32]
        Bc[core, :, 288:320] = Sn[:, 32 * rk:32 * rk + 32]
    return A, Bc


def _rope_tables():
    f32 = np.float32
    n_freq = 16
    inv = (np.float32(10000.0) ** (-np.arange(n_freq, dtype=f32) / f32(n_freq))).astype(f32)
    tok = np.arange(16384)
    row = (tok // 64).astype(f32)
    col = (tok % 64).astype(f32)
    ar = row[:, None] * inv
    ac = col[:, None] * inv
    ang = np.concatenate([ar, ar, ac, ac], axis=-1).astype(f32)
    cos, sin = np.cos(ang).astype(f32), np.sin(ang).astype(f32)
    d = np.arange(64)
    sign = np.where((d % 32) < 16, -1.0, 1.0).astype(f32)
    sin_s = sin * sign[None, :]
    return cos.T.copy(), sin_s.T.copy()


_ROT_PERM = np.array([(d + 16) if (d % 32) < 16 else (d - 16) for d in range(64)])


def _prep_inputs(inp):
    f32 = np.float32
    g = {k: np.asarray(v, dtype=f32) for k, v in inp.items()}
    A, Bc = _constants()
    cosT, sinT = _rope_tables()
    shared = {}
    shared["w_mod"] = np.stack([_tile_fm(g["w_mod"][l]) for l in range(DEPTH)])
    shared["b_modT"] = np.ascontiguousarray(g["b_mod"].reshape(DEPTH, 96, 128).transpose(0, 2, 1))
    shared["g1T"] = np.ascontiguousarray(g["norm1_g"].reshape(DEPTH, 16, 128).transpose(0, 2, 1))
    shared["g2T"] = np.ascontiguousarray(g["norm2_g"].reshape(DEPTH, 16, 128).transpose(0, 2, 1))
    shared["gfT"] = np.ascontiguousarray(g["final_g"].reshape(16, 128).T)
    w_fm, w_tm = [], []
    perm_cols = np.concatenate([hh * 64 + _ROT_PERM for hh in range(32)])
    for l in range(DEPTH):
        W = g["w_in"][l]
        qk = W[:, 0:2048]
        fm_cols = np.concatenate([qk, qk[:, perm_cols], W[:, 3072:4096], W[:, 5120:6144], W[:, 6144:12288]], axis=1)
        w_fm.append(_tile_fm(fm_cols))
        tm_cols = np.concatenate([W[:, 2048:3072], W[:, 4096:5120]], axis=1)
        w_tm.append(_tile_tm(tm_cols))
    shared["w_fm"] = np.stack(w_fm)
    shared["w_tm"] = np.stack(w_tm)
    lam = np.stack([g["lambda_q1"], g["lambda_k1"], g["lambda_q2"], g["lambda_k2"]], axis=1)
    shared["lamv"] = np.ascontiguousarray(np.broadcast_to(lam[:, None], (DEPTH, 128, 4, 64)))
    shared["sublnT"] = np.ascontiguousarray(g["subln_g"].reshape(DEPTH, 128, 1))
    shared["sgu_gbc"] = np.ascontiguousarray(np.broadcast_to(g["sgu_norm_g"][:, None, :], (DEPTH, 128, 1024)))
    shared["sgu_wT"] = np.ascontiguousarray(g["sgu_w"].transpose(0, 3, 1, 2)).reshape(DEPTH, 128, 1024)
    shared["sgu_bbc"] = np.ascontiguousarray(np.broadcast_to(g["sgu_b"].reshape(DEPTH, 1, 1024), (DEPTH, 128, 1024)))
    shared["w_pa"] = np.stack([_tile_fm(g["w_proj_att"][l]) for l in range(DEPTH)])
    shared["w_ps"] = np.stack([_tile_fm(g["w_proj_sgu"][l]) for l in range(DEPTH)])
    shared["w_pf"] = np.stack([_tile_fm(g["w_proj_fourier"][l]) for l in range(DEPTH)])
    shared["w_o"] = np.stack([_tile_fm(g["w_out"][l]) for l in range(DEPTH)])
    shared["w_1"] = np.stack([_tile_fm(g["w_mlp_in"][l]) for l in range(DEPTH)])
    w2 = []
    for l in range(DEPTH):
        t = _tile_fm(g["w_mlp_out"][l])
        t = t.reshape(16, 128, 4, 2048).transpose(0, 2, 1, 3)
        w2.append(np.ascontiguousarray(t).reshape(64, 128, 2048))
    shared["w_2"] = np.stack(w2)
    shared["constsA"] = A
    in_maps = []
    for core in range(8):
        b, rk = core // 4, core % 4
        m = dict(shared)
        xin = np.zeros((D, TT), f32)
        xin[:, 0:NL] = g["x"][b, rk * NL:(rk + 1) * NL, :].T
        if rk == 0:
            xin[:, NL:TT] = g["ctx"][b].T
        m["xin"] = xin
        cv = np.stack([g["c"][b], g["c_ctx"]], axis=-1)
        m["cvec"] = np.ascontiguousarray(cv.reshape(16, 128, 2).transpose(1, 0, 2))
        rope = np.zeros((128, 2, TT), f32)
        rope[:, 0, NL:TT] = 1.0
        cs = cosT[:, rk * NL:(rk + 1) * NL]
        sn = sinT[:, rk * NL:(rk + 1) * NL]
        rope[0:64, 0, 0:NL] = cs
        rope[64:128, 0, 0:NL] = cs
        rope[0:64, 1, 0:NL] = sn
        rope[64:128, 1, 0:NL] = sn
        m["ropeT"] = rope
        m["constsB"] = Bc[core]
        in_maps.append(m)
    return in_maps


_NC_CACHE = {}


def kernel(**inputs):
    in_maps = _prep_inputs(inputs)
    if "nc" not in _NC_CACHE:
        _NC_CACHE["nc"] = build_program()
    nc = _NC_CACHE["nc"]
    res = run_bass_kernel_spmd(nc, in_maps, core_ids=list(range(8)))
    out = np.empty((2, 16384, D), np.float32)
    for core in range(8):
        b, rk = core // 4, core % 4
        out[b, rk * NL:(rk + 1) * NL, :] = res.results[core]["outT"].T
    return out
```

```python
import contextlib
import math
import numpy as np
import concourse.bass as bass
import concourse.mybir as mybir
from concourse.bass_utils import run_bass_kernel_spmd

F32 = mybir.dt.float32
BF16 = mybir.dt.bfloat16
AF = mybir.ActivationFunctionType
ALU = mybir.AluOpType
AX = mybir.AxisListType

D = 2048
NL = 4096
NC = 256
TT = NL + NC
NBLK = [(i * 512, 512) for i in range(8)] + [(NL, NC)]
DEPTH = 2
EXN = TT * 4096
KT_OFF, V_OFF, Z_OFF = 0, 1024 * TT, 2048 * TT
ENGS = ("pe", "act", "dve", "pool", "sp")


class Res:
    __slots__ = ("name", "last_w", "readers", "sem", "cnt", "inh")

    def __init__(self, name):
        self.name = name
        self.last_w = None
        self.readers = []
        self.sem = None
        self.cnt = 0
        self.inh = []


class Sched:
    def __init__(self, nc, sem_alloc):
        self.nc = nc
        self.sem_alloc = sem_alloc
        self.streams = {e: [] for e in ENGS}
        self.count = {e: 0 for e in ENGS}
        self.esem = {e: sem_alloc("c_" + e) for e in ENGS if e != "sp"}
        self.waited = {e: {} for e in ENGS}
        self.pending = []
        self.nsem = 4
        self.last = {}
        self.free = []
        self.phase_stack = []

    def _need(self, eng, tok, raw):
        sem, val, teng = tok[0], tok[1], tok[2]
        if teng == eng and (eng in ("pe", "sp") or not raw):
            return False
        key = id(sem)
        if self.waited[eng].get(key, 0) >= val:
            return False
        self.waited[eng][key] = val
        return True

    def op(self, eng, fn, reads=(), writes=(), dma_res=None, extra=(), store=False, dma_inc=16):
        is_dma = dma_res is not None
        deps = [(t, False) for t in extra]
        follow = set()
        for r in reads:
            if r.last_w is not None:
                deps.append((r.last_w, True))
        for w in writes:
            if is_dma and w is dma_res and w.last_w is not None and w.last_w[3] and not w.readers:
                deps.extend(w.inh)
                follow.add(id(w))
            else:
                if w.last_w is not None:
                    deps.append((w.last_w, False))
                deps.extend((t, False) for t in w.readers)
        waits = []
        for t, raw in deps:
            if self._need(eng, t, raw):
                waits.append((t[0], t[1]))
        if is_dma:
            if dma_res.sem is None:
                if self.free:
                    dma_res.sem, dma_res.cnt = self.free.pop()
                else:
                    dma_res.sem, dma_res.cnt = self.sem_alloc("d%d_%s" % (self.nsem, dma_res.name)), 0
                    self.nsem += 1
                if self.phase_stack:
                    self.phase_stack[-1].append(dma_res)
            dma_res.cnt += dma_inc
            tok = (dma_res.sem, dma_res.cnt, None, True)
            inc = (dma_res.sem, dma_inc)
        else:
            self.count[eng] += 1
            tok = (self.esem[eng], self.count[eng], eng, False)
            inc = (self.esem[eng], 1)
        self.streams[eng].append((waits, fn, inc))
        if not is_dma:
            self.last[eng] = tok
        for r in reads:
            r.readers.append(tok)
        for w in writes:
            if id(w) not in follow:
                w.inh = list(deps)
            w.last_w = tok
            w.readers = []
        if store:
            self.pending.append(tok)
        return tok

    def ensure_sem(self, res):
        if res.sem is None:
            res.sem, res.cnt = self.sem_alloc("g%d_%s" % (self.nsem, res.name)), 0
            self.nsem += 1

    def barrier(self, eng="sp"):
        waits = []
        for t in self.pending:
            if self._need(eng, t, True):
                waits.append((t[0], t[1]))
        self.pending = []
        self.streams[eng].append((waits, None, None))

    def full_barrier(self):
        toks = list(self.last.values()) + list(self.pending)
        self.pending = []
        for eng in ENGS:
            self.wait_tok(eng, toks)

    def wait_tok(self, eng, toks):
        waits = []
        for t in toks:
            if self._need(eng, t, True):
                waits.append((t[0], t[1]))
        self.streams[eng].append((waits, None, None))

    def emit(self, block):
        def run(stream):
            def f(e):
                for waits, fn, inc in stream:
                    for (s, v) in waits:
                        e.wait_ge(s, v)
                    if fn is not None:
                        fn(e).then_inc(inc[0], inc[1])
            return f
        block.tensor(run(self.streams["pe"]))
        block.scalar(run(self.streams["act"]))
        block.vector(run(self.streams["dve"]))
        block.gpsimd(run(self.streams["pool"]))
        block.sync(run(self.streams["sp"]))


class Ring:
    def __init__(self, tiles, name):
        self.tiles = tiles
        self.res = [Res("%s%d" % (name, i)) for i in range(len(tiles))]
        self.i = 0

    def next(self):
        k = self.i % len(self.tiles)
        self.i += 1
        return self.tiles[k], self.res[k]


def build_program(n_layers=DEPTH, phases=("p1", "ag", "p2", "four", "p3"), dump=None):
    nc = bass.Bass("TRN2", target_bir_lowering=False)
    P = 128

    def din(name, shape, dt=F32):
        return nc.dram_tensor(name, list(shape), dt, kind="ExternalInput").ap()

    def dint(name, shape, dt=BF16):
        return nc.dram_tensor(name, list(shape), dt).ap()

    xin = din("xin", [D, TT])
    cvec = din("cvec", [P, 16, 2])
    ropeT = din("ropeT", [P, 2, TT])
    w_mod = din("w_mod", [DEPTH, 96, P, 2048])
    b_modT = din("b_modT", [DEPTH, P, 96])
    g1T = din("g1T", [DEPTH, P, 16])
    g2T = din("g2T", [DEPTH, P, 16])
    gfT = din("gfT", [P, 16])
    w_fm = din("w_fm", [DEPTH, 96, P, 2048])
    w_tm = din("w_tm", [DEPTH, 16, P, 2048])
    lamv = din("lamv", [DEPTH, P, 4, 64])
    sublnT = din("sublnT", [DEPTH, P, 1])
    sgu_gbc = din("sgu_gbc", [DEPTH, P, 1024])
    sgu_wT = din("sgu_wT", [DEPTH, P, 1024])
    sgu_bbc = din("sgu_bbc", [DEPTH, P, 1024])
    w_pa = din("w_pa", [DEPTH, 16, P, 1024])
    w_ps = din("w_ps", [DEPTH, 16, P, 1024])
    w_pf = din("w_pf", [DEPTH, 16, P, 1024])
    w_o = din("w_o", [DEPTH, 16, P, 2048])
    w_1 = din("w_1", [DEPTH, 64, P, 2048])
    w_2 = din("w_2", [DEPTH, 64, P, 2048])
    constsA = din("constsA", [P, 2944])
    constsB = din("constsB", [P, 320])
    outT = nc.dram_tensor("outT", [D, NL], F32, kind="ExternalOutput").ap()
    dumps = {}

    wb_fm = dint("wb_fm", [DEPTH, 96, P, 2048])
    wb_tm = dint("wb_tm", [DEPTH, 16, P, 2048])
    wb_pa = dint("wb_pa", [DEPTH, 16, P, 1024])
    wb_ps = dint("wb_ps", [DEPTH, 16, P, 1024])
    wb_pf = dint("wb_pf", [DEPTH, 16, P, 1024])
    wb_o = dint("wb_o", [DEPTH, 16, P, 2048])
    wb_1 = dint("wb_1", [DEPTH, 64, P, 2048])
    wb_2 = dint("wb_2", [DEPTH, 64, P, 2048])
    qT = dint("qT", [1024, TT])
    gatesT = dint("gatesT", [6144, TT])
    sguT = dint("sguT", [1024, TT])
    attT = dint("attT", [1024, TT])
    fourTM = dint("fourTM", [TT, 1024])
    xT1 = dint("xT1", [D, TT], F32)
    exk_in = dint("exk_in", [1024, NL])
    exk_out = dint("exk_out", [16, 4, 64, NL])
    exv_in = dint("exv_in", [8, NL, 128])
    exv_out = dint("exv_out", [8, 4 * NL, 128])
    exz_in = dint("exz_in", [16, NL, 128])
    exz_out = dint("exz_out", [16, 4 * NL, 128])
    exkc_in = dint("exkc_in", [1024, NC])
    exkc_out = dint("exkc_out", [4, 1024, NC])
    exvc_in = dint("exvc_in", [NC, 1024])
    exvc_out = dint("exvc_out", [4, NC, 1024])
    zctx = dint("zctx", [16, NC, 128])
    if dump:
        for nm, shape, dt in dump:
            dumps[nm] = nc.dram_tensor("dump_" + nm, list(shape), dt, kind="ExternalOutput").ap()

    es = contextlib.ExitStack()
    with es:
        def sem_alloc(name):
            return es.enter_context(nc.semaphore(name))

        S = Sched(nc, sem_alloc)

        arena = {"cur": 16640}

        def sb(name, shape, dt=F32):
            n = 1
            for d_ in shape[1:]:
                n *= d_
            nbytes = n * (4 if dt == F32 else 2)
            off = (arena["cur"] + 63) // 64 * 64
            arena["cur"] = off + nbytes
            assert arena["cur"] <= 229376 - 256, (name, arena["cur"])
            uid = "%s_%d" % (name, off)
            return nc.alloc_sbuf_tensor_at(uid + "_%d" % len(arena), list(shape), dt, offset=off)

        marks = []

        def phase_begin():
            marks.append(arena["cur"])
            S.phase_stack.append([])
            if "wring" in arena:
                arena["wring"].res = [Res("wr%d" % i) for i in range(len(arena["wring"].tiles))]
            arena[len(arena)] = 0

        def phase_end():
            S.full_barrier()
            for r_ in S.phase_stack.pop():
                S.free.append((r_.sem, r_.cnt))
            arena["cur"] = marks.pop()
            arena[len(arena)] = 0

        ps2 = [nc.alloc_psum_tensor("ps2_%d" % i, [P, 1024], F32) for i in range(4)]
        psum = [ps2[i // 2][:, (i % 2) * 512:(i % 2 + 1) * 512] for i in range(8)]
        Rps = [Res("ps%d" % i) for i in range(8)]

        cstB = sb("cstB", [P, 320])
        cb = sb("cb", [P, 2944], BF16)
        cownb = sb("cownb", [P, 64], BF16)
        Rcst = Res("cstB")
        Rcb = Res("cb")
        S.op("sp", lambda e: e.dma_start(out=cstB[:], in_=constsB[:, :]), writes=[Rcst], dma_res=Rcst)
        phase_begin()
        cstA = sb("cstA", [P, 2944])
        RcA = Res("cstA")
        S.op("sp", lambda e: e.dma_start(out=cstA[:], in_=constsA[:, :]), writes=[RcA], dma_res=RcA)
        S.op("dve", lambda e: e.tensor_copy(out=cb[:], in_=cstA[:]), reads=[RcA], writes=[Rcb])
        S.op("dve", lambda e: e.tensor_copy(out=cownb[:], in_=cstB[:, 256:320]), reads=[Rcst], writes=[Rcb])
        phase_end()
        ones_b = cb[:, 0:128]
        ident_b = cb[:, 128:256]
        CS1 = cb[:, 256:512]
        CS2 = cb[:, 512:768]
        Wc = [cb[:, 768:1280], cb[:, 1280:1792]]
        Cx = [[cb[:, 1792 + n * 512:1792 + n * 512 + 256], cb[:, 1792 + n * 512 + 256:1792 + (n + 1) * 512]] for n in range(2)]
        Tc = cstB[:, 0:128]
        Ts = cstB[:, 128:256]
        eps6 = cstB[:, 256 - 0 + 0:256 + 0] if False else None
        epst = sb("epst", [P, 2])
        Reps = Res("eps")
        S.op("pool", lambda e: e.memset(epst[:, 0:1], 1e-6), writes=[Reps])
        S.op("pool", lambda e: e.memset(epst[:, 1:2], 1e-5), writes=[Reps])
        eps6 = epst[:, 0:1]
        eps5 = epst[:, 1:2]

        Rw = {}

        def cast_w(name, src, dst, l, rows):
            r = Res("w_%s%d" % (name, l))
            Rw[(name, l)] = r
            s2 = src[l].rearrange("r p x -> (r p) x")
            d2 = dst[l].rearrange("r p x -> (r p) x")
            n = rows * P
            step = 1024
            for a in range(0, n, step):
                b = min(n, a + step)
                S.op("pool", (lambda a=a, b=b: lambda e: e.dma_start(out=d2[a:b, :], in_=s2[a:b, :]))(),
                     writes=[r], dma_res=r)

        for l in range(n_layers):
            cast_w("fm", w_fm, wb_fm, l, 96)
            cast_w("tm", w_tm, wb_tm, l, 16)
            cast_w("pa", w_pa, wb_pa, l, 16)
            cast_w("ps", w_ps, wb_ps, l, 16)
            cast_w("pf", w_pf, wb_pf, l, 16)
            cast_w("o", w_o, wb_o, l, 16)
            cast_w("1", w_1, wb_1, l, 64)
            cast_w("2", w_2, wb_2, l, 64)

        NW = 6
        wring = Ring([sb("wr%d" % i, [P, 2048], BF16) for i in range(NW)], "wr")
        arena["wring"] = wring

        class WPipe:
            def __init__(self, items):
                self.items = items
                self.slots = []
                self.k = 0
                for _ in range(min(NW - 1, len(items))):
                    self._issue()

            def _issue(self):
                i = len(self.slots)
                ap, wres = self.items[i]
                t, r = wring.next()
                x = ap.shape[-1]
                q_eng = "sp" if i % 2 == 0 else "act"
                S.op(q_eng, lambda e: e.dma_start(out=t[:, 0:x], in_=ap), reads=[wres], writes=[r], dma_res=r)
                self.slots.append((t, r))

            def get(self):
                t, r = self.slots[self.k]
                self.k += 1
                if len(self.slots) < len(self.items):
                    self._issue()
                return t, r

        modT = sb("modT", [P, DEPTH, 96, 2])
        tabA = sb("tabA", [P, DEPTH, 2, 2, 16])
        bmt = sb("bmt", [P, DEPTH, 96])
        g12 = sb("g12", [P, DEPTH, 2, 16])
        gft = sb("gft", [P, 16])
        cv = sb("cv", [P, 16, 2])
        cvs = sb("cvs", [P, 16, 2])
        lamt = sb("lamt", [P, DEPTH, 4, 64])
        lamp = sb("lamp", [P, DEPTH, 2, 64])
        lams = sb("lams", [P, DEPTH, 2])
        neglam = sb("neglam", [P, DEPTH])
        sublt = sb("sublt", [P, DEPTH])
        Rmod, Rcv, Rsm, Rlam = Res("mod"), Res("cv"), Res("small"), Res("lam")
        S.op("sp", lambda e: e.dma_start(out=cv[:], in_=cvec[:, :, :]), writes=[Rcv], dma_res=Rcv)
        S.op("act", lambda e: e.activation(out=cvs[:], in_=cv[:], func=AF.Sigmoid), reads=[Rcv], writes=[Rmod])
        S.op("dve", lambda e: e.tensor_tensor(out=cvs[:], in0=cvs[:], in1=cv[:], op=ALU.mult), reads=[Rmod, Rcv], writes=[Rmod])
        for l in range(DEPTH):
            S.op("sp", (lambda l=l: lambda e: e.dma_start(out=bmt[:, l, :], in_=b_modT[l]))(), writes=[Rsm], dma_res=Rsm)
            S.op("sp", (lambda l=l: lambda e: e.dma_start(out=g12[:, l, 0, :], in_=g1T[l]))(), writes=[Rsm], dma_res=Rsm)
            S.op("sp", (lambda l=l: lambda e: e.dma_start(out=g12[:, l, 1, :], in_=g2T[l]))(), writes=[Rsm], dma_res=Rsm)
        S.op("sp", lambda e: e.dma_start(out=gft[:], in_=gfT[:, :]), writes=[Rsm], dma_res=Rsm)

        phase_begin()
        wm_ring = Ring([sb("wm%d" % i, [P, 2048]) for i in range(3)], "wm")
        for l in range(n_layers):
            for j in range(96):
                t, r = wm_ring.next()
                S.op("sp", (lambda t=t, l=l, j=j: lambda e: e.dma_start(out=t[:], in_=w_mod[l, j]))(), writes=[r], dma_res=r)
                pb = j % 2
                for kc in range(16):
                    S.op("pe", (lambda t=t, kc=kc, pb=pb: lambda e: e.matmul(psum[pb][:, 0:2], lhsT=t[:, kc * 128:(kc + 1) * 128], rhs=cvs[:, kc, :], start=(kc == 0), stop=(kc == 15)))(),
                         reads=[r, Rmod], writes=[Rps[pb]])
                S.op("dve", (lambda l=l, j=j, pb=pb: lambda e: e.tensor_tensor(out=modT[:, l, j, :], in0=psum[pb][:, 0:2], in1=bmt[:, l, j:j + 1].to_broadcast([P, 2]), op=ALU.add))(),
                     reads=[Rps[pb], Rsm], writes=[Rmod])
            for w, j0 in ((0, 16), (1, 64)):
                for kind in range(2):
                    S.op("dve", (lambda l=l, w=w, j0=j0, kind=kind: lambda e: e.scalar_tensor_tensor(
                        out=tabA[:, l, w, kind, :], in0=modT[:, l, j0:j0 + 16, kind], scalar=1.0, in1=g12[:, l, w, :],
                        op0=ALU.add, op1=ALU.mult))(), reads=[Rmod, Rsm], writes=[Rmod])
            linit = 0.8 - 0.6 * math.exp(-0.3 * l)
            S.op("sp", (lambda l=l: lambda e: e.dma_start(out=lamt[:, l], in_=lamv[l]))(), writes=[Rlam], dma_res=Rlam)
            S.op("sp", (lambda l=l: lambda e: e.dma_start(out=sublt[:, l:l + 1], in_=sublnT[l]))(), writes=[Rlam], dma_res=Rlam)
            S.op("dve", (lambda l=l: lambda e: e.tensor_tensor(out=lamp[:, l, 0, :], in0=lamt[:, l, 0, :], in1=lamt[:, l, 1, :], op=ALU.mult))(), reads=[Rlam], writes=[Rlam])
            S.op("dve", (lambda l=l: lambda e: e.tensor_tensor(out=lamp[:, l, 1, :], in0=lamt[:, l, 2, :], in1=lamt[:, l, 3, :], op=ALU.mult))(), reads=[Rlam], writes=[Rlam])
            S.op("dve", (lambda l=l: lambda e: e.tensor_reduce(out=lams[:, l, :], in_=lamp[:, l], axis=AX.X, op=ALU.add))(), reads=[Rlam], writes=[Rlam])
            S.op("act", (lambda l=l: lambda e: e.activation(out=lams[:, l, :], in_=lams[:, l, :], func=AF.Exp))(), reads=[Rlam], writes=[Rlam])
            S.op("dve", (lambda l=l, linit=linit: lambda e: e.scalar_tensor_tensor(out=neglam[:, l:l + 1], in0=lams[:, l, 1:2], scalar=-linit, in1=lams[:, l, 0:1], op0=ALU.add, op1=ALU.subtract))(), reads=[Rlam], writes=[Rlam])
            S.op("dve", (lambda l=l, linit=linit: lambda e: e.tensor_scalar(out=sublt[:, l:l + 1], in0=sublt[:, l:l + 1], scalar1=1.0 - linit, scalar2=None, op0=ALU.mult))(), reads=[Rlam], writes=[Rlam])
        phase_end()

        def mod_col(l, j0, kind, kc):
            return modT[:, l, j0 + kc, kind:kind + 1]

        def norm_stats(T, n):
            S.op("act", lambda e: e.activation(out=T.sqb[:, :, 0:n], in_=T.xt[:, :, 0:n], func=AF.Square), reads=[T.Rxt], writes=[T.Rsq])
            for kc in range(16):
                S.op("pe", (lambda kc=kc: lambda e: e.matmul(psum[7][:, 0:n], lhsT=ones_b, rhs=T.sqb[:, kc, 0:n], start=(kc == 0), stop=(kc == 15)))(),
                     reads=[T.Rsq, Rcb], writes=[Rps[7]])
            S.op("act", lambda e: e.activation(out=T.rstd[:, 0:n], in_=psum[7][:, 0:n], func=AF.Sqrt, scale=1.0 / D, bias=eps6), reads=[Rps[7], Reps], writes=[T.Rrstd])
            S.op("dve", lambda e: e.reciprocal(out=T.rstd[:, 0:n], in_=T.rstd[:, 0:n]), reads=[T.Rrstd], writes=[T.Rrstd])

        def norm_mod(T, l, w, kind, n):
            j_sh = 0 if w == 0 else 48
            norm_stats(T, n)
            for kc in range(16):
                tf, rtf = T.tmpf[kc % 2], T.Rtmpf[kc % 2]
                S.op("dve", (lambda kc=kc, tf=tf: lambda e: e.scalar_tensor_tensor(out=tf[:, 0:n], in0=T.xt[:, kc, 0:n], scalar=tabA[:, l, w, kind, kc:kc + 1], in1=T.rstd[:, 0:n], op0=ALU.mult, op1=ALU.mult))(),
                     reads=[T.Rxt, T.Rrstd, Rmod], writes=[rtf])
                S.op("act", (lambda kc=kc, tf=tf: lambda e: e.activation(out=T.hx[:, kc, 0:n], in_=tf[:, 0:n], func=AF.Identity, bias=mod_col(l, j_sh, kind, kc), scale=1.0))(),
                     reads=[rtf, Rmod], writes=[T.Rhx])

        class NS:
            pass

        def common_tiles():
            T = NS()
            T.xt = sb("xt", [P, 16, 512]); T.Rxt = Res("xt")
            T.sqb = sb("sqb", [P, 16, 512], BF16); T.Rsq = Res("sqb")
            T.hx = sb("hx", [P, 16, 512], BF16); T.Rhx = Res("hx")
            T.rstd = sb("rstd", [P, 512]); T.Rrstd = Res("rstd")
            T.tmpf = [sb("tmpf%d" % i, [P, 512]) for i in range(4)]
            T.Rtmpf = [Res("tmpf%d" % i) for i in range(4)]
            T.stg = Ring([sb("stg%d" % i, [P, 512], BF16) for i in range(4)], "stg")
            return T

        Rexout = Res("exout")
        S.ensure_sem(Rexout)

        def do_layer(l):
            last = l == DEPTH - 1
            xsrc = xin if l == 0 else xT1

            if "p1" in phases:
                phase_begin()
                T = common_tiles()
                sgw_f = sb("sgw_f", [P, 1024]); sgw_b = sb("sgw_b", [P, 1024], BF16)
                sgb = sb("sgb", [P, 1024]); sgg = sb("sgg", [P, 1024]); Rsg = Res("sg")
                rope_t = [sb("rope%d" % i, [P, 2, 512]) for i in range(2)]
                Rrope = [Res("rope%d" % i) for i in range(2)]
                sut = sb("sut", [P, 8, 512], BF16); Rsut = Res("sut")
                ft = sb("ft", [P, 8, 512], BF16); Rft = Res("ft")
                svall = [sb("svall%d" % i, [P, 1024]) for i in range(4)]
                Rsvall = [Res("svall%d" % i) for i in range(4)]
                Rsvf = Res("svf")
                vn = sb("vn", [P, 1024], BF16); Rvn = Res("vn")
                ssq = sb("ssq", [P, 2]); junk = sb("junk", [P, 1024])
                zst = [sb("zst%d" % i, [P, 2048], BF16) for i in range(2)]
                Rzst = [Res("zst%d" % i) for i in range(2)]
                S.op("sp", (lambda l=l: lambda e: e.dma_start(out=sgw_f[:], in_=sgu_wT[l]))(), writes=[Rsg], dma_res=Rsg)
                S.op("sp", (lambda l=l: lambda e: e.dma_start(out=sgb[:], in_=sgu_bbc[l]))(), writes=[Rsg], dma_res=Rsg)
                S.op("sp", (lambda l=l: lambda e: e.dma_start(out=sgg[:], in_=sgu_gbc[l]))(), writes=[Rsg], dma_res=Rsg)
                S.op("dve", lambda e: e.tensor_copy(out=sgw_b[:], in_=sgw_f[:]), reads=[Rsg], writes=[Rsg])
                def p1_block(bi, t0, n):
                    kind = 0 if bi < 8 else 1
                    nsub = n // 128
                    S.op("sp", (lambda t0=t0, n=n: lambda e: e.dma_start(out=T.xt[:, :, 0:n], in_=xsrc[:, t0:t0 + n].rearrange("(k p) t -> p k t", p=P)))(),
                         writes=[T.Rxt], dma_res=T.Rxt)
                    rp, Rrp = rope_t[bi % 2], Rrope[bi % 2]
                    S.op("sp", (lambda t0=t0, n=n, rp=rp: lambda e: e.dma_start(out=rp[:, :, 0:n], in_=ropeT[:, :, t0:t0 + n]))(), writes=[Rrp], dma_res=Rrp)
                    norm_mod(T, l, 0, kind, n)
                    need_all = (not last) or kind == 0
                    items = []
                    fm_list = [("qk", c) for c in range(16) if (need_all or c >= 8)]
                    if need_all:
                        fm_list += [("su", c) for c in range(8)] + [("f", c) for c in range(8)] + [("g", c) for c in range(48)]
                    for kindc, c in fm_list:
                        if kindc == "qk":
                            items.append((wb_fm[l, c], Rw[("fm", l)]))
                            items.append((wb_fm[l, 16 + c], Rw[("fm", l)]))
                        else:
                            base = {"su": 32, "f": 40, "g": 48}[kindc]
                            items.append((wb_fm[l, base + c], Rw[("fm", l)]))
                    tm_list = [0, 1] + ([2, 3] if need_all else [])
                    for wh in tm_list:
                        for kq in range(4):
                            items.append((wb_tm[l, wh * 4 + kq], Rw[("tm", l)]))
                    pipe = WPipe(items)
                    pcount = [0]

                    def nextps():
                        k = pcount[0] % 6
                        pcount[0] += 1
                        return psum[k], Rps[k]

                    for kindc, c in fm_list:
                        wt, wr = pipe.get()
                        pa, Rpa = nextps()
                        for kc in range(16):
                            S.op("pe", (lambda wt=wt, kc=kc, pa=pa: lambda e: e.matmul(pa[:, 0:n], lhsT=wt[:, kc * 128:(kc + 1) * 128], rhs=T.hx[:, kc, 0:n], start=(kc == 0), stop=(kc == 15)))(),
                                 reads=[wr, T.Rhx], writes=[Rpa])
                        if kindc == "qk":
                            wt2, wr2 = pipe.get()
                            pb_, Rpb = nextps()
                            for kc in range(16):
                                S.op("pe", (lambda wt2=wt2, kc=kc, pb_=pb_: lambda e: e.matmul(pb_[:, 0:n], lhsT=wt2[:, kc * 128:(kc + 1) * 128], rhs=T.hx[:, kc, 0:n], start=(kc == 0), stop=(kc == 15)))(),
                                     reads=[wr2, T.Rhx], writes=[Rpb])
                            S.op("dve", (lambda pa=pa, rp=rp: lambda e: e.tensor_tensor(out=T.tmpf[2][:, 0:n], in0=pa[:, 0:n], in1=rp[:, 0, 0:n], op=ALU.mult))(), reads=[Rpa, Rrp], writes=[T.Rtmpf[2]])
                            S.op("dve", (lambda pb_=pb_, rp=rp: lambda e: e.tensor_tensor(out=T.tmpf[3][:, 0:n], in0=pb_[:, 0:n], in1=rp[:, 1, 0:n], op=ALU.mult))(), reads=[Rpb, Rrp], writes=[T.Rtmpf[3]])
                            st, rst = T.stg.next()
                            S.op("pool", (lambda st=st: lambda e: e.tensor_tensor(out=st[:, 0:n], in0=T.tmpf[2][:, 0:n], in1=T.tmpf[3][:, 0:n], op=ALU.add))(), reads=[T.Rtmpf[2], T.Rtmpf[3]], writes=[rst])
                            if c < 8:
                                dst = qT[c * 128:(c + 1) * 128, t0:t0 + n]
                            elif kind == 0:
                                dst = exk_in[(c - 8) * 128:(c - 7) * 128, t0:t0 + n]
                            else:
                                dst = exkc_in[(c - 8) * 128:(c - 7) * 128, 0:NC]
                            S.op("sp", (lambda st=st, dst=dst: lambda e: e.dma_start(out=dst, in_=st[:, 0:n]))(), reads=[rst], dma_res=rst, store=True)
                        elif kindc == "su":
                            S.op("act", (lambda pa=pa, c=c: lambda e: e.activation(out=sut[:, c, 0:n], in_=pa[:, 0:n], func=AF.Gelu))(), reads=[Rpa], writes=[Rsut])
                        elif kindc == "f":
                            S.op("dve", (lambda pa=pa, c=c: lambda e: e.tensor_copy(out=ft[:, c, 0:n], in_=pa[:, 0:n]))(), reads=[Rpa], writes=[Rft])
                        else:
                            st, rst = T.stg.next()
                            S.op("act", (lambda pa=pa, st=st: lambda e: e.activation(out=st[:, 0:n], in_=pa[:, 0:n], func=AF.Sigmoid))(), reads=[Rpa], writes=[rst])
                            S.op("sp", (lambda st=st, c=c: lambda e: e.dma_start(out=gatesT[c * 128:(c + 1) * 128, t0:t0 + n], in_=st[:, 0:n]))(), reads=[rst], dma_res=rst, store=True)

                    for wh in tm_list:
                        for kq in range(4):
                            wt, wr = pipe.get()
                            for s in range(nsub):
                                for k4 in range(4):
                                    kc = kq * 4 + k4
                                    S.op("pe", (lambda wt=wt, s=s, k4=k4, kc=kc: lambda e: e.matmul(psum[s][:, :], lhsT=T.hx[:, kc, s * 128:(s + 1) * 128], rhs=wt[:, k4 * 512:(k4 + 1) * 512], start=(kc == 0), stop=(kc == 15)))(),
                                         reads=[wr, T.Rhx], writes=[Rps[s]])
                        col0 = (wh % 2) * 512
                        for s in range(nsub):
                            if wh < 2:
                                st, rst = T.stg.next()
                                S.op("dve", (lambda s=s, st=st: lambda e: e.tensor_copy(out=st[:, 0:512], in_=psum[s][:, :]))(), reads=[Rps[s]], writes=[rst])
                                if kind == 0:
                                    h0 = (wh % 2) * 4
                                    vdst = exv_in[h0:h0 + 4, t0 + s * 128:t0 + (s + 1) * 128, :].rearrange("h t e -> t h e")
                                    S.op("sp", (lambda st=st, vdst=vdst: lambda e: e.dma_start(out=vdst, in_=st[:, 0:512].rearrange("p (h e) -> p h e", e=128)))(), reads=[rst], dma_res=rst, store=True)
                                else:
                                    S.op("sp", (lambda s=s, st=st, col0=col0: lambda e: e.dma_start(out=exvc_in[s * 128:(s + 1) * 128, col0:col0 + 512], in_=st[:, 0:512]))(), reads=[rst], dma_res=rst, store=True)
                            else:
                                S.op("act", (lambda s=s, col0=col0: lambda e: e.activation(out=svall[s][:, col0:col0 + 512], in_=psum[s][:, :], func=AF.Gelu))(), reads=[Rps[s]], writes=[Rsvall[s]])

                    if need_all:
                        for s in range(nsub):
                            zs, Rzs = zst[s % 2], Rzst[s % 2]
                            S.op("dve", (lambda s=s: lambda e: e.tensor_tensor(out=junk[:], in0=svall[s][:], in1=svall[s][:], op=ALU.mult))(), reads=[Rsvall[s]], writes=[Rsvf])
                            S.op("dve", lambda e: e.tensor_reduce(out=ssq[:, 0:1], in_=junk[:], axis=AX.X, op=ALU.add), reads=[Rsvf], writes=[Rsvf])
                            S.op("act", lambda e: e.activation(out=ssq[:, 1:2], in_=ssq[:, 0:1], func=AF.Sqrt, scale=1.0 / 1024, bias=eps6), reads=[Rsvf, Reps], writes=[Rsvf])
                            S.op("dve", lambda e: e.reciprocal(out=ssq[:, 1:2], in_=ssq[:, 1:2]), reads=[Rsvf], writes=[Rsvf])
                            S.op("dve", (lambda s=s: lambda e: e.scalar_tensor_tensor(out=vn[:], in0=svall[s][:], scalar=ssq[:, 1:2], in1=sgg[:], op0=ALU.mult, op1=ALU.mult))(), reads=[Rsvall[s], Rsvf, Rsg], writes=[Rvn])
                            for g in range(8):
                                pg, Rpg = psum[4 + g % 2], Rps[4 + g % 2]
                                S.op("pe", (lambda g=g, pg=pg: lambda e: e.matmul(pg[:, 0:128], lhsT=vn[:, g * 128:(g + 1) * 128], rhs=sgw_b[:, g * 128:(g + 1) * 128], start=True, stop=True))(),
                                     reads=[Rvn, Rsg], writes=[Rpg])
                                S.op("dve", (lambda g=g, pg=pg: lambda e: e.tensor_tensor(out=T.tmpf[2][:, 0:128], in0=pg[:, 0:128], in1=sgb[:, g * 128:(g + 1) * 128], op=ALU.add))(), reads=[Rpg, Rsg], writes=[T.Rtmpf[2]])
                                S.op("dve", (lambda g=g, s=s: lambda e: e.tensor_tensor(out=sut[:, g, s * 128:(s + 1) * 128], in0=T.tmpf[2][:, 0:128], in1=sut[:, g, s * 128:(s + 1) * 128], op=ALU.mult))(), reads=[T.Rtmpf[2], Rsut], writes=[Rsut])
                            for g in range(4):
                                pz, Rpz = psum[6 + g % 2], Rps[6 + g % 2]
                                for k2 in range(2):
                                    S.op("pe", (lambda g=g, k2=k2, pz=pz, s=s: lambda e: e.matmul(pz[:, :], lhsT=ft[:, g * 2 + k2, s * 128:(s + 1) * 128], rhs=Wc[k2], start=(k2 == 0), stop=(k2 == 1)))(),
                                         reads=[Rft, Rcb], writes=[Rpz])
                                S.op("act", (lambda g=g, pz=pz, zs=zs: lambda e: e.activation(out=zs[:, g * 512:(g + 1) * 512], in_=pz[:, :], func=AF.Identity))(), reads=[Rpz], writes=[Rzs])
                            zdst = (exz_in[:, t0 + s * 128:t0 + (s + 1) * 128, :] if kind == 0 else zctx[:, s * 128:(s + 1) * 128, :]).rearrange("q t c -> t q c")
                            S.op("sp", (lambda zs=zs, zdst=zdst: lambda e: e.dma_start(out=zdst, in_=zs[:].rearrange("p (q c) -> p q c", c=128)))(), reads=[Rzs], dma_res=Rzs, store=True)
                        S.op("sp", (lambda t0=t0, n=n: lambda e: e.dma_start(out=sguT[:, t0:t0 + n].rearrange("(g p) t -> p g t", p=P), in_=sut[:, :, 0:n]))(), reads=[Rsut], dma_res=Rsut, store=True)
                for bi, (t0, n) in enumerate(NBLK):
                    p1_block(bi, t0, n)
                phase_end()

            if "ag" in phases:
                pieces = []
                for i in range(16):
                    pieces.append((exk_in[i * 64:(i + 1) * 64, :], exk_out[i]))
                for i in range(8):
                    pieces.append((exv_in[i], exv_out[i]))
                for i in range(16):
                    pieces.append((exz_in[i], exz_out[i]))
                pieces.append((exkc_in, exkc_out))
                pieces.append((exvc_in, exvc_out))
                for (pi, po) in pieces:
                    S.op("pool", (lambda pi=pi, po=po: lambda e: e.collective_compute("AllGather", ALU.bypass, replica_groups=[[0, 1, 2, 3], [4, 5, 6, 7]], ins=[pi.opt()], outs=[po.opt()]))(),
                         writes=[Rexout], dma_res=Rexout, dma_inc=1)


            if "p2" in phases:
                phase_begin()
                KT = sb("KT", [P, 16640], BF16)
                Vt = sb("Vt", [P, 130, 128], BF16)
                QT = sb("QT", [P, TT], BF16)
                Rkv = Res("kv")
                ptr = Ring([sb("pt%d" % i, [P, 1024], BF16) for i in range(5)], "pt")
                accr = Ring([sb("pacc%d" % i, [P, 1024], BF16) for i in range(2)], "pacc")
                LG = 4
                ef = [sb("ef%d" % i, [P, 512]) for i in range(3)]
                Ref = [Res("ef%d" % i) for i in range(3)]
                e3b = sb("e3b", [P, 512], BF16); Re3 = Res("e3b")
                ostg = Ring([sb("ostg%d" % i, [P, 512], BF16) for i in range(2)], "ostg")
                RS = [Res("S0"), Res("S1")]
                RO = [Res("O0"), Res("L0"), Res("O1"), Res("L1")]
                Obank = [psum[4], psum[5], psum[6], psum[7]]
                qblocks = [(i * 512, 512, list(range(130))) for i in range(8)]
                if not last:
                    qblocks.append((NL, NC, [128, 129]))
                def p2_head(h):
                    for m in range(2):
                        S.op("sp", (lambda m=m: lambda e: e.dma_start(out=KT[m * 64:(m + 1) * 64, 0:4 * NL].rearrange("p (r t) -> p r t", r=4), in_=exk_out[m * 8 + h].rearrange("r f t -> f r t")))(),
                             reads=[Rexout], writes=[Rkv], dma_res=Rkv)
                        S.op("sp", (lambda m=m: lambda e: e.dma_start(out=KT[m * 64:(m + 1) * 64, 16384:16640], in_=exkc_out[0, m * 512 + h * 64:m * 512 + (h + 1) * 64, :]))(),
                             reads=[Rexout], writes=[Rkv], dma_res=Rkv)
                        S.op("sp", (lambda m=m: lambda e: e.dma_start(out=QT[m * 64:(m + 1) * 64, :], in_=qT[m * 512 + h * 64:m * 512 + (h + 1) * 64, :]))(),
                             writes=[Rkv], dma_res=Rkv)
                    S.op("sp", lambda e: e.dma_start(out=Vt[:, 0:128, :], in_=exv_out[h].rearrange("(b p) c -> p b c", p=P)), reads=[Rexout], writes=[Rkv], dma_res=Rkv)
                    S.op("sp", lambda e: e.dma_start(out=Vt[:, 128:130, :], in_=exvc_out[0, :, h * 128:(h + 1) * 128].rearrange("(b p) c -> p b c", p=P)), reads=[Rexout], writes=[Rkv], dma_res=Rkv)

                    def p2_qblock(q0, nq, kbs):
                        def qk(i):
                            kb = kbs[i]
                            Sx = ps2[i % 2]
                            for m in range(2):
                                S.op("pe", (lambda m=m, kb=kb, Sx=Sx: lambda e: e.matmul(Sx[:, m * 512:m * 512 + nq], lhsT=KT[m * 64:(m + 1) * 64, kb * 128:(kb + 1) * 128], rhs=QT[m * 64:(m + 1) * 64, q0:q0 + nq], start=True, stop=True))(),
                                     reads=[Rkv], writes=[RS[i % 2]])
                        qk(0)
                        nk = len(kbs)
                        grp = {}
                        for i in range(nk):
                            kb = kbs[i]
                            if i + 1 < nk:
                                qk(i + 1)
                            Sx = ps2[i % 2]
                            pt, Rpt = ptr.next()
                            S.op("act", (lambda Sx=Sx, pt=pt: lambda e: e.activation(out=pt[:].rearrange("p (m q) -> p m q", m=2)[:, :, 0:nq], in_=Sx[:].rearrange("p (m q) -> p m q", m=2)[:, :, 0:nq], func=AF.Exp, scale=0.125))(),
                                 reads=[RS[i % 2]], writes=[Rpt])
                            for m in range(2):
                                S.op("pe", (lambda m=m, kb=kb, pt=pt, i=i: lambda e: e.matmul(Obank[2 * m][:, 0:nq], lhsT=Vt[:, kb, :], rhs=pt[:, m * 512:m * 512 + nq], start=(i == 0), stop=(i == nk - 1)))(),
                                     reads=[Rkv, Rpt], writes=[RO[2 * m]])
                            gi = i % LG
                            if gi == 0:
                                grp["first"] = (pt, Rpt)
                                grp["src"] = (pt, Rpt)
                            elif gi == 1:
                                acc, Racc = accr.next()
                                fp_, Rfp = grp["first"]
                                S.op("dve", (lambda acc=acc, fp_=fp_, pt=pt: lambda e: e.tensor_tensor(out=acc[:], in0=fp_[:], in1=pt[:], op=ALU.add))(), reads=[Rfp, Rpt], writes=[Racc])
                                grp["src"] = (acc, Racc)
                            else:
                                acc, Racc = grp["src"]
                                S.op("dve", (lambda acc=acc, pt=pt: lambda e: e.tensor_tensor(out=acc[:], in0=acc[:], in1=pt[:], op=ALU.add))(), reads=[Racc, Rpt], writes=[Racc])
                            if gi == LG - 1 or i == nk - 1:
                                src, Rsrc = grp["src"]
                                first_g = (i // LG == 0)
                                for m in range(2):
                                    S.op("pe", (lambda m=m, src=src, first_g=first_g, i=i: lambda e: e.matmul(Obank[2 * m + 1][:, 0:nq], lhsT=ones_b, rhs=src[:, m * 512:m * 512 + nq], start=first_g, stop=(i == nk - 1)))(),
                                         reads=[Rcb, Rsrc], writes=[RO[2 * m + 1]])
                        S.op("dve", lambda e: e.reciprocal(out=ef[0][:, 0:nq], in_=Obank[1][:, 0:nq]), reads=[RO[1]], writes=[Ref[0]])
                        S.op("dve", lambda e: e.reciprocal(out=ef[1][:, 0:nq], in_=Obank[3][:, 0:nq]), reads=[RO[3]], writes=[Ref[1]])
                        S.op("dve", lambda e: e.tensor_tensor(out=ef[0][:, 0:nq], in0=Obank[0][:, 0:nq], in1=ef[0][:, 0:nq], op=ALU.mult), reads=[RO[0], Ref[0]], writes=[Ref[0]])
                        S.op("dve", lambda e: e.tensor_tensor(out=ef[1][:, 0:nq], in0=Obank[2][:, 0:nq], in1=ef[1][:, 0:nq], op=ALU.mult), reads=[RO[2], Ref[1]], writes=[Ref[1]])
                        S.op("dve", lambda e: e.scalar_tensor_tensor(out=ef[2][:, 0:nq], in0=ef[1][:, 0:nq], scalar=neglam[:, l:l + 1], in1=ef[0][:, 0:nq], op0=ALU.mult, op1=ALU.add), reads=[Ref[0], Ref[1], Rlam], writes=[Ref[2]])
                        S.op("act", lambda e: e.activation(out=e3b[:, 0:nq], in_=ef[2][:, 0:nq], func=AF.Square), reads=[Ref[2]], writes=[Re3])
                        S.op("pe", lambda e: e.matmul(ps2[0][:, 0:nq], lhsT=ones_b, rhs=e3b[:, 0:nq], start=True, stop=True), reads=[Re3, Rcb], writes=[RS[0]])
                        S.op("act", lambda e: e.activation(out=ef[0][:, 0:nq], in_=ps2[0][:, 0:nq], func=AF.Sqrt, scale=1.0 / 128, bias=eps5), reads=[RS[0], Reps], writes=[Ref[0]])
                        S.op("dve", lambda e: e.reciprocal(out=ef[0][:, 0:nq], in_=ef[0][:, 0:nq]), reads=[Ref[0]], writes=[Ref[0]])
                        st, rst = ostg.next()
                        S.op("dve", (lambda st=st: lambda e: e.scalar_tensor_tensor(out=st[:, 0:nq], in0=ef[2][:, 0:nq], scalar=sublt[:, l:l + 1], in1=ef[0][:, 0:nq], op0=ALU.mult, op1=ALU.mult))(), reads=[Ref[2], Ref[0], Rlam], writes=[rst])
                        S.op("sp", (lambda st=st: lambda e: e.dma_start(out=attT[h * 128:(h + 1) * 128, q0:q0 + nq], in_=st[:, 0:nq]))(), reads=[rst], dma_res=rst, store=True)
                    for (q0, nq, kbs) in qblocks:
                        p2_qblock(q0, nq, kbs)
                for h in range(8):
                    p2_head(h)
                phase_end()

            if "four" in phases:
                phase_begin()
                Zt = sb("Zt", [P, 128, 128], BF16); RZt = Res("Zt")
                Bt = sb("Bt", [P, 2, 64, 128], BF16); RBt = Res("Bt")
                Yt = sb("Yt", [P, 128, 256], BF16); RYt = Res("Yt")
                tw = [sb("tw%d" % i, [P, 512]) for i in range(4)]
                Rtw = [Res("tw%d" % i) for i in range(4)]
                Tc4 = Tc.unsqueeze(1).to_broadcast([P, 4, 128])
                Ts4 = Ts.unsqueeze(1).to_broadcast([P, 4, 128])
                for g in range(4):
                    for cq in range(4):
                        q = g * 4 + cq
                        S.op("sp", (lambda q=q: lambda e: e.dma_start(out=Zt[:, :, :], in_=exz_out[q].rearrange("(a n) c -> a n c", n=128)))(),
                             reads=[Rexout], writes=[RZt], dma_res=RZt)
                        for cp in range(32):
                            pz, Rpz = psum[cp % 2], Rps[cp % 2]
                            for ci in range(2):
                                c = 2 * cp + ci
                                S.op("pe", (lambda c=c, ci=ci, pz=pz: lambda e: e.matmul(pz[:, ci * 256:(ci + 1) * 256], lhsT=Zt[:, :, c], rhs=CS1, start=True, stop=False))(), reads=[RZt, Rcb], writes=[Rpz])
                                S.op("pe", (lambda c=c, ci=ci, pz=pz: lambda e: e.matmul(pz[:, ci * 256:(ci + 1) * 256], lhsT=Zt[:, :, 64 + c], rhs=CS2, start=False, stop=True))(), reads=[RZt, Rcb], writes=[Rpz])
                            k = (cp % 2) * 2
                            t1, t2 = tw[k], tw[k + 1]
                            S.op("dve", (lambda pz=pz, t1=t1: lambda e: e.tensor_tensor(out=t1[:].rearrange("p (a k) -> p a k", k=128), in0=pz.rearrange("p (a k) -> p a k", k=128), in1=Tc4, op=ALU.mult))(), reads=[Rpz, Rcst], writes=[Rtw[k]])
                            S.op("dve", (lambda pz=pz, t2=t2: lambda e: e.tensor_tensor(out=t2[:].rearrange("p (a k) -> p a k", k=128), in0=pz.rearrange("p (a k) -> p a k", k=128), in1=Ts4, op=ALU.mult))(), reads=[Rpz, Rcst], writes=[Rtw[k + 1]])
                            t1v = t1[:].rearrange("p (c r k) -> p c r k", c=2, r=2)
                            t2v = t2[:].rearrange("p (c r k) -> p c r k", c=2, r=2)
                            S.op("pool", (lambda cp=cp, t1v=t1v, t2v=t2v: lambda e: e.tensor_tensor(out=Bt[:, 0, 2 * cp:2 * cp + 2, :], in0=t1v[:, :, 0, :], in1=t2v[:, :, 1, :], op=ALU.add))(), reads=[Rtw[k], Rtw[k + 1]], writes=[RBt])
                            S.op("pool", (lambda cp=cp, t1v=t1v, t2v=t2v: lambda e: e.tensor_tensor(out=Bt[:, 1, 2 * cp:2 * cp + 2, :], in0=t1v[:, :, 1, :], in1=t2v[:, :, 0, :], op=ALU.subtract))(), reads=[Rtw[k], Rtw[k + 1]], writes=[RBt])
                        for cg in range(16):
                            py, Rpy = psum[2 + cg % 2], Rps[2 + cg % 2]
                            S.op("pe", (lambda cg=cg, py=py: lambda e: e.matmul(py[0:32, :], lhsT=cownb[:, 0:32], rhs=Bt[:, 0, 4 * cg:4 * cg + 4, :], start=True, stop=False))(), reads=[RBt, Rcb], writes=[Rpy])
                            S.op("pe", (lambda cg=cg, py=py: lambda e: e.matmul(py[0:32, :], lhsT=cownb[:, 32:64], rhs=Bt[:, 1, 4 * cg:4 * cg + 4, :], start=False, stop=True))(), reads=[RBt, Rcb], writes=[Rpy])
                            ch0 = cq * 64 + 4 * cg
                            S.op("act", (lambda py=py, ch0=ch0: lambda e: e.activation(out=Yt[0:32, :, ch0:ch0 + 4].rearrange("p k c -> p c k"), in_=py[0:32, :].rearrange("p (c k) -> p c k", k=128), func=AF.Identity, scale=1.0 / 2048))(), reads=[Rpy], writes=[RYt])
                    S.op("sp", (lambda g=g: lambda e: e.dma_start(out=fourTM[0:NL, g * 256:(g + 1) * 256].rearrange("(a k) c -> a k c", k=128), in_=Yt[0:32, :, :]))(), reads=[RYt], dma_res=RYt, store=True)
                if not last:
                    Zc = sb("Zc", [P, 2, 16, 128], BF16); RZc = Res("Zc")
                    Yc = sb("Yc", [P, 2, 1024], BF16); RYc = Res("Yc")
                    for nch in range(2):
                        S.op("sp", (lambda nch=nch: lambda e: e.dma_start(out=Zc[:, nch, :, :], in_=zctx[:, nch * 128:(nch + 1) * 128, :].rearrange("q t c -> t q c")))(), writes=[RZc], dma_res=RZc)
                    for kch in range(2):
                        for g in range(4):
                            pc, Rpc = psum[4 + g % 2], Rps[4 + g % 2]
                            cnt = 0
                            for nch in range(2):
                                for ri in range(2):
                                    S.op("pe", (lambda nch=nch, ri=ri, g=g, kch=kch, pc=pc, cnt=cnt: lambda e: e.matmul(pc[:, 0:256], lhsT=Cx[nch][ri][:, kch * 128:(kch + 1) * 128], rhs=Zc[:, nch, 4 * g:4 * g + 4, ri * 64:(ri + 1) * 64], start=(cnt == 0), stop=(cnt == 3)))(),
                                         reads=[RZc, Rcb], writes=[Rpc])
                                    cnt += 1
                            S.op("act", (lambda pc=pc, kch=kch, g=g: lambda e: e.activation(out=Yc[:, kch, g * 256:(g + 1) * 256], in_=pc[:, 0:256], func=AF.Identity, scale=1.0 / 256))(), reads=[Rpc], writes=[RYc])
                    S.op("sp", lambda e: e.dma_start(out=fourTM[NL:TT, :].rearrange("(k p) c -> p k c", p=P), in_=Yc[:]), reads=[RYc], dma_res=RYc, store=True)
                phase_end()

            if "p3" in phases:
                phase_begin()
                T = common_tiles()
                hid = sb("hid", [P, 32, 512], BF16); Rhid = Res("hid")
                att = sb("att", [P, 8, 512], BF16); sgu = sb("sgu", [P, 8, 512], BF16)
                fourT = sb("fourT", [P, 8, 512], BF16); RfT = Res("fourT")
                fstg = sb("fstg", [P, 4, 1024], BF16)
                Rin = Res("p3in")
                Rfs = Res("fstg")
                gring = Ring([sb("gat%d" % i, [P, 3, 512], BF16) for i in range(2)], "gat")
                gsrc = gatesT.rearrange("(w f) t -> f w t", w=3)
                blocks = NBLK if not last else NBLK[:8]
                def p3_block(bi, t0, n):
                    kind = 0 if bi < 8 else 1
                    nsub = n // 128
                    S.op("sp", (lambda t0=t0, n=n: lambda e: e.dma_start(out=T.xt[:, :, 0:n], in_=xsrc[:, t0:t0 + n].rearrange("(k p) t -> p k t", p=P)))(), writes=[T.Rxt], dma_res=T.Rxt)
                    S.op("sp", (lambda t0=t0, n=n: lambda e: e.dma_start(out=att[:, :, 0:n], in_=attT[:, t0:t0 + n].rearrange("(k p) t -> p k t", p=P)))(), writes=[Rin], dma_res=Rin)
                    S.op("sp", (lambda t0=t0, n=n: lambda e: e.dma_start(out=sgu[:, :, 0:n], in_=sguT[:, t0:t0 + n].rearrange("(k p) t -> p k t", p=P)))(), writes=[Rin], dma_res=Rin)
                    S.op("sp", (lambda t0=t0, n=n, nsub=nsub: lambda e: e.dma_start(out=fstg[:, 0:nsub, :], in_=fourTM[t0:t0 + n, :].rearrange("(s p) c -> p s c", p=P)))(), writes=[Rfs], dma_res=Rfs)
                    for s in range(nsub):
                        pT, RpT = psum[6 + s % 2], Rps[6 + s % 2]
                        pTb = pT.bitcast(BF16)
                        for c in range(8):
                            S.op("pe", (lambda s=s, c=c, pTb=pTb: lambda e: e.transpose(pTb[:, c * 128:(c + 1) * 128], fstg[:, s, c * 128:(c + 1) * 128], ident_b))(), reads=[Rfs, Rcb], writes=[RpT])
                        S.op("dve", (lambda s=s, pTb=pTb: lambda e: e.tensor_copy(out=fourT[:, :, s * 128:(s + 1) * 128], in_=pTb[:, 0:1024].rearrange("p (c k) -> p c k", k=128)))(), reads=[RpT], writes=[RfT])
                    items = []
                    for j in range(16):
                        items += [(wb_pa[l, j], Rw[("pa", l)]), (wb_ps[l, j], Rw[("ps", l)]), (wb_pf[l, j], Rw[("pf", l)])]
                    items += [(wb_o[l, j], Rw[("o", l)]) for j in range(16)]
                    for half in range(2):
                        items += [(wb_1[l, half * 32 + m], Rw[("1", l)]) for m in range(32)]
                        for j in range(16):
                            items += [(wb_2[l, j * 4 + half * 2 + pp], Rw[("2", l)]) for pp in range(2)]
                    pipe = WPipe(items)
                    for j in range(16):
                        gt_, Rgt = gring.next()
                        S.op("sp", (lambda j=j, gt_=gt_: lambda e: e.dma_start(out=gt_[:, :, 0:n], in_=gsrc[j * 128:(j + 1) * 128, :, t0:t0 + n]))(), writes=[Rgt], dma_res=Rgt)
                        srcs = [(att, Rin), (sgu, Rin), (fourT, RfT)]
                        for bidx in range(3):
                            wt, wr = pipe.get()
                            src, Rsrc = srcs[bidx]
                            for kc in range(8):
                                S.op("pe", (lambda wt=wt, kc=kc, bidx=bidx, src=src: lambda e: e.matmul(psum[bidx][:, 0:n], lhsT=wt[:, kc * 128:(kc + 1) * 128], rhs=src[:, kc, 0:n], start=(kc == 0), stop=(kc == 7)))(),
                                     reads=[wr, Rsrc], writes=[Rps[bidx]])
                        S.op("dve", (lambda gt_=gt_: lambda e: e.tensor_tensor(out=T.tmpf[0][:, 0:n], in0=psum[0][:, 0:n], in1=gt_[:, 0, 0:n], op=ALU.mult))(), reads=[Rps[0], Rgt], writes=[T.Rtmpf[0]])
                        S.op("dve", (lambda gt_=gt_: lambda e: e.tensor_tensor(out=T.tmpf[1][:, 0:n], in0=psum[1][:, 0:n], in1=gt_[:, 1, 0:n], op=ALU.mult))(), reads=[Rps[1], Rgt], writes=[T.Rtmpf[1]])
                        S.op("dve", (lambda gt_=gt_: lambda e: e.tensor_tensor(out=T.tmpf[2][:, 0:n], in0=psum[2][:, 0:n], in1=gt_[:, 2, 0:n], op=ALU.mult))(), reads=[Rps[2], Rgt], writes=[T.Rtmpf[2]])
                        S.op("pool", lambda e: e.tensor_tensor(out=T.tmpf[0][:, 0:n], in0=T.tmpf[0][:, 0:n], in1=T.tmpf[1][:, 0:n], op=ALU.add), reads=[T.Rtmpf[0], T.Rtmpf[1]], writes=[T.Rtmpf[0]])
                        S.op("pool", (lambda j=j: lambda e: e.tensor_tensor(out=T.sqb[:, j, 0:n], in0=T.tmpf[0][:, 0:n], in1=T.tmpf[2][:, 0:n], op=ALU.add))(), reads=[T.Rtmpf[0], T.Rtmpf[2]], writes=[T.Rsq])
                    for j in range(16):
                        wt, wr = pipe.get()
                        pa, Rpa = psum[3 + j % 2], Rps[3 + j % 2]
                        for kc in range(16):
                            S.op("pe", (lambda wt=wt, kc=kc, pa=pa: lambda e: e.matmul(pa[:, 0:n], lhsT=wt[:, kc * 128:(kc + 1) * 128], rhs=T.sqb[:, kc, 0:n], start=(kc == 0), stop=(kc == 15)))(),
                                 reads=[wr, T.Rsq], writes=[Rpa])
                        S.op("dve", (lambda j=j, pa=pa: lambda e: e.scalar_tensor_tensor(out=T.xt[:, j, 0:n], in0=pa[:, 0:n], scalar=mod_col(l, 32, kind, j), in1=T.xt[:, j, 0:n], op0=ALU.mult, op1=ALU.add))(),
                             reads=[Rpa, T.Rxt, Rmod], writes=[T.Rxt])
                    norm_mod(T, l, 1, kind, n)
                    for half in range(2):
                        for m in range(32):
                            wt, wr = pipe.get()
                            pa, Rpa = psum[m % 4], Rps[m % 4]
                            for kc in range(16):
                                S.op("pe", (lambda wt=wt, kc=kc, pa=pa: lambda e: e.matmul(pa[:, 0:n], lhsT=wt[:, kc * 128:(kc + 1) * 128], rhs=T.hx[:, kc, 0:n], start=(kc == 0), stop=(kc == 15)))(),
                                     reads=[wr, T.Rhx], writes=[Rpa])
                            tf, rtf = T.tmpf[m % 4], T.Rtmpf[m % 4]
                            S.op("act", (lambda pa=pa, tf=tf: lambda e: e.activation(out=tf[:, 0:n], in_=pa[:, 0:n], func=AF.Relu))(), reads=[Rpa], writes=[rtf])
                            S.op("pool", (lambda m=m, tf=tf: lambda e: e.tensor_tensor(out=hid[:, m, 0:n], in0=tf[:, 0:n], in1=tf[:, 0:n], op=ALU.mult))(), reads=[rtf], writes=[Rhid])
                        for j in range(16):
                            pa, Rpa = psum[4 + j % 2], Rps[4 + j % 2]
                            for pp in range(2):
                                wt, wr = pipe.get()
                                for kc in range(16):
                                    S.op("pe", (lambda wt=wt, kc=kc, pp=pp, pa=pa: lambda e: e.matmul(pa[:, 0:n], lhsT=wt[:, kc * 128:(kc + 1) * 128], rhs=hid[:, pp * 16 + kc, 0:n], start=(pp == 0 and kc == 0), stop=(pp == 1 and kc == 15)))(),
                                         reads=[wr, Rhid], writes=[Rpa])
                            S.op("dve", (lambda j=j, pa=pa: lambda e: e.scalar_tensor_tensor(out=T.xt[:, j, 0:n], in0=pa[:, 0:n], scalar=mod_col(l, 80, kind, j), in1=T.xt[:, j, 0:n], op0=ALU.mult, op1=ALU.add))(),
                                 reads=[Rpa, T.Rxt, Rmod], writes=[T.Rxt])
                    if not last:
                        S.op("sp", (lambda t0=t0, n=n: lambda e: e.dma_start(out=xT1[:, t0:t0 + n].rearrange("(k p) t -> p k t", p=P), in_=T.xt[:, :, 0:n]))(), reads=[T.Rxt], dma_res=T.Rxt, store=True)
                    else:
                        norm_stats(T, n)
                        for kc in range(16):
                            S.op("dve", (lambda kc=kc: lambda e: e.scalar_tensor_tensor(out=T.xt[:, kc, 0:n], in0=T.xt[:, kc, 0:n], scalar=gft[:, kc:kc + 1], in1=T.rstd[:, 0:n], op0=ALU.mult, op1=ALU.mult))(),
                                 reads=[T.Rxt, T.Rrstd, Rsm], writes=[T.Rxt])
                        S.op("sp", (lambda t0=t0, n=n: lambda e: e.dma_start(out=outT[:, t0:t0 + n].rearrange("(k p) t -> p k t", p=P), in_=T.xt[:, :, 0:n]))(), reads=[T.Rxt], dma_res=T.Rxt, store=True)
                for bi, (t0, n) in enumerate(blocks):
                    p3_block(bi, t0, n)
                phase_end()

        for l in range(n_layers):
            do_layer(l)

        if dump:
            srcmap = {"qT": qT, "gatesT": gatesT, "sguT": sguT, "attT": attT, "fourTM": fourTM, "xT1": xT1,
                      "modT": None}
            Rd = Res("dump")
            for nm, shape, dt in dump:
                if nm == "modT":
                    S.op("sp", lambda e: e.dma_start(out=dumps["modT"], in_=modT[:].rearrange("p l j k -> p (l j k)")), reads=[Rmod], dma_res=Rd, store=True)
                else:
                    src = srcmap[nm]
                    S.op("sp", (lambda nm=nm, src=src: lambda e: e.dma_start(out=dumps[nm], in_=src))(), dma_res=Rd, store=True)
        S.full_barrier()
        block = es.enter_context(nc.Block())
        S.emit(block)
    return nc


def _tile_fm(W):
    K, M = W.shape
    t = W.reshape(K // 128, 128, M // 128, 128).transpose(2, 1, 0, 3)
    return np.ascontiguousarray(t).reshape(M // 128, 128, (K // 128) * 128)


def _tile_tm(W):
    K, M = W.shape
    nb = M // 512
    t = W.reshape(4, 4, 128, nb, 512).transpose(3, 0, 2, 1, 4)
    return np.ascontiguousarray(t).reshape(nb * 4, 128, 2048)


def _constants():
    f32 = np.float32
    A = np.zeros((128, 2944), f32)
    A[:, 0:128] = 1.0
    A[:, 128:256] = np.eye(128, dtype=f32)
    n = np.arange(128)
    ang = 2 * np.pi * np.outer(n, n) / 128.0
    C, Sn = np.cos(ang), np.sin(ang)
    A[:, 256:384], A[:, 384:512] = C, -Sn
    A[:, 512:640], A[:, 640:768] = Sn, C
    ch = np.arange(256)
    a2 = 2 * np.pi * np.outer(ch, ch) / 256.0
    Cc, Sc = np.cos(a2), np.sin(a2)
    Wp = np.zeros((256, 512))
    for cq in range(4):
        Wp[:, cq * 128:cq * 128 + 64] = Cc[:, cq * 64:(cq + 1) * 64]
        Wp[:, cq * 128 + 64:cq * 128 + 128] = -Sc[:, cq * 64:(cq + 1) * 64]
    A[:, 768:1280] = Wp[0:128]
    A[:, 1280:1792] = Wp[128:256]
    for nch in range(2):
        A[:, 1792 + nch * 512:1792 + nch * 512 + 256] = Cc[nch * 128:(nch + 1) * 128]
        A[:, 1792 + nch * 512 + 256:1792 + (nch + 1) * 512] = Sc[nch * 128:(nch + 1) * 128]
    Bc = np.zeros((8, 128, 320), f32)
    at = 2 * np.pi * np.outer(n, n) / 16384.0
    Bc[:, :, 0:128] = np.cos(at)
    Bc[:, :, 128:256] = np.sin(at)
    for core in range(8):
        rk = core % 4
        Bc[core, :, 256:288] = C[:, 32 * rk:32 * rk + 32]
        Bc[core, :, 288:320] = Sn[:, 32 * rk:32 * rk + 32]
    return A, Bc


def _rope_tables():
    f32 = np.float32
    n_freq = 16
    inv = (np.float32(10000.0) ** (-np.arange(n_freq, dtype=f32) / f32(n_freq))).astype(f32)
    tok = np.arange(16384)
    row = (tok // 64).astype(f32)
    col = (tok % 64).astype(f32)
    ar = row[:, None] * inv
    ac = col[:, None] * inv
    ang = np.concatenate([ar, ar, ac, ac], axis=-1).astype(f32)
    cos, sin = np.cos(ang).astype(f32), np.sin(ang).astype(f32)
    d = np.arange(64)
    sign = np.where((d % 32) < 16, -1.0, 1.0).astype(f32)
    sin_s = sin * sign[None, :]
    return cos.T.copy(), sin_s.T.copy()


_ROT_PERM = np.array([(d + 16) if (d % 32) < 16 else (d - 16) for d in range(64)])


def _prep_inputs(inp):
    f32 = np.float32
    g = {k: np.asarray(v, dtype=f32) for k, v in inp.items()}
    A, Bc = _constants()
    cosT, sinT = _rope_tables()
    shared = {}
    shared["w_mod"] = np.stack([_tile_fm(g["w_mod"][l]) for l in range(DEPTH)])
    shared["b_modT"] = np.ascontiguousarray(g["b_mod"].reshape(DEPTH, 96, 128).transpose(0, 2, 1))
    shared["g1T"] = np.ascontiguousarray(g["norm1_g"].reshape(DEPTH, 16, 128).transpose(0, 2, 1))
    shared["g2T"] = np.ascontiguousarray(g["norm2_g"].reshape(DEPTH, 16, 128).transpose(0, 2, 1))
    shared["gfT"] = np.ascontiguousarray(g["final_g"].reshape(16, 128).T)
    w_fm, w_tm = [], []
    perm_cols = np.concatenate([hh * 64 + _ROT_PERM for hh in range(32)])
    for l in range(DEPTH):
        W = g["w_in"][l]
        qk = W[:, 0:2048]
        fm_cols = np.concatenate([qk, qk[:, perm_cols], W[:, 3072:4096], W[:, 5120:6144], W[:, 6144:12288]], axis=1)
        w_fm.append(_tile_fm(fm_cols))
        tm_cols = np.concatenate([W[:, 2048:3072], W[:, 4096:5120]], axis=1)
        w_tm.append(_tile_tm(tm_cols))
    shared["w_fm"] = np.stack(w_fm)
    shared["w_tm"] = np.stack(w_tm)
    lam = np.stack([g["lambda_q1"], g["lambda_k1"], g["lambda_q2"], g["lambda_k2"]], axis=1)
    shared["lamv"] = np.ascontiguousarray(np.broadcast_to(lam[:, None], (DEPTH, 128, 4, 64)))
    shared["sublnT"] = np.ascontiguousarray(g["subln_g"].reshape(DEPTH, 128, 1))
    shared["sgu_gbc"] = np.ascontiguousarray(np.broadcast_to(g["sgu_norm_g"][:, None, :], (DEPTH, 128, 1024)))
    shared["sgu_wT"] = np.ascontiguousarray(g["sgu_w"].transpose(0, 3, 1, 2)).reshape(DEPTH, 128, 1024)
    shared["sgu_bbc"] = np.ascontiguousarray(np.broadcast_to(g["sgu_b"].reshape(DEPTH, 1, 1024), (DEPTH, 128, 1024)))
    shared["w_pa"] = np.stack([_tile_fm(g["w_proj_att"][l]) for l in range(DEPTH)])
    shared["w_ps"] = np.stack([_tile_fm(g["w_proj_sgu"][l]) for l in range(DEPTH)])
    shared["w_pf"] = np.stack([_tile_fm(g["w_proj_fourier"][l]) for l in range(DEPTH)])
    shared["w_o"] = np.stack([_tile_fm(g["w_out"][l]) for l in range(DEPTH)])
    shared["w_1"] = np.stack([_tile_fm(g["w_mlp_in"][l]) for l in range(DEPTH)])
    w2 = []
    for l in range(DEPTH):
        t = _tile_fm(g["w_mlp_out"][l])
        t = t.reshape(16, 128, 4, 2048).transpose(0, 2, 1, 3)
        w2.append(np.ascontiguousarray(t).reshape(64, 128, 2048))
    shared["w_2"] = np.stack(w2)
    shared["constsA"] = A
    in_maps = []
    for core in range(8):
        b, rk = core // 4, core % 4
        m = dict(shared)
        xin = np.zeros((D, TT), f32)
        xin[:, 0:NL] = g["x"][b, rk * NL:(rk + 1) * NL, :].T
        if rk == 0:
            xin[:, NL:TT] = g["ctx"][b].T
        m["xin"] = xin
        cv = np.stack([g["c"][b], g["c_ctx"]], axis=-1)
        m["cvec"] = np.ascontiguousarray(cv.reshape(16, 128, 2).transpose(1, 0, 2))
        rope = np.zeros((128, 2, TT), f32)
        rope[:, 0, NL:TT] = 1.0
        cs = cosT[:, rk * NL:(rk + 1) * NL]
        sn = sinT[:, rk * NL:(rk + 1) * NL]
        rope[0:64, 0, 0:NL] = cs
        rope[64:128, 0, 0:NL] = cs
        rope[0:64, 1, 0:NL] = sn
        rope[64:128, 1, 0:NL] = sn
        m["ropeT"] = rope
        m["constsB"] = Bc[core]
        in_maps.append(m)
    return in_maps


_NC_CACHE = {}


def kernel(**inputs):
    in_maps = _prep_inputs(inputs)
    if "nc" not in _NC_CACHE:
        _NC_CACHE["nc"] = build_program()
    nc = _NC_CACHE["nc"]
    res = run_bass_kernel_spmd(nc, in_maps, core_ids=list(range(8)))
    out = np.empty((2, 16384, D), np.float32)
    for core in range(8):
        b, rk = core // 4, core % 4
        out[b, rk * NL:(rk + 1) * NL, :] = res.results[core]["outT"].T
    return out
```
